# Optimizing a Trainium2 kernel written in Bass

```python
import math
import jax, jax.numpy as jnp
from jax import lax
import numpy as np

D_MODEL = 1024
BATCH = 16
SEQ = 2048
DEPTH = 4

F32 = jnp.float32
RMS_EPS = 1e-6
MIX_W = 512
N_BRANCH = 4

SSD_HEAD_DIM = 64
SSD_HEADS = MIX_W // SSD_HEAD_DIM
SSD_GROUPS = 2
SSD_STATE = 128
SSD_CONV = 4
SSD_CHUNK = 128
SSD_XBC = MIX_W + 2 * SSD_GROUPS * SSD_STATE
SSD_IN = MIX_W + SSD_XBC + SSD_HEADS

SC_CONV = 3
SC_IN = 3 * MIX_W

SG_GROUPS = 4
SG_CHUNK = 128
SG_IN = 2 * MIX_W

NSA_HEADS = 8
NSA_KV_HEADS = 2
NSA_REP = NSA_HEADS // NSA_KV_HEADS
NSA_HEAD_DIM = 64
CMP_BLOCK = 32
CMP_STRIDE = 16
SLC_BLOCK = 64
SLC_TOPN = 8
SLC_QBLOCK = 32
WIN = 256
WIN_QBLOCK = 128
FORCE_SCORE = 1e9
NSA_KV_W = NSA_KV_HEADS * NSA_HEAD_DIM
NSA_IN = NSA_HEADS * NSA_HEAD_DIM + 6 * NSA_KV_W + 3 * NSA_HEADS

D_IN = SSD_IN + SC_IN + SG_IN + NSA_IN
D_FF = ((8 * D_MODEL // 3 + 255) // 256) * 256

kernel_name = 'hybrid_gated_parallel_mixer_trunk'


def rms_norm(x, g):
    xf = x.astype(F32)
    y = xf * lax.rsqrt(jnp.mean(xf * xf, axis=-1, keepdims=True) + RMS_EPS)
    return (y * g.astype(F32)).astype(x.dtype)


def causal_dwconv(x, w):
    k, ch = w.shape
    return lax.conv_general_dilated(x, w[:, None, :].astype(x.dtype), window_strides=(1,),
                                    padding=[(k - 1, 0)], dimension_numbers=('NWC', 'WIO', 'NWC'),
                                    feature_group_count=ch)


def masked_softmax(s, mask):
    s = jnp.where(mask, s, -jnp.inf)
    m = jnp.max(s, axis=-1, keepdims=True)
    m = jnp.where(jnp.isfinite(m), m, 0.0)
    p = jnp.exp(s - m)
    return p / jnp.maximum(jnp.sum(p, axis=-1, keepdims=True), 1e-30)


def alibi_slopes():
    return 2.0 ** (-(8.0 / NSA_HEADS) * jnp.arange(1, NSA_HEADS + 1, dtype=F32))


def segsum(a):
    L = a.shape[-1]
    x = jnp.broadcast_to(a[..., :, None], a.shape + (L,))
    x = jnp.where(jnp.tril(jnp.ones((L, L), bool), -1), x, 0.0)
    s = jnp.cumsum(x, axis=-2)
    return jnp.where(jnp.tril(jnp.ones((L, L), bool)), s, -jnp.inf)


def ssd_mixer(u, conv_w, conv_b, dt_bias, a_log, d_skip, norm_g):
    Bsz, T, _ = u.shape
    H, P, G, N, Q = SSD_HEADS, SSD_HEAD_DIM, SSD_GROUPS, SSD_STATE, SSD_CHUNK
    R = H // G
    nc = T // Q
    z = u[..., :MIX_W]
    xbc = jax.nn.silu(causal_dwconv(u[..., MIX_W:MIX_W + SSD_XBC], conv_w) + conv_b)
    dt = jax.nn.softplus(u[..., MIX_W + SSD_XBC:].astype(F32) + dt_bias.astype(F32))
    xs = xbc[..., :MIX_W].astype(F32).reshape(Bsz, T, H, P)
    bm = xbc[..., MIX_W:MIX_W + G * N].astype(F32).reshape(Bsz, nc, Q, G, N)
    cm = xbc[..., MIX_W + G * N:].astype(F32).reshape(Bsz, nc, Q, G, N)
    xdt = (xs * dt[..., None]).reshape(Bsz, nc, Q, G, R, P)
    dta = (dt * -jnp.exp(a_log.astype(F32))).reshape(Bsz, nc, Q, G, R).transpose(0, 3, 4, 1, 2)
    a_cs = jnp.cumsum(dta, axis=-1)
    l_mat = jnp.exp(segsum(dta))
    cb = jnp.einsum('bclgn,bcsgn->bgcls', cm, bm)
    y_diag = jnp.einsum('bgrcls,bcsgrp->bclgrp', cb[:, :, None] * l_mat, xdt)
    decay_states = jnp.exp(a_cs[..., -1:] - a_cs)
    chunk_states = jnp.einsum('bclgn,bgrcl,bclgrp->cbgrpn', bm, decay_states, xdt)
    chunk_decay = jnp.moveaxis(jnp.exp(a_cs[..., -1]), -1, 0)

    def carry_state(h, inp):
        s_c, d_c = inp
        return h * d_c[..., None, None] + s_c, h

    _, h_in = lax.scan(carry_state, jnp.zeros((Bsz, G, R, P, N), F32), (chunk_states, chunk_decay))
    y_off = jnp.einsum('bclgn,cbgrpn,bgrcl->bclgrp', cm, h_in, jnp.exp(a_cs))
    y = (y_diag + y_off).reshape(Bsz, T, H, P) + xs * d_skip.astype(F32)[:, None]
    y = y.reshape(Bsz, T, MIX_W) * jax.nn.silu(z.astype(F32))
    y = rms_norm(y.reshape(Bsz, T, G, MIX_W // G), norm_g.reshape(G, MIX_W // G))
    return y.reshape(Bsz, T, MIX_W).astype(u.dtype)


def short_conv_mixer(u, conv_w):
    b_gate, c_gate, hx = jnp.split(u, 3, axis=-1)
    return b_gate * causal_dwconv(c_gate * hx, conv_w)


def spatial_gating_mixer(u, norm_g, w_s, b_s):
    Bsz, T, _ = u.shape
    uv = jax.nn.gelu(u)
    gate_u = uv[..., :MIX_W]
    v = rms_norm(uv[..., MIX_W:], norm_g)
    nc = T // SG_CHUNK
    v = v.reshape(Bsz, nc, SG_CHUNK, SG_GROUPS, MIX_W // SG_GROUPS)
    w = jnp.where(jnp.tril(jnp.ones((SG_CHUNK, SG_CHUNK), bool)), w_s, 0.0).astype(v.dtype)
    v = jnp.einsum('gts,bcsgd->bctgd', w, v) + b_s.T[:, :, None]
    return gate_u * v.reshape(Bsz, T, MIX_W)


def nsa_mixer(u, q_norm_g, k_norm_g, cmp_pe, cmp_w1, cmp_w2):
    Bsz, T, _ = u.shape
    H, G, R, dh = NSA_HEADS, NSA_KV_HEADS, NSA_REP, NSA_HEAD_DIM
    q_width = H * dh
    scale = dh ** -0.5
    q = rms_norm(u[..., :q_width].reshape(Bsz, T, H, dh), q_norm_g)
    q = q.reshape(Bsz, T, G, R, dh).transpose(0, 2, 3, 1, 4)
    kv = u[..., q_width:q_width + 6 * NSA_KV_W].reshape(Bsz, T, 6, G, dh)
    k_cmp, v_cmp, k_slc, v_slc, k_win, v_win = (kv[:, :, i] for i in range(6))
    gates = jax.nn.sigmoid(u[..., q_width + 6 * NSA_KV_W:].astype(F32))
    gates = gates.reshape(Bsz, T, G, R, 3).transpose(0, 2, 3, 1, 4)
    slopes = alibi_slopes().reshape(G, R)[None, :, :, None, None]
    t_pos = jnp.arange(T)

    n_cmp = (T - CMP_BLOCK) // CMP_STRIDE + 1
    cmp_start = jnp.arange(n_cmp) * CMP_STRIDE
    cmp_idx = cmp_start[:, None] + jnp.arange(CMP_BLOCK)[None, :]

    def compress(t, pe, w1, w2):
        blk = t[:, cmp_idx] + pe[:, None, :]
        blk = blk.transpose(0, 1, 3, 2, 4).reshape(Bsz, n_cmp, G, CMP_BLOCK * dh)
        return jax.nn.gelu(blk @ w1) @ w2

    kc = rms_norm(compress(k_cmp, cmp_pe[0], cmp_w1[0], cmp_w2[0]), k_norm_g[0])
    vc = compress(v_cmp, cmp_pe[1], cmp_w1[1], cmp_w2[1])
    dist_c = t_pos[:, None] - (cmp_start + CMP_BLOCK - 1)[None, :]
    s_c = jnp.einsum('bgrtd,bngd->bgrtn', q, kc, preferred_element_type=F32) * scale - slopes * dist_c.astype(F32)
    p_cmp = masked_softmax(s_c, dist_c >= 0)
    o_cmp = jnp.einsum('bgrtn,bngd->bgrtd', p_cmp.astype(vc.dtype), vc)

    n_slc = T // SLC_BLOCK
    top_n = min(SLC_TOPN, n_slc)
    slc_start = jnp.arange(n_slc) * SLC_BLOCK
    overlap = ((cmp_start[:, None] < slc_start[None, :] + SLC_BLOCK)
               & (cmp_start[:, None] + CMP_BLOCK > slc_start[None, :])).astype(F32)
    importance = jnp.einsum('bgrtn,nj->bgtj', p_cmp, overlap)
    cur = t_pos // SLC_BLOCK
    jb = jnp.arange(n_slc)
    forced = (jb[None, :] == 0) | (jb[None, :] == cur[:, None]) | (jb[None, :] == cur[:, None] - 1)
    future = jb[None, :] > cur[:, None]
    score = jnp.where(future, -jnp.inf, jnp.where(forced, FORCE_SCORE, importance))
    top_val, top_idx = lax.top_k(score, top_n)
    top_ok = top_val > -jnp.inf

    ks = rms_norm(k_slc, k_norm_g[1]).reshape(Bsz, n_slc, SLC_BLOCK, G, dh).transpose(0, 3, 1, 2, 4)
    vs = v_slc.reshape(Bsz, n_slc, SLC_BLOCK, G, dh).transpose(0, 3, 1, 2, 4)
    gather = jax.vmap(jax.vmap(lambda blocks, ids: blocks[ids]))
    nqb = T // SLC_QBLOCK
    n_keys = top_n * SLC_BLOCK
    q_sb = jnp.moveaxis(q.reshape(Bsz, G, R, nqb, SLC_QBLOCK, dh), 3, 0)
    idx_sb = jnp.moveaxis(top_idx.reshape(Bsz, G, nqb, SLC_QBLOCK, top_n), 2, 0)
    ok_sb = jnp.moveaxis(top_ok.reshape(Bsz, G, nqb, SLC_QBLOCK, top_n), 2, 0)

    def slc_block(args):
        qb, ib, okb, t0 = args
        kg = gather(ks, ib).reshape(Bsz, G, SLC_QBLOCK, n_keys, dh)
        vg = gather(vs, ib).reshape(Bsz, G, SLC_QBLOCK, n_keys, dh)
        tq = t0 + jnp.arange(SLC_QBLOCK)
        kpos = (ib[..., None] * SLC_BLOCK + jnp.arange(SLC_BLOCK)).reshape(Bsz, G, SLC_QBLOCK, n_keys)
        dist = tq[:, None] - kpos
        ok = jnp.repeat(okb, SLC_BLOCK, axis=-1) & (dist >= 0)
        s = jnp.einsum('bgrqd,bgqkd->bgrqk', qb, kg, preferred_element_type=F32) * scale \
            - slopes * dist[:, :, None].astype(F32)
        p = masked_softmax(s, ok[:, :, None])
        return jnp.einsum('bgrqk,bgqkd->bgrqd', p.astype(vg.dtype), vg)

    o_slc = lax.map(slc_block, (q_sb, idx_sb, ok_sb, jnp.arange(nqb) * SLC_QBLOCK))
    o_slc = jnp.moveaxis(o_slc, 0, 3).reshape(Bsz, G, R, T, dh)

    nwb = T // WIN_QBLOCK
    nback = WIN // WIN_QBLOCK

    def band(t):
        tb = t.reshape(Bsz, nwb, WIN_QBLOCK, G, dh)
        tb = jnp.pad(tb, ((0, 0), (nback, 0), (0, 0), (0, 0), (0, 0)))
        tb = jnp.concatenate([tb[:, i:i + nwb] for i in range(nback + 1)], axis=2)
        return tb.transpose(1, 0, 3, 2, 4)

    kw = band(rms_norm(k_win, k_norm_g[2]))
    vw = band(v_win)
    q_wb = jnp.moveaxis(q.reshape(Bsz, G, R, nwb, WIN_QBLOCK, dh), 3, 0)

    def win_block(args):
        qb, kb, vb, t0 = args
        tq = t0 + jnp.arange(WIN_QBLOCK)
        kpos = t0 - nback * WIN_QBLOCK + jnp.arange((nback + 1) * WIN_QBLOCK)
        dist = tq[:, None] - kpos[None, :]
        ok = (dist >= 0) & (dist < WIN) & (kpos[None, :] >= 0)
        s = jnp.einsum('bgrqd,bgkd->bgrqk', qb, kb, preferred_element_type=F32) * scale - slopes * dist.astype(F32)
        p = masked_softmax(s, ok)
        return jnp.einsum('bgrqk,bgkd->bgrqd', p.astype(vb.dtype), vb)

    o_win = lax.map(win_block, (q_wb, kw, vw, jnp.arange(nwb) * WIN_QBLOCK))
    o_win = jnp.moveaxis(o_win, 0, 3).reshape(Bsz, G, R, T, dh)

    o = gates[..., 0:1] * o_cmp + gates[..., 1:2] * o_slc + gates[..., 2:3] * o_win
    return o.transpose(0, 3, 1, 2, 4).reshape(Bsz, T, q_width).astype(u.dtype)


def hybrid_layer(x, c, ada_w, ada_b, norm_mix_g, norm_ffn_g, w_in,
                 ssd_conv_w, ssd_conv_b, ssd_dt_bias, ssd_a_log, ssd_d, ssd_norm_g,
                 sc_conv_w, sg_norm_g, sg_w, sg_b,
                 nsa_q_norm_g, nsa_k_norm_g, nsa_cmp_pe, nsa_cmp_w1, nsa_cmp_w2,
                 w_branch, w_branch_gate, w_out, w_ffn_in, w_ffn_out):
    mod = jax.nn.silu(c) @ ada_w + ada_b
    shift1, scale1, gate1, shift2, scale2, gate2 = jnp.split(mod[:, None, :], 6, axis=-1)
    h = rms_norm(x, norm_mix_g) * (1.0 + scale1) + shift1
    u = h @ w_in
    o1 = SSD_IN
    o2 = o1 + SC_IN
    o3 = o2 + SG_IN
    branches = (
        ssd_mixer(u[..., :o1], ssd_conv_w, ssd_conv_b, ssd_dt_bias, ssd_a_log, ssd_d, ssd_norm_g),
        short_conv_mixer(u[..., o1:o2], sc_conv_w),
        spatial_gating_mixer(u[..., o2:o3], sg_norm_g, sg_w, sg_b),
        nsa_mixer(u[..., o3:], nsa_q_norm_g, nsa_k_norm_g, nsa_cmp_pe, nsa_cmp_w1, nsa_cmp_w2),
    )
    merged = jnp.zeros_like(h)
    for i in range(N_BRANCH):
        merged = merged + jax.nn.sigmoid(h @ w_branch_gate[i]) * (branches[i] @ w_branch[i])
    x = x + gate1 * (merged @ w_out)
    h2 = rms_norm(x, norm_ffn_g) * (1.0 + scale2) + shift2
    a, b = jnp.split(h2 @ w_ffn_in, 2, axis=-1)
    return x + gate2 * ((jax.nn.silu(a) * b) @ w_ffn_out)


def setup_inputs(seed: int = 0) -> dict:
    key = jax.random.key(seed)
    ks = iter(jax.random.split(key, 32))
    L, D = DEPTH, D_MODEL

    def nrm(shape, s):
        return s * jax.random.normal(next(ks), shape, F32)

    dt = jnp.exp(jax.random.uniform(next(ks), (L, SSD_HEADS), F32, math.log(1e-3), math.log(1e-1)))
    a_init = jax.random.uniform(next(ks), (L, SSD_HEADS), F32, 1.0, 16.0)
    return {
        'x': nrm((BATCH, SEQ, D), 1.0),
        'c': nrm((BATCH, D), 1.0),
        'ada_w': nrm((L, D, 6 * D), 0.5 * D ** -0.5),
        'ada_b': nrm((L, 6 * D), 0.01),
        'norm_mix_g': 1.0 + nrm((L, D), 0.05),
        'norm_ffn_g': 1.0 + nrm((L, D), 0.05),
        'w_in': nrm((L, D, D_IN), D ** -0.5),
        'ssd_conv_w': nrm((L, SSD_CONV, SSD_XBC), SSD_CONV ** -0.5),
        'ssd_conv_b': nrm((L, SSD_XBC), 0.02),
        'ssd_dt_bias': dt + jnp.log(-jnp.expm1(-dt)),
        'ssd_a_log': jnp.log(a_init),
        'ssd_d': 1.0 + nrm((L, SSD_HEADS), 0.1),
        'ssd_norm_g': 1.0 + nrm((L, MIX_W), 0.05),
        'sc_conv_w': nrm((L, SC_CONV, MIX_W), SC_CONV ** -0.5),
        'sg_norm_g': 1.0 + nrm((L, MIX_W), 0.05),
        'sg_w': nrm((L, SG_GROUPS, SG_CHUNK, SG_CHUNK), 0.5 * SG_CHUNK ** -0.5),
        'sg_b': 1.0 + nrm((L, SG_GROUPS, SG_CHUNK), 0.1),
        'nsa_q_norm_g': 1.0 + nrm((L, NSA_HEAD_DIM), 0.05),
        'nsa_k_norm_g': 1.0 + nrm((L, 3, NSA_HEAD_DIM), 0.05),
        'nsa_cmp_pe': nrm((L, 2, CMP_BLOCK, NSA_HEAD_DIM), 0.1),
        'nsa_cmp_w1': nrm((L, 2, CMP_BLOCK * NSA_HEAD_DIM, NSA_HEAD_DIM), (CMP_BLOCK * NSA_HEAD_DIM) ** -0.5),
        'nsa_cmp_w2': nrm((L, 2, NSA_HEAD_DIM, NSA_HEAD_DIM), NSA_HEAD_DIM ** -0.5),
        'w_branch': nrm((L, N_BRANCH, MIX_W, D), MIX_W ** -0.5),
        'w_branch_gate': nrm((L, N_BRANCH, D, D), D ** -0.5),
        'w_out': nrm((L, D, D), D ** -0.5),
        'w_ffn_in': nrm((L, D, 2 * D_FF), D ** -0.5),
        'w_ffn_out': nrm((L, D_FF, D), D_FF ** -0.5),
    }


def reference(x, c, ada_w, ada_b, norm_mix_g, norm_ffn_g, w_in,
              ssd_conv_w, ssd_conv_b, ssd_dt_bias, ssd_a_log, ssd_d, ssd_norm_g,
              sc_conv_w, sg_norm_g, sg_w, sg_b,
              nsa_q_norm_g, nsa_k_norm_g, nsa_cmp_pe, nsa_cmp_w1, nsa_cmp_w2,
              w_branch, w_branch_gate, w_out, w_ffn_in, w_ffn_out):
    for l in range(DEPTH):
        x = hybrid_layer(x, c, ada_w[l], ada_b[l], norm_mix_g[l], norm_ffn_g[l], w_in[l],
                         ssd_conv_w[l], ssd_conv_b[l], ssd_dt_bias[l], ssd_a_log[l], ssd_d[l], ssd_norm_g[l],
                         sc_conv_w[l], sg_norm_g[l], sg_w[l], sg_b[l],
                         nsa_q_norm_g[l], nsa_k_norm_g[l], nsa_cmp_pe[l], nsa_cmp_w1[l], nsa_cmp_w2[l],
                         w_branch[l], w_branch_gate[l], w_out[l], w_ffn_in[l], w_ffn_out[l])
    return x
```

```python
import numpy as np
import concourse.bass as bass
import concourse.mybir as mybir
from concourse.bass_utils import run_bass_kernel_spmd
from contextlib import ExitStack

F32 = mybir.dt.float32
BF16 = mybir.dt.bfloat16
AF = mybir.ActivationFunctionType
ALU = mybir.AluOpType
AX = mybir.AxisListType

ENGS = ("pe", "act", "dve", "pool", "sp")
N_DMA_SEMS = 16

D = 1024
T = 2048
DEPTH = 4
NCORES = 8
SEQ_PER_CORE = 2
KC = 8
TT = 512
NTT = 4
NB = 16
D_IN = 5408
D_FF = 2816
NHC = 22
EPS = 1e-6
NEG = -30000.0
C_Z, C_XBC, C_DT = 0, 512, 1536
C_B, C_C, C_HX = 1544, 2056, 2568
C_GU, C_GV = 3080, 3592
C_Q = 4104
C_KCMP, C_VCMP, C_KSLC, C_VSLC, C_KWIN, C_VWIN = 4616, 4744, 4872, 5000, 5128, 5256
C_GATES = 5384


class Buf:
    __slots__ = ("t", "name", "nslots", "lw", "rd")

    def __init__(self, t, name, nslots=1):
        self.t = t
        self.name = name
        self.nslots = nslots
        self.lw = [None] * nslots
        self.rd = [[] for _ in range(nslots)]

    def __getitem__(self, idx):
        return self.t[idx]


class Op:
    __slots__ = ("eng", "fn", "deps", "is_dma", "sig", "sig_idx", "dsem", "dval", "idx", "waits", "prewait")

    def __init__(self, eng, fn, is_dma):
        self.eng = eng
        self.fn = fn
        self.is_dma = is_dma
        self.deps = set()
        self.sig = False
        self.sig_idx = 0
        self.dsem = None
        self.dval = 0
        self.waits = []
        self.prewait = None


class Prog:
    def __init__(self, nc, stack):
        self.nc = nc
        self.stack = stack
        self.ops = []
        self.final_dma = []
        self.nbuf = 0

    def sbuf(self, shape, dtype, name=None, nslots=1):
        self.nbuf += 1
        name = f"sb{self.nbuf}_{name or ""}"
        t = self.stack.enter_context(self.nc.sbuf_tensor(name, list(shape), dtype))
        return Buf(t, name, nslots)

    def psum(self, shape, dtype=F32, name=None, nslots=1):
        self.nbuf += 1
        name = f"ps{self.nbuf}_{name or ""}"
        t = self.stack.enter_context(self.nc.psum_tensor(name, list(shape), dtype))
        return Buf(t, name, nslots)

    @staticmethod
    def _norm(acc):
        out = []
        for a in acc:
            if a is None:
                continue
            if isinstance(a, Buf):
                out.append((a, range(a.nslots)))
            else:
                b, s = a
                if isinstance(s, int):
                    s = (s,)
                out.append((b, s))
        return out

    def op(self, eng, fn, reads=(), writes=(), dma=False, final=False):
        o = Op(eng, fn, dma)
        o.idx = len(self.ops)
        rl = self._norm(reads)
        wl = self._norm(writes)
        for b, slots in rl:
            for s in slots:
                w = b.lw[s]
                if w is not None:
                    o.deps.add(w)
        for b, slots in wl:
            for s in slots:
                w = b.lw[s]
                if w is not None:
                    o.deps.add(w)
                for r in b.rd[s]:
                    o.deps.add(r)
        for b, slots in rl:
            for s in slots:
                b.rd[s].append(o)
        for b, slots in wl:
            for s in slots:
                b.lw[s] = o
                b.rd[s] = []
        o.deps.discard(o)
        self.ops.append(o)
        if final:
            self.final_dma.append(o)
        return o

    def mm(self, out, lhsT, rhs, start=True, stop=True, reads=(), writes=(), **kw):
        return self.op("pe", lambda e: e.matmul(out, lhsT, rhs, start=start, stop=stop, **kw), reads, writes)

    def tr(self, out, in_, ident, reads=(), writes=()):
        return self.op("pe", lambda e: e.transpose(out, in_, ident), reads, writes)

    def act(self, out, in_, func, reads=(), writes=(), **kw):
        return self.op("act", lambda e: e.activation(out, in_, func, **kw), reads, writes)

    def tt(self, eng, out, in0, in1, op, reads=(), writes=()):
        return self.op(eng, lambda e: e.tensor_tensor(out, in0, in1, op), reads, writes)

    def ts(self, eng, out, in0, s1, s2, op0, op1=None, reads=(), writes=()):
        if op1 is None:
            return self.op(eng, lambda e: e.tensor_scalar(out, in0, s1, None, op0), reads, writes)
        return self.op(eng, lambda e: e.tensor_scalar(out, in0, s1, s2, op0, op1), reads, writes)

    def stt(self, out, in0, scalar, in1, op0, op1, reads=(), writes=()):
        return self.op("dve", lambda e: e.scalar_tensor_tensor(out, in0, scalar, in1, op0, op1), reads, writes)

    def copy(self, eng, out, in_, reads=(), writes=()):
        if eng == "act":
            return self.op(eng, lambda e: e.copy(out, in_), reads, writes)
        return self.op(eng, lambda e: e.tensor_copy(out, in_), reads, writes)

    def dma(self, out, in_, reads=(), writes=(), eng="sp", final=False, **kw):
        if eng == "pool":
            kw.setdefault("max_dma_last_dim", 4096)
        return self.op(eng, lambda e: e.dma_start(out, in_, **kw), reads, writes, dma=True, final=final)

    def finalize(self):
        ops = self.ops
        for o in ops:
            need = []
            for d in o.deps:
                if d.is_dma or o.is_dma:
                    need.append(d)
                elif d.eng != o.eng:
                    need.append(d)
                elif o.eng != "pe":
                    need.append(d)
            best = {}
            keep = []
            for d in need:
                if d.is_dma:
                    keep.append(d)
                else:
                    b = best.get(d.eng)
                    if b is None or d.idx > b.idx:
                        best[d.eng] = d
            keep.extend(best.values())
            o.deps = keep
            for d in keep:
                if not d.is_dma:
                    d.sig = True
        cnt = {e: 0 for e in ENGS}
        dcnt = {e: 0 for e in ENGS}
        for o in ops:
            if o.is_dma:
                i = dcnt[o.eng]
                dcnt[o.eng] += 1
                o.dsem = (o.eng, i % N_DMA_SEMS)
                o.dval = 16 * (i // N_DMA_SEMS + 1)
                if i >= N_DMA_SEMS:
                    o.prewait = (o.dsem, o.dval - 16)
            elif o.sig:
                cnt[o.eng] += 1
                o.sig_idx = cnt[o.eng]
        waited = {e: {} for e in ENGS}
        nw = 0
        for o in ops:
            w = waited[o.eng]
            req = {}
            if o.prewait is not None:
                req[o.prewait[0]] = o.prewait[1]
            for d in o.deps:
                if d.is_dma:
                    k, v = d.dsem, d.dval
                else:
                    k, v = d.eng, d.sig_idx
                if req.get(k, 0) < v:
                    req[k] = v
            for k, v in req.items():
                if w.get(k, 0) < v:
                    w[k] = v
                    o.waits.append((k, v))
                    nw += 1
        self.stats = dict(n_ops=len(ops), n_waits=nw, sig=dict(cnt), dma=dict(dcnt),
                          per_eng={e: sum(1 for o in ops if o.eng == e) for e in ENGS})
        return self.stats

    def emit(self):
        nc = self.nc
        st = self.stack
        esem = {e: st.enter_context(nc.semaphore(f"s_{e}")) for e in ENGS}
        dsem = {}
        for e in ENGS:
            if self.stats["dma"][e]:
                for j in range(N_DMA_SEMS):
                    dsem[(e, j)] = st.enter_context(nc.semaphore(f"d_{e}{j}"))

        def semof(k):
            return dsem[k] if isinstance(k, tuple) else esem[k]

        per = {e: [o for o in self.ops if o.eng == e] for e in ENGS}
        finals = self.final_dma
        block = st.enter_context(nc.Block())

        def run(eh, name):
            for o in per[name]:
                for k, v in o.waits:
                    eh.wait_ge(semof(k), v)
                ins = o.fn(eh)
                if o.is_dma:
                    ins.then_inc(dsem[o.dsem], 16)
                elif o.sig:
                    ins.then_inc(esem[name], 1)
            if name == "sp":
                for o in finals:
                    eh.wait_ge(dsem[o.dsem], o.dval)

        @block.tensor
        def _(e):
            run(e, "pe")

        @block.scalar
        def _(e):
            run(e, "act")

        @block.vector
        def _(e):
            run(e, "dve")

        @block.gpsimd
        def _(e):
            run(e, "pool")

        @block.sync
        def _(e):
            run(e, "sp")


CF_IDENT, CF_TRIU, CF_ONES, CF_M1, CF_M2 = 0, 128, 256, 384, 896
NCF = 1408
CB_IDENT, CB_ONES, CB_NEGM, CB_CM, CB_DM, CB_WM, CB_E, CB_OVL = 0, 128, 256, 384, 2432, 4480, 4864, 6912
NCB = 6945


def _const_tables():
    p = np.arange(128)
    ctf = np.zeros((128, NCF), np.float32)
    ctf[:, CF_IDENT:CF_IDENT + 128] = np.eye(128)
    ctf[:, CF_TRIU:CF_TRIU + 128] = (p[:, None] <= p[None, :])
    ctf[:, CF_ONES:CF_ONES + 128] = 1.0
    m1 = np.zeros((128, 16, 32), np.float32)
    m2 = np.zeros((128, 16, 32), np.float32)
    j = np.arange(32)
    for qb in range(16):
        t = qb * 128 + p
        cur = t // 64
        forced = (j[None, :] == 0) | (j[None, :] == cur[:, None]) | (j[None, :] == cur[:, None] - 1)
        future = j[None, :] > cur[:, None]
        m1[:, qb, :] = np.where(forced | future, 0.0, 1.0)
        m2[:, qb, :] = np.where(future, -1e30, np.where(forced, 1e9, 0.0))
    ctf[:, CF_M1:CF_M1 + 512] = m1.reshape(128, 512)
    ctf[:, CF_M2:CF_M2 + 512] = m2.reshape(128, 512)

    ctb = np.zeros((128, NCB), np.float32)
    ctb[:, CB_IDENT:CB_IDENT + 128] = np.eye(128)
    ctb[:, CB_ONES:CB_ONES + 128] = 1.0
    ctb[:, CB_NEGM:CB_NEGM + 128] = np.where(p[None, :] < p[:, None], -10000.0, 0.0)
    tpos = np.arange(T)
    n = np.arange(128)
    ctb[:, CB_CM:CB_CM + T] = np.where(tpos[None, :] >= 16 * n[:, None] + 31, 0.0, NEG)
    dm = np.zeros((128, 4, 512), np.float32)
    tl = np.arange(512)
    for i in range(4):
        dm[:, i, :] = np.where(tl[None, :] >= 128 * i + p[:, None], 0.0, NEG)
    ctb[:, CB_DM:CB_DM + 2048] = dm.reshape(128, 2048)
    wm = np.zeros((128, 3, 128), np.float32)
    b = np.arange(128)
    wm[:, 0, :] = np.where(b[None, :] >= p[:, None], 0.0, NEG)
    wm[:, 2, :] = np.where(b[None, :] < p[:, None], 0.0, NEG)
    ctb[:, CB_WM:CB_WM + 384] = wm.reshape(128, 384)
    e = np.zeros((128, 16, 128), np.float32)
    for c in range(16):
        for key in range(128):
            e[2 * c + key // 64, c, key] = 1.0
    ctb[:, CB_E:CB_E + 2048] = e.reshape(128, 2048)
    ovl = np.zeros((128, 33), np.float32)
    cs = 16 * n
    ss = 64 * np.arange(32)
    ovl[:, 0:32] = ((cs[:, None] < ss[None, :] + 64) & (cs[:, None] + 32 > ss[None, :]))
    ovl[:, 32] = 1.0
    ctb[:, CB_OVL:CB_OVL + 33] = ovl

    lk = np.zeros((4, T), np.float32)
    lk[0] = 128 * (tpos // 128)
    lk[1] = tpos % 128
    lk[2] = 1.0
    lk[3] = 1.0
    lkc = np.zeros((4, 128), np.float32)
    lkc[0] = 16 * n
    lkc[1] = 31
    lkc[2] = 1.0
    lkc[3] = 1.0
    rh = np.zeros((4, 8, T), np.float32)
    for h in range(8):
        s8 = 8.0 * 2.0 ** (-(h + 1))
        rh[0, h] = s8
        rh[1, h] = s8
        rh[2, h] = -s8 * 128 * (tpos // 128)
        rh[3, h] = -s8 * (tpos % 128)
    return ctf, ctb, lk, lkc, rh


PP_CONVW, PP_CONVB, PP_DTB, PP_ALOG, PP_DSKIP, PP_SSDNG = 0, 32, 40, 48, 56, 64
PP_SCW, PP_SGNG, PP_SGB, PP_QG, PP_KG12, PP_KG0, PP_PE = 576, 588, 1100, 1612, 1613, 1615, 1679
NPP = 1711
PL_ADAB, PL_GMIX, PL_GFFN = 0, 48, 56
NPL = 64


def _pack_params(inp):
    L = DEPTH
    rep = lambda v: np.broadcast_to(np.asarray(v, np.float32).reshape(1, -1), (128, v.size))
    pp = np.zeros((L, 128, NPP), np.float32)
    pl = np.zeros((128, L, NPL), np.float32)
    for l in range(L):
        pp[l, :, PP_CONVW:PP_CONVW + 32] = inp["ssd_conv_w"][l].reshape(4, 8, 128).transpose(2, 1, 0).reshape(128, 32)
        pp[l, :, PP_CONVB:PP_CONVB + 8] = inp["ssd_conv_b"][l].reshape(8, 128).T
        pp[l, :, PP_DTB:PP_DTB + 8] = rep(inp["ssd_dt_bias"][l])
        pp[l, :, PP_ALOG:PP_ALOG + 8] = rep(inp["ssd_a_log"][l])
        pp[l, :, PP_DSKIP:PP_DSKIP + 8] = rep(inp["ssd_d"][l])
        pp[l, :, PP_SSDNG:PP_SSDNG + 512] = rep(inp["ssd_norm_g"][l])
        pp[l, :, PP_SCW:PP_SCW + 12] = inp["sc_conv_w"][l].reshape(3, 4, 128).transpose(2, 1, 0).reshape(128, 12)
        pp[l, :, PP_SGNG:PP_SGNG + 512] = rep(inp["sg_norm_g"][l])
        pp[l, :, PP_SGB:PP_SGB + 512] = rep(inp["sg_b"][l])
        pp[l, :, PP_QG] = np.tile(inp["nsa_q_norm_g"][l], 2)
        pp[l, :, PP_KG12] = np.tile(inp["nsa_k_norm_g"][l, 1], 2)
        pp[l, :, PP_KG12 + 1] = np.tile(inp["nsa_k_norm_g"][l, 2], 2)
        pp[l, :, PP_KG0:PP_KG0 + 64] = rep(inp["nsa_k_norm_g"][l, 0])
        pp[l, :, PP_PE:PP_PE + 32] = inp["nsa_cmp_pe"][l].reshape(2, 16, 128).transpose(2, 0, 1).reshape(128, 32)
        pl[:, l, PL_ADAB:PL_ADAB + 48] = inp["ada_b"][l].reshape(48, 128).T
        pl[:, l, PL_GMIX:PL_GMIX + 8] = inp["norm_mix_g"][l].reshape(8, 128).T
        pl[:, l, PL_GFFN:PL_GFFN + 8] = inp["norm_ffn_g"][l].reshape(8, 128).T
    return pp, pl


def build_program(n_layers=DEPTH, n_seq=SEQ_PER_CORE, debug=False, stop=None):
    nc = bass.Bass("TRN2", target_bir_lowering=False)
    L = DEPTH
    din = lambda name, shape: nc.dram_tensor(name, list(shape), F32, kind="ExternalInput").ap()
    xT_d = din("xT", [n_seq, D, T])
    cT_d = din("cT", [128, KC, n_seq])
    pp_d = din("pp", [L, 128, NPP])
    pl_d = din("pl", [128, L, NPL])
    ctf_d = din("ctf", [128, NCF])
    ctb_d = din("ctb", [128, NCB])
    lk_d = din("lk", [4, T])
    lkc_d = din("lkc", [4, 128])
    rh_d = din("rh", [4, 8, T])
    sgw_d = din("sgwT", [L, 128, 4, 128])
    ada_w_d = din("ada_w", [L, D, 6 * D])
    w_in_d = din("w_in", [L, D, D_IN])
    w1_d = din("cmp_w1", [L, 2, 2048, 64])
    w2_d = din("cmp_w2", [L, 2, 64, 64])
    wb_d = din("w_branch", [L, 4, 512, D])
    wg_d = din("w_branch_gate", [L, 4, D, D])
    wo_d = din("w_out", [L, D, D])
    wfi_d = din("w_ffn_in", [L, D, 2 * D_FF])
    wfo_d = din("w_ffn_out", [L, D_FF, D])
    outT_d = nc.dram_tensor("outT", [n_seq, D, T], F32, kind="ExternalOutput").ap()
    xa_d = nc.dram_tensor("xres_a", [D, T], F32, kind="Internal").ap()
    xb_d = nc.dram_tensor("xres_b", [D, T], F32, kind="Internal").ap()
    dbg_d = None
    if debug:
        dbg_d = nc.dram_tensor("dbg", [4, 128, 4, T], F32, kind="ExternalOutput").ap()

    with ExitStack() as st:
        P = Prog(nc, st)
        XA = Buf(xa_d, "xa", 32)
        XB = Buf(xb_d, "xb", 32)
        OUT = Buf(outT_d, "out", 32)

        ctf = P.sbuf([128, NCF], F32, "ctf")
        ctb = P.sbuf([128, NCB], BF16, "ctb")
        pp = P.sbuf([128, NPP], F32, "pp")
        plb = P.sbuf([128, L, NPL], F32, "pl")
        sc = P.sbuf([128, KC, n_seq], F32, "sc")
        modT = P.sbuf([128, L, 48, n_seq], F32, "modT")
        der = P.sbuf([128, 2, 8], F32, "der")
        hT = P.sbuf([128, KC, T], BF16, "hT", nslots=32)
        oT = [P.sbuf([128, 4, T], BF16, f"oT{i}", nslots=16) for i in range(4)]
        mT = P.sbuf([128, KC, T], BF16, "mT", nslots=32)
        wbufs = [P.sbuf([128, KC, 512], BF16, f"wb{i}") for i in range(3)]
        psb = [P.psum([128, 512], F32, f"bank{i}") for i in range(8)]
        sq_b = [P.sbuf([128, 512], BF16, f"sq{i}") for i in range(2)]
        f32s = [P.sbuf([128, 512], F32, f"fs{i}") for i in range(4)]
        f32l = [P.sbuf([128, 512], F32, f"fl{i}") for i in range(2)]
        smalls = P.sbuf([128, 256], F32, "smalls", nslots=1)
        dtb = P.sbuf([128, NB, 8], F32, "dt")
        gsig = P.sbuf([128, NB, 24], F32, "gsig")
        kcaug = P.sbuf([68, 2, 128], BF16, "kcaug")
        VC = P.sbuf([128, 2, 97], BF16, "VC")
        cbias = P.sbuf([64, 2], F32, "cbias")
        GTb = P.sbuf([64, 128], BF16, "GT")
        kcn = P.sbuf([128, 64], BF16, "kcn")
        peb = P.sbuf([128, 32], BF16, "peb")
        sgwb = P.sbuf([128, 4, 128], BF16, "sgwb")
        state = P.sbuf([128, 512], F32, "state")
        stateb = P.sbuf([128, 512], BF16, "stateb")
        tails = P.sbuf([128, 8, 3], F32, "tails")
        raw = [P.sbuf([128, 515], F32, f"raw{i}") for i in range(2)]
        zero_b = P.sbuf([1, 260], BF16, "zerob")
        negA = P.sbuf([128, 8], F32, "negA")

        identF = ctf[:, CF_IDENT:CF_IDENT + 128]
        triU = ctf[:, CF_TRIU:CF_TRIU + 128]
        onesF = ctf[:, CF_ONES:CF_ONES + 128]
        identB = ctb[:, CB_IDENT:CB_IDENT + 128]
        onesB = ctb[:, CB_ONES:CB_ONES + 128]
        NEGM = ctb[:, CB_NEGM:CB_NEGM + 128]

        def CS(buf, chunks):
            if isinstance(chunks, int):
                chunks = (chunks,)
            return (buf, [4 * c + i for c in chunks for i in range(4)])

        def TS(buf, chunk, tt):
            return (buf, 4 * chunk + tt)

        def TSA(buf, nch, tt):
            return (buf, [4 * c + tt for c in range(nch)])

        held = set()
        rr = {"ps": 0, "wb": 0, "sq": 0, "fs": 0, "raw": 0, "fl": 0}

        def ps_get(hold=False):
            for _ in range(16):
                i = rr["ps"] % 8
                rr["ps"] += 1
                if i not in held:
                    if hold:
                        held.add(i)
                    return psb[i]
            raise RuntimeError("no psum bank")

        def ps_release(b):
            held.discard(psb.index(b))

        def psbf(b):
            return b.t[:].bitcast(BF16)

        def wb_get():
            i = rr["wb"] % 3
            rr["wb"] += 1
            return wbufs[i]

        def sq_get():
            i = rr["sq"] % 2
            rr["sq"] += 1
            return sq_b[i]

        def fs_get():
            i = rr["fs"] % 4
            rr["fs"] += 1
            return f32s[i]

        def fl_get():
            i = rr["fl"] % 2
            rr["fl"] += 1
            return f32l[i]

        def wload(dst_ap, src_ap, wbuf):
            P.dma(dst_ap, src_ap, writes=[wbuf], eng="pool")

        def kpn(ap2d):
            return ap2d.rearrange("(k p) n -> p k n", p=128)

        def mt_view(s0, ns, parts, shape_tail, dtype=BF16):
            return _view(mT, s0, ns, parts, shape_tail, dtype)

        def _view(buf, s0, ns, parts, shape_tail, dtype=BF16):
            ap = buf.t[0:parts, s0:s0 + ns, :].rearrange("p a b -> p (a b)")
            if dtype == F32:
                ap = ap.bitcast(F32)
            n = int(np.prod(shape_tail))
            ap = ap[:, 0:n]
            if len(shape_tail) == 1:
                return ap
            if len(shape_tail) == 2:
                return ap.rearrange("p (a b) -> p a b", a=shape_tail[0])
            if len(shape_tail) == 3:
                return ap.rearrange("p (a b c) -> p a b c", a=shape_tail[0], b=shape_tail[1])
            raise ValueError

        P.dma(ctf[:], ctf_d, writes=[ctf])
        P.dma(ctb[:], ctb_d, writes=[ctb], eng="pool")
        P.dma(plb[:], pl_d, writes=[plb])
        P.dma(sc[:], cT_d, writes=[sc])
        P.op("dve", lambda e: e.memset(zero_b[:], 0.0), writes=[zero_b])
        P.dma(kcaug[64:68, 0, :], lkc_d, writes=[kcaug], eng="pool")
        P.dma(kcaug[64:68, 1, :], lkc_d, writes=[kcaug], eng="pool")
        P.copy("dve", VC[:, 0, 64:97], ctb[:, CB_OVL:CB_OVL + 33], reads=[ctb], writes=[VC])
        P.copy("dve", VC[:, 1, 64:97], ctb[:, CB_OVL:CB_OVL + 33], reads=[ctb], writes=[VC])
        P.act(sc[:], sc[:], AF.Silu, reads=[sc], writes=[sc])
        stg = [(_view(hT, 4 * i, 4, 128, [KC, 512], F32), [CS(hT, range(4 * i, 4 * i + 4))]) for i in range(2)]
        nblk = 0
        for l in range(n_layers):
            for cb in range(12):
                sv, sacc = stg[nblk % 2]
                nblk += 1
                P.dma(sv, kpn(ada_w_d[l][:, cb * 512:(cb + 1) * 512]), writes=sacc)
                ps = ps_get()
                for m in range(4):
                    for k in range(KC):
                        P.mm(ps[:, m * n_seq:(m + 1) * n_seq], sv[:, k, m * 128:(m + 1) * 128], sc[:, k, :],
                             start=(k == 0), stop=(k == KC - 1), reads=sacc + [sc], writes=[ps])
                P.tt("dve", modT[:, l, cb * 4:(cb + 1) * 4, :],
                     ps[:, 0:4 * n_seq].rearrange("p (m s) -> p m s", m=4),
                     plb[:, l, PL_ADAB + cb * 4:PL_ADAB + (cb + 1) * 4].unsqueeze(2).broadcast_to([128, 4, n_seq]),
                     ALU.add, reads=[ps, plb], writes=[modT])

        def xsrc_ap(src, s, k, tt):
            if src is XA:
                return xa_d[k * 128:(k + 1) * 128, tt * TT:(tt + 1) * TT], [(XA, k * 4 + tt)]
            if src is XB:
                return xb_d[k * 128:(k + 1) * 128, tt * TT:(tt + 1) * TT], [(XB, k * 4 + tt)]
            return xT_d[s][k * 128:(k + 1) * 128, tt * TT:(tt + 1) * TT], []

        def norm_phase(src, s, l, which):
            A = der[:, which, :]
            shift_c0 = 0 if which == 0 else 24
            for tt in range(NTT):
                sl = slice(tt * TT, (tt + 1) * TT)
                half = tt % 2
                xt = _view(mT, 4 * half, 4, 128, [KC, TT], F32)
                xacc = [CS(mT, range(4 * half, 4 * half + 4))]
                if src is None:
                    P.dma(xt, kpn(xT_d[s][:, sl]), writes=xacc)
                else:
                    dsl = [(src, [k * 4 + tt for k in range(KC)])]
                    P.dma(xt, kpn(src.t[:, sl]), reads=dsl, writes=xacc)
                ps = ps_get()
                for k in range(KC):
                    sq = sq_get()
                    P.act(sq[:], xt[:, k, :], AF.Square, reads=xacc, writes=[sq])
                    P.mm(ps[:], onesB, sq[:], start=(k == 0), stop=(k == KC - 1), reads=[ctb, sq], writes=[ps])
                r = fl_get()
                P.act(r[:], ps[:], AF.Sqrt, reads=[ps], writes=[r], bias=EPS, scale=1.0 / D)
                P.op("dve", lambda e, r=r: e.reciprocal(r[:], r[:]), reads=[r], writes=[r])
                for k in range(KC):
                    tmp = fs_get()
                    P.stt(tmp[:], xt[:, k, :], A[:, k:k + 1], r[:], ALU.mult, ALU.mult,
                          reads=xacc + [der, r], writes=[tmp])
                    P.act(hT[:, k, sl], tmp[:], AF.Identity, reads=[tmp, modT], writes=[TS(hT, k, tt)],
                          bias=modT[:, l, shift_c0 + k, s:s + 1], scale=1.0)

        def proj_fm(wbuf, wcol0, m, tt, ps, parts=128):
            sl = slice(tt * TT, (tt + 1) * TT)
            for k in range(KC):
                P.mm(ps[0:m, :], wbuf[:, k, wcol0:wcol0 + m], hT[:, k, sl], start=(k == 0), stop=(k == KC - 1),
                     reads=[wbuf, TS(hT, k, tt)], writes=[ps])

        def layer(s, l, src, dst, dst_is_out):
            ppv = lambda c0, n: pp[:, c0:c0 + n]
            P.dma(pp[:], pp_d[l], writes=[pp])
            P.stt(der[:, 0, :], modT[:, l, 8:16, s], 1.0, plb[:, l, PL_GMIX:PL_GMIX + 8], ALU.add, ALU.mult,
                  reads=[modT, plb], writes=[der])
            P.stt(der[:, 1, :], modT[:, l, 32:40, s], 1.0, plb[:, l, PL_GFFN:PL_GFFN + 8], ALU.add, ALU.mult,
                  reads=[modT, plb], writes=[der])
            gate1 = lambda k: modT[:, l, 16 + k, s:s + 1]
            gate2 = lambda k: modT[:, l, 40 + k, s:s + 1]
            P.act(negA[:], ppv(PP_ALOG, 8), AF.Exp, reads=[pp], writes=[negA])
            P.ts("dve", negA[:], negA[:], -1.0, None, ALU.mult, reads=[negA], writes=[negA])
            P.copy("dve", peb[:], ppv(PP_PE, 32), reads=[pp], writes=[peb])

            norm_phase(None if src is None else src, s, l, 0)

            if stop == "dbgh":
                for i in range(2):
                    for cc in range(4):
                        stg32 = _view(oT[0], 0, 2, 128, [T], F32)
                        P.copy("dve", stg32, hT[:, 4 * i + cc, :], reads=[CS(hT, 4 * i + cc)], writes=[CS(oT[0], (0, 1))])
                        P.dma(dbg_d[i, :, cc, :], stg32, reads=[CS(oT[0], (0, 1))], final=True)
                for cc in range(4):
                    stg32 = _view(oT[0], 0, 2, 128, [T], F32)
                    P.op("dve", lambda e, stg32=stg32: e.memset(stg32, 0.0), writes=[CS(oT[0], (0, 1))])
                    P.copy("dve", stg32[:, 0:48], modT[:, l, :, s], reads=[modT], writes=[CS(oT[0], (0, 1))])
                    P.copy("dve", stg32[:, 48:64], der[:].rearrange("p a b -> p (a b)"), reads=[der], writes=[CS(oT[0], (0, 1))])
                    P.dma(dbg_d[2, :, cc, :], stg32, reads=[CS(oT[0], (0, 1))], final=True)
                return
            if stop == "n1":
                return
            qaug = _view(mT, 0, 2, 68, [8, TT])
            QA = [CS(mT, (0, 1))]
            kaug_s = _view(mT, 2, 2, 68, [2, T])
            KS = [CS(mT, (2, 3))]
            kaug_w = _view(mT, 4, 2, 68, [2, T])
            KW = [CS(mT, (4, 5))]
            MB = _view(mT, 6, 2, 32, [2, T])
            MBA = [CS(mT, (6, 7))]
            vslc = _view(oT[0], 0, 2, 128, [NB, 2, 66])
            VS = [CS(oT[0], (0, 1))]
            vwin = _view(oT[0], 2, 2, 128, [NB, 2, 66])
            VW = [CS(oT[0], (2, 3))]
            cmpk = _view(oT[1], 0, 2, 64, [2, T])
            CK = [CS(oT[1], (0, 1))]
            cmpv = _view(oT[1], 2, 2, 64, [2, T])
            CV = [CS(oT[1], (2, 3))]
            ocomb = _view(oT[2], 0, 2, 128, [4, 8, 64], F32)
            OC = [CS(oT[2], (0, 1))]
            cmpP = _view(oT[2], 2, 1, 128, [4, TT])
            CP = [CS(oT[2], 2)]
            PTs = _view(oT[2], 3, 1, 128, [4, TT])
            PTA = [CS(oT[2], 3)]

            for g in range(2):
                P.dma(kaug_s[64:68, g, :], lk_d, writes=KS, eng="pool")
                P.dma(kaug_w[64:68, g, :], lk_d, writes=KW, eng="pool")
            P.op("dve", lambda e: e.memset(vslc[:, :, :, 64:65], 1.0), writes=VS)
            P.op("dve", lambda e: e.memset(vwin[:, :, :, 64:65], 1.0), writes=VW)

            if stop == "tm0":
                return
            wsm = wb_get()
            W = w_in_d[l]
            wload(wsm[:, :, 0:128], kpn(W[:, C_VSLC:C_VSLC + 128]), wsm)
            wload(wsm[:, :, 128:256], kpn(W[:, C_VWIN:C_VWIN + 128]), wsm)
            wload(wsm[:, :, 256:384], kpn(W[:, C_GATES - 104:C_GATES + 24]), wsm)
            wload(wsm[:, :, 384:512], kpn(W[:, C_DT - 120:C_DT + 8]), wsm)
            if stop == "tm1":
                return
            for tb in range(NB):
                bs = slice(tb * 128, (tb + 1) * 128)
                ps = ps_get()
                for k in range(KC):
                    P.mm(ps[:, 0:512], hT[:, k, bs], wsm[:, k, 0:512], start=(k == 0), stop=(k == KC - 1),
                         reads=[TS(hT, k, tb // 4), wsm], writes=[ps])
                import os as _os
                _sk = _os.environ.get("K_SKIP", "")
                if "a" not in _sk:
                    P.copy("dve", vslc[:, tb, :, 0:64], ps[:, 0:128].rearrange("p (g d) -> p g d", g=2),
                           reads=[ps], writes=VS)
                if "b" not in _sk:
                    P.copy("dve", vwin[:, tb, :, 0:64], ps[:, 128:256].rearrange("p (g d) -> p g d", g=2),
                           reads=[ps], writes=VW)
                if "c" not in _sk:
                    P.copy("dve", gsig[:, tb, :], ps[:, 360:384], reads=[ps], writes=[gsig])
                if "d" not in _sk:
                    P.tt("dve", dtb[:, tb, :], ps[:, 504:512], ppv(PP_DTB, 8), ALU.add, reads=[ps, pp], writes=[dtb])
            if stop == "tm2":
                return
            P.act(gsig[:], gsig[:], AF.Sigmoid, reads=[gsig], writes=[gsig])
            P.act(dtb[:], dtb[:], AF.Exp, reads=[dtb], writes=[dtb])
            P.act(dtb[:], dtb[:], AF.Ln, reads=[dtb], writes=[dtb], bias=1.0, scale=1.0)

            if stop == "tmsmall":
                return
            wk = wb_get()
            wload(wk[:, :, 0:128], kpn(W[:, C_KCMP:C_KCMP + 128]), wk)
            wload(wk[:, :, 128:256], kpn(W[:, C_VCMP:C_VCMP + 128]), wk)
            wload(wk[:, :, 256:384], kpn(W[:, C_KSLC:C_KSLC + 128]), wk)
            wload(wk[:, :, 384:512], kpn(W[:, C_KWIN:C_KWIN + 128]), wk)

            def norm64(ps, gcol, out_ap, out_acc):
                sq = sq_get()
                P.act(sq[0:64, :], ps[0:64, :], AF.Square, reads=[ps], writes=[sq])
                ps2 = ps_get()
                P.mm(ps2[0:64, :], onesB[0:64, 0:64], sq[0:64, :], reads=[ctb, sq], writes=[ps2])
                r = fs_get()
                P.act(r[0:64, :], ps2[0:64, :], AF.Sqrt, reads=[ps2], writes=[r], bias=EPS, scale=1.0 / 64)
                P.op("dve", lambda e, r=r: e.reciprocal(r[0:64, :], r[0:64, :]), reads=[r], writes=[r])
                P.stt(out_ap, ps[0:64, :], pp[0:64, gcol:gcol + 1], r[0:64, :], ALU.mult, ALU.mult,
                      reads=[ps, pp, r], writes=out_acc)

            for which in range(4):
                for g in range(2):
                    for tt in range(NTT):
                        sl = slice(tt * TT, (tt + 1) * TT)
                        ps = ps_get()
                        proj_fm(wk, which * 128 + g * 64, 64, tt, ps)
                        if which == 0:
                            P.copy("act", cmpk[:, g, sl], ps[0:64, :], reads=[ps], writes=CK)
                        elif which == 1:
                            P.copy("act", cmpv[:, g, sl], ps[0:64, :], reads=[ps], writes=CV)
                        elif which == 2:
                            norm64(ps, PP_KG12, kaug_s[0:64, g, sl], KS)
                        else:
                            norm64(ps, PP_KG12 + 1, kaug_w[0:64, g, sl], KW)

            if stop == "kproj":
                return
            for kv in range(2):
                wcm = wb_get()
                wcf = wcm[:].rearrange("p k n -> p (k n)")
                w1b_v = wcf[0:64, 0:2048].rearrange("p (l e) -> p l e", l=32)
                w1f_v = wcf[:, 2048:3072].rearrange("p (c e) -> p c e", c=16)
                w2b_v = wcf[0:64, 3072:3136]
                w1b = w1f = w2b = wcm
                P.dma(w1b_v, w1_d[l, kv].rearrange("(l d) e -> d l e", d=64), writes=[wcm], eng="pool")
                P.dma(w1f_v, w1_d[l, kv].rearrange("(c p) e -> p c e", p=128), writes=[wcm], eng="pool")
                P.dma(w2b_v, w2_d[l, kv], writes=[wcm], eng="pool")
                psc = ps_get()
                for c in range(16):
                    P.mm(psc[0:64, 0:1], w1f_v[:, c, :], peb[:, kv * 16 + c:kv * 16 + c + 1], start=(c == 0), stop=(c == 15),
                         reads=[w1f, peb], writes=[psc])
                P.copy("dve", cbias[:, kv:kv + 1], psc[0:64, 0:1], reads=[psc], writes=[cbias])
                rawb, racc = (cmpk, CK) if kv == 0 else (cmpv, CV)
                for g in range(2):
                    ps = ps_get()
                    for li in range(32):
                        P.mm(ps[0:64, 0:127], w1b_v[:, li, :], rawb[:, g, li:li + 16 * 126 + 1:16], start=(li == 0), stop=(li == 31),
                             reads=[w1b] + racc, writes=[ps])
                    P.act(GTb[:, 0:127], ps[0:64, 0:127], AF.Gelu_apprx_tanh, reads=[ps, cbias], writes=[GTb],
                          bias=cbias[:, kv:kv + 1], scale=1.0)
                    ps2 = ps_get()
                    P.mm(ps2[0:127, 0:64], GTb[:, 0:127], w2b_v, reads=[GTb, w2b], writes=[ps2])
                    if kv == 0:
                        junk = fs_get()
                        P.act(junk[0:127, 0:64], ps2[0:127, 0:64], AF.Square, reads=[ps2], writes=[junk, smalls],
                              accum_out=smalls[0:127, 0:1])
                        P.act(smalls[0:127, 1:2], smalls[0:127, 0:1], AF.Sqrt, reads=[smalls], writes=[smalls],
                              bias=EPS, scale=1.0 / 64)
                        P.op("dve", lambda e: e.reciprocal(smalls[0:127, 2:3], smalls[0:127, 1:2]), reads=[smalls], writes=[smalls])
                        P.stt(kcn[0:127, :], ps2[0:127, 0:64], smalls[0:127, 2:3], pp[0:127, PP_KG0:PP_KG0 + 64],
                              ALU.mult, ALU.mult, reads=[ps2, smalls, pp], writes=[kcn])
                        pst = ps_get()
                        P.tr(psbf(pst)[0:64, 0:127], kcn[0:127, :], identB[0:127, 0:127], reads=[kcn, ctb], writes=[pst])
                        P.copy("dve", kcaug[0:64, g, 0:127], psbf(pst)[0:64, 0:127], reads=[pst], writes=[kcaug])
                    else:
                        P.copy("dve", VC[0:127, g, 0:64], ps2[0:127, 0:64], reads=[ps2], writes=[VC])

            if stop == "compress":
                return
            wq = wb_get()
            wload(wq[:], kpn(W[:, C_Q:C_Q + 512]), wq)
            gs4 = gsig[:].rearrange("p b (h i) -> p b h i", i=3)
            m1v = ctf[:, CF_M1:CF_M1 + 512].rearrange("p (b j) -> p b j", b=16)
            m2v = ctf[:, CF_M2:CF_M2 + 512].rearrange("p (b j) -> p b j", b=16)
            Ev = ctb[0:32, CB_E:CB_E + 2048].rearrange("p (c k) -> p c k", c=16)
            DMv = ctb[:, CB_DM:CB_DM + 2048].rearrange("p (i t) -> p i t", i=4)
            WMv = ctb[:, CB_WM:CB_WM + 384]
            CMv = ctb[:, CB_CM:CB_CM + T]
            sm = smalls

            for tt in range(NTT):
                sl = slice(tt * TT, (tt + 1) * TT)
                P.dma(qaug[64:68, :, :], rh_d[:, :, sl], writes=QA, eng="pool")
                for h in range(8):
                    ps = ps_get()
                    proj_fm(wq, h * 64, 64, tt, ps)
                    norm64(ps, PP_QG, qaug[0:64, h, :], QA)
                nmax = min(127, 32 * tt + 31)
                for g in range(2):
                    for r in range(4):
                        h = 4 * g + r
                        pss = ps_get()
                        P.mm(pss[0:nmax, :], kcaug[0:68, g, 0:nmax], qaug[0:68, h, :], start=True, stop=False,
                             reads=[kcaug] + QA, writes=[pss])
                        P.mm(pss[0:nmax, :], identB[0:nmax, 0:nmax], CMv[0:nmax, sl], start=False, stop=True,
                             reads=[ctb], writes=[pss])
                        P.act(cmpP[0:nmax, r, :], pss[0:nmax, :], AF.Exp, reads=[pss], writes=[(oT[2], 8 + r)], scale=0.125)
                    for qb in range(4):
                        tb = 4 * tt + qb
                        qs = slice(qb * 128, (qb + 1) * 128)
                        pso = ps_get()
                        pso_v = pso[:, 0:388].rearrange("p (r c) -> p r c", r=4)
                        for r in range(4):
                            P.mm(pso_v[:, r, :], cmpP[0:nmax, r, qs], VC[0:nmax, g, :], reads=CP + [VC], writes=[pso])
                        P.ts("dve", sm[:, 0:4], pso_v[:, :, 96], 1e-30, None, ALU.max, reads=[pso], writes=[sm])
                        P.op("dve", lambda e: e.reciprocal(sm[:, 4:8], sm[:, 0:4]), reads=[sm], writes=[sm])
                        tmp = fs_get()
                        tv = tmp[:, 0:128].rearrange("p (r j) -> p r j", r=4)
                        P.tt("dve", tv, pso_v[:, :, 64:96], sm[:, 4:8].unsqueeze(2).broadcast_to([128, 4, 32]), ALU.mult,
                             reads=[pso, sm], writes=[tmp])
                        P.op("dve", lambda e, tv=tv: e.tensor_reduce(sm[:, 8:40], tv.rearrange("p r j -> p j r"), AX.X, ALU.add),
                             reads=[tmp], writes=[sm])
                        P.tt("dve", sm[:, 40:44], sm[:, 4:8], gs4[:, tb, 4 * g:4 * g + 4, 0], ALU.mult, reads=[sm, gsig], writes=[sm])
                        P.tt("dve", ocomb[:, qb, 4 * g:4 * g + 4, :], pso_v[:, :, 0:64],
                             sm[:, 40:44].unsqueeze(2).broadcast_to([128, 4, 64]), ALU.mult, reads=[pso, sm], writes=OC)
                        P.tt("dve", sm[:, 44:76], sm[:, 8:40], m1v[:, tb, :], ALU.mult, reads=[sm, ctf], writes=[sm])
                        P.tt("dve", sm[:, 44:76], sm[:, 44:76], m2v[:, tb, :], ALU.add, reads=[sm, ctf], writes=[sm])
                        P.op("dve", lambda e: e.max(sm[:, 76:84], sm[:, 44:76]), reads=[sm], writes=[sm])
                        nsel = sq_get()
                        P.ts("dve", nsel[:, 0:32], sm[:, 44:76], sm[:, 83:84], None, ALU.is_lt, reads=[sm], writes=[nsel])
                        pst = ps_get()
                        P.tr(psbf(pst)[0:32, 0:128], nsel[:, 0:32], identB, reads=[nsel, ctb], writes=[pst])
                        P.ts("dve", MB[:, g, tb * 128:(tb + 1) * 128], psbf(pst)[0:32, 0:128], NEG, None, ALU.mult,
                             reads=[pst], writes=MBA)
                for h in range(8):
                    g = h // 4
                    for branch in (1, 2):
                        acc = ps_get(hold=True)
                        acc_v = acc[:, 0:260].rearrange("p (q c) -> p q c", q=4)
                        P.mm(acc[:, 0:260], zero_b[0:1, 0:128], zero_b[0:1, 0:260], start=True, stop=False,
                             reads=[zero_b], writes=[acc], skip_group_check=True)
                        if branch == 1:
                            chunks = range(0, 4 * tt + 4)
                        else:
                            chunks = range(max(0, 4 * tt - 2), 4 * tt + 4)
                        for ci, c in enumerate(chunks):
                            ks = slice(c * 128, (c + 1) * 128)
                            i = c - 4 * tt
                            pss = ps_get()
                            pt = PTs[:, ci % 4, :]
                            PTA = [(oT[2], 12 + ci % 4)]
                            if branch == 1:
                                P.mm(pss[:, :], kaug_s[0:68, g, ks], qaug[0:68, h, :], start=True, stop=False,
                                     reads=KS + QA, writes=[pss])
                                P.mm(pss[:, :], Ev[:, c, :], MB[:, g, sl], start=False, stop=(i < 0),
                                     reads=[ctb] + MBA, writes=[pss])
                                if i >= 0:
                                    P.mm(pss[:, :], identB, DMv[:, i, :], start=False, stop=True, reads=[ctb], writes=[pss])
                                P.act(pt, pss[:, :], AF.Exp, reads=[pss], writes=PTA, scale=0.125)
                                for qb in range(max(i, 0), 4):
                                    P.mm(acc_v[:, qb, :], pt[:, qb * 128:(qb + 1) * 128], vslc[:, c, g, 0:65], start=False, stop=False,
                                         reads=PTA + VS, writes=[acc], skip_group_check=True)
                            else:
                                qlo = max(i, 0)
                                qhi = min(i + 2, 3)
                                nq = qhi - qlo + 1
                                rel_lo = qlo - i
                                N = nq * 128
                                P.mm(pss[:, 0:N], kaug_w[0:68, g, ks], qaug[0:68, h, qlo * 128:(qhi + 1) * 128], start=True, stop=False,
                                     reads=KW + QA, writes=[pss])
                                P.mm(pss[:, 0:N], identB, WMv[:, rel_lo * 128:(rel_lo + nq) * 128], start=False, stop=True,
                                     reads=[ctb], writes=[pss])
                                P.act(pt[:, 0:N], pss[:, 0:N], AF.Exp, reads=[pss], writes=PTA, scale=0.125)
                                for qi in range(nq):
                                    qb = qlo + qi
                                    P.mm(acc_v[:, qb, :], pt[:, qi * 128:(qi + 1) * 128], vwin[:, c, g, 0:65], start=False, stop=False,
                                         reads=PTA + VW, writes=[acc], skip_group_check=True)
                        P.ts("dve", sm[:, 100:104], acc_v[:, :, 64], 1e-30, None, ALU.max, reads=[acc], writes=[sm])
                        P.op("dve", lambda e: e.reciprocal(sm[:, 104:108], sm[:, 100:104]), reads=[sm], writes=[sm])
                        P.tt("dve", sm[:, 108:112], sm[:, 104:108], gs4[:, 4 * tt:4 * tt + 4, h, branch], ALU.mult,
                             reads=[sm, gsig], writes=[sm])
                        tmp = fs_get()
                        tv = tmp[:, 0:256].rearrange("p (q d) -> p q d", q=4)
                        P.tt("dve", tv, acc_v[:, :, 0:64], sm[:, 108:112].unsqueeze(2).broadcast_to([128, 4, 64]), ALU.mult,
                             reads=[acc, sm], writes=[tmp])
                        P.tt("dve", ocomb[:, :, h, :], ocomb[:, :, h, :], tv, ALU.add, reads=OC + [tmp], writes=OC)
                        ps_release(acc)
                for qb in range(4):
                    tb = 4 * tt + qb
                    pst = ps_get()
                    oc2 = ocomb[:, qb, :, :].rearrange("p h d -> p (h d)")
                    for cc in range(4):
                        P.tr(pst[:, cc * 128:(cc + 1) * 128], oc2[:, cc * 128:(cc + 1) * 128], identF, reads=OC + [ctf], writes=[pst])
                    P.copy("act", oT[3][:, :, tb * 128:(tb + 1) * 128], pst[:, :].rearrange("p (c t) -> p c t", c=4),
                           reads=[pst], writes=[TSA(oT[3], 4, tt)])

            if stop == "nsa":
                return
            wz = wb_get()
            wload(wz[:], kpn(W[:, C_Z:C_Z + 512]), wz)
            wx = [wb_get(), wb_get()]
            wload(wx[0][:], kpn(W[:, C_XBC:C_XBC + 512]), wx[0])
            wload(wx[1][:], kpn(W[:, C_XBC + 512:C_XBC + 1024]), wx[1])
            xbcT = _view(mT, 0, 2, 128, [8, TT])
            XBC = [CS(mT, (0, 1))]
            dtab = _view(mT, 2, 2, 128, [8, 128], F32)
            DTA = [CS(mT, (2, 3))]
            LT = _view(mT, 4, 2, 128, [8, 128], F32)
            LTA = [CS(mT, (4, 5))]
            Mb = _view(mT, 6, 1, 128, [8, 128])
            MA = [CS(mT, 6)]
            s7 = _view(mT, 7, 1, 128, [2048])
            S7 = [CS(mT, 7)]
            xdt = s7[:, 0:512].rearrange("p (h d) -> p h d", h=8)
            xdd = s7[:, 512:1024].rearrange("p (h d) -> p h d", h=8)
            xs_sb = s7[:, 1024:1536].rearrange("p (h d) -> p h d", h=8)
            Btm = s7[:, 1536:1792]
            P.op("dve", lambda e: e.memset(state[:], 0.0), writes=[state])
            P.op("dve", lambda e: e.memset(stateb[:], 0.0), writes=[stateb])
            P.op("dve", lambda e: e.memset(tails[:], 0.0), writes=[tails])
            convw = pp[:, PP_CONVW:PP_CONVW + 32].rearrange("p (c k) -> p c k", c=8)
            dsk = pp[:, PP_DSKIP:PP_DSKIP + 8]
            for tt in range(NTT):
                for ch in range(8):
                    ps = ps_get()
                    proj_fm(wx[ch // 4], (ch % 4) * 128, 128, tt, ps)
                    rw = raw[rr["raw"] % 2]
                    rr["raw"] += 1
                    P.copy("dve", rw[:, 0:3], tails[:, ch, :], reads=[tails], writes=[rw])
                    P.copy("act", rw[:, 3:515], ps[:, :], reads=[ps], writes=[rw])
                    P.copy("dve", tails[:, ch, :], rw[:, 512:515], reads=[rw], writes=[tails])
                    acc = fs_get()
                    P.ts("dve", acc[:], rw[:, 0:512], convw[:, ch, 0:1], None, ALU.mult, reads=[rw, pp], writes=[acc])
                    for kk in range(1, 4):
                        P.stt(acc[:], rw[:, kk:kk + 512], convw[:, ch, kk:kk + 1], acc[:], ALU.mult, ALU.add,
                              reads=[rw, pp, acc], writes=[acc])
                    P.act(xbcT[:, ch, :], acc[:], AF.Silu, reads=[acc, pp], writes=XBC,
                          bias=pp[:, PP_CONVB + ch:PP_CONVB + ch + 1], scale=1.0)
                for lt in range(4):
                    tb = 4 * tt + lt
                    cs = slice(lt * 128, (lt + 1) * 128)
                    bs = slice(tb * 128, (tb + 1) * 128)
                    psx = ps_get()
                    pxb = psbf(psx)
                    for ch in range(4):
                        P.tr(pxb[:, ch * 128:(ch + 1) * 128], xbcT[:, ch, cs], identB, reads=XBC + [ctb], writes=[psx])
                    psB = ps_get()
                    pBb = psbf(psB)
                    for g in range(2):
                        P.tr(pBb[:, g * 128:(g + 1) * 128], xbcT[:, 4 + g, cs], identB, reads=XBC + [ctb], writes=[psB])
                    P.tt("dve", sm[:, 120:128], dtb[:, tb, :], negA[:], ALU.mult, reads=[dtb, negA], writes=[sm])
                    P.copy("dve", dtab, sm[:, 120:128].unsqueeze(2).broadcast_to([128, 8, 128]), reads=[sm], writes=DTA)
                    psa = ps_get()
                    P.mm(psa[:, 0:8], triU, sm[:, 120:128], reads=[ctf, sm], writes=[psa])
                    P.mm(psa[:, 8:16], onesF, sm[:, 120:128], reads=[ctf, sm], writes=[psa])
                    P.copy("dve", sm[:, 128:136], psa[:, 0:8], reads=[psa], writes=[sm])
                    P.tt("dve", sm[:, 136:144], psa[:, 8:16], sm[:, 128:136], ALU.subtract, reads=[psa, sm], writes=[sm])
                    P.copy("dve", sm[:, 144:152], psa[:, 8:16], reads=[psa], writes=[sm])
                    P.ts("dve", sm[:, 176:184], sm[:, 128:136], -1.0, None, ALU.mult, reads=[sm], writes=[sm])
                    P.act(sm[:, 152:176], sm[:, 128:152], AF.Exp, reads=[sm], writes=[sm])
                    ea = sm[:, 152:160]
                    dec = sm[:, 160:168]
                    cdec = sm[:, 168:176]
                    psl = [ps_get(), ps_get()]
                    for h in range(8):
                        pl_ = psl[h // 4]
                        o_ = pl_[:, (h % 4) * 128:(h % 4 + 1) * 128]
                        P.mm(o_, dtab[:, h, :], triU, start=True, stop=False, reads=DTA + [ctf], writes=[pl_])
                        P.mm(o_, identB, NEGM, start=False, stop=True, reads=[ctb], writes=[pl_])
                    for h in range(8):
                        pl_ = psl[h // 4]
                        o_ = pl_[:, (h % 4) * 128:(h % 4 + 1) * 128]
                        P.act(LT[:, h, :], o_, AF.Exp, reads=[pl_, sm], writes=LTA, bias=sm[:, 176 + h:177 + h], scale=1.0)
                    psc_ = ps_get()
                    for g in range(2):
                        P.mm(psc_[:, g * 128:(g + 1) * 128], xbcT[:, 4 + g, cs], xbcT[:, 6 + g, cs], reads=XBC, writes=[psc_])
                    cbT = fs_get()
                    P.copy("act", cbT[:, 0:256], psc_[:, 0:256], reads=[psc_], writes=[cbT])
                    P.tt("dve", Mb.rearrange("p (g r) l -> p g r l", g=2),
                         LT.rearrange("p (g r) l -> p g r l", g=2),
                         cbT[:, 0:256].rearrange("p (g l) -> p g l", g=2).unsqueeze(2).broadcast_to([128, 2, 4, 128]),
                         ALU.mult, reads=LTA + [cbT], writes=MA)
                    pxv = pxb[:, 0:512].rearrange("p (h d) -> p h d", h=8)
                    P.tt("dve", xdt, pxv, dtb[:, tb, :].unsqueeze(2).broadcast_to([128, 8, 64]), ALU.mult,
                         reads=[psx, dtb], writes=S7)
                    P.copy("act", Btm, pBb[:, 0:256], reads=[psB], writes=S7)
                    P.tt("dve", xdd, xdt, dec.unsqueeze(2).broadcast_to([128, 8, 64]), ALU.mult, reads=S7 + [sm], writes=S7)
                    psy = ps_get()
                    for h in range(8):
                        P.mm(psy[:, h * 64:(h + 1) * 64], Mb[:, h, :], xdt[:, h, :], reads=MA + S7, writes=[psy])
                    pso = ps_get()
                    for g in range(2):
                        P.mm(pso[:, g * 256:(g + 1) * 256], xbcT[:, 6 + g, cs], stateb[:, g * 256:(g + 1) * 256],
                             reads=XBC + [stateb], writes=[pso])
                    y1 = fs_get()
                    y1v = y1[:].rearrange("p (h d) -> p h d", h=8)
                    P.tt("dve", y1v, pso[:, :].rearrange("p (h d) -> p h d", h=8),
                         ea.unsqueeze(2).broadcast_to([128, 8, 64]), ALU.mult, reads=[pso, sm], writes=[y1])
                    P.tt("dve", y1[:], y1[:], psy[:, :], ALU.add, reads=[y1, psy], writes=[y1])
                    y2 = fs_get()
                    P.tt("dve", y2[:].rearrange("p (h d) -> p h d", h=8), pxv,
                         dsk.unsqueeze(2).broadcast_to([128, 8, 64]), ALU.mult, reads=[psx, pp], writes=[y2])
                    P.tt("dve", y1[:], y1[:], y2[:], ALU.add, reads=[y1, y2], writes=[y1])
                    pst_ = ps_get()
                    for g in range(2):
                        P.mm(pst_[:, g * 256:(g + 1) * 256], Btm[:, g * 128:(g + 1) * 128],
                             xdd[:, 4 * g:4 * g + 4, :].rearrange("p h d -> p (h d)"), reads=S7, writes=[pst_])
                    stv = state[:].rearrange("p (h d) -> p h d", h=8)
                    P.tt("dve", stv, stv, cdec.unsqueeze(2).broadcast_to([128, 8, 64]), ALU.mult, reads=[state, sm], writes=[state])
                    P.tt("dve", state[:], state[:], pst_[:, :], ALU.add, reads=[state, pst_], writes=[state])
                    P.copy("act", stateb[:], state[:], reads=[state], writes=[stateb])
                    psz = ps_get()
                    for k in range(KC):
                        P.mm(psz[:, :], hT[:, k, bs], wz[:, k, :], start=(k == 0), stop=(k == KC - 1), reads=[TS(hT, k, tt), wz], writes=[psz])
                    zs = fs_get()
                    P.act(zs[:], psz[:, :], AF.Silu, reads=[psz], writes=[zs])
                    P.tt("dve", y1[:], y1[:], zs[:], ALU.mult, reads=[y1, zs], writes=[y1])
                    for g in range(2):
                        P.act(zs[:, g * 256:(g + 1) * 256], y1[:, g * 256:(g + 1) * 256], AF.Square, reads=[y1], writes=[zs, sm],
                              accum_out=sm[:, 184 + g:185 + g])
                    P.act(sm[:, 186:188], sm[:, 184:186], AF.Sqrt, reads=[sm], writes=[sm], bias=EPS, scale=1.0 / 256)
                    P.op("dve", lambda e: e.reciprocal(sm[:, 188:190], sm[:, 186:188]), reads=[sm], writes=[sm])
                    oa = sq_get()
                    for g in range(2):
                        P.stt(oa[:, g * 256:(g + 1) * 256], y1[:, g * 256:(g + 1) * 256], sm[:, 188 + g:189 + g],
                              pp[:, PP_SSDNG + g * 256:PP_SSDNG + (g + 1) * 256], ALU.mult, ALU.mult,
                              reads=[y1, sm, pp], writes=[oa])
                    pso2 = ps_get()
                    po2 = psbf(pso2)
                    for ch in range(4):
                        P.tr(po2[:, ch * 128:(ch + 1) * 128], oa[:, ch * 128:(ch + 1) * 128], identB, reads=[oa, ctb], writes=[pso2])
                    P.copy("act", oT[0][:, :, bs], po2[:, 0:512].rearrange("p (c t) -> p c t", c=4), reads=[pso2], writes=[TSA(oT[0], 4, tt)])

            if stop == "ssd":
                return
            wB = [wb_get(), wb_get(), wb_get()]
            for i_, c0 in enumerate((C_B, C_C, C_HX)):
                wload(wB[i_][:], kpn(W[:, c0:c0 + 512]), wB[i_])
            chx = _view(mT, 0, 3, 128, [2050], F32)
            CHX = [CS(mT, (0, 1, 2))]
            accB = _view(mT, 3, 2, 128, [2048], F32)
            ACB = [CS(mT, (3, 4))]
            bsb = _view(mT, 5, 1, 128, [2048])
            BSB = [CS(mT, 5)]
            P.op("dve", lambda e: e.memset(chx[:, 0:2], 0.0), writes=CHX)
            scw = pp[:, PP_SCW:PP_SCW + 12].rearrange("p (c k) -> p c k", c=4)
            for cc in range(4):
                for tt in range(NTT):
                    sl = slice(tt * TT, (tt + 1) * TT)
                    psb_ = ps_get()
                    proj_fm(wB[0], cc * 128, 128, tt, psb_)
                    psc_ = ps_get()
                    proj_fm(wB[1], cc * 128, 128, tt, psc_)
                    psh = ps_get()
                    proj_fm(wB[2], cc * 128, 128, tt, psh)
                    P.copy("act", bsb[:, sl], psb_[:, :], reads=[psb_], writes=BSB)
                    hx = fs_get()
                    P.copy("act", hx[:], psh[:, :], reads=[psh], writes=[hx])
                    P.tt("dve", chx[:, 2 + tt * TT:2 + (tt + 1) * TT], psc_[:, :], hx[:], ALU.mult, reads=[psc_, hx], writes=CHX)
                P.ts("dve", accB, chx[:, 0:2048], scw[:, cc, 0:1], None, ALU.mult, reads=CHX + [pp], writes=ACB)
                P.stt(accB, chx[:, 1:2049], scw[:, cc, 1:2], accB, ALU.mult, ALU.add, reads=CHX + [pp] + ACB, writes=ACB)
                P.stt(accB, chx[:, 2:2050], scw[:, cc, 2:3], accB, ALU.mult, ALU.add, reads=CHX + [pp] + ACB, writes=ACB)
                P.tt("dve", oT[1][:, cc, :], accB, bsb, ALU.mult, reads=ACB + BSB, writes=[CS(oT[1], cc)])

            if stop == "sconv":
                return
            wC = [wb_get(), wb_get()]
            wload(wC[0][:], kpn(W[:, C_GU:C_GU + 512]), wC[0])
            wload(wC[1][:], kpn(W[:, C_GV:C_GV + 512]), wC[1])
            sgw32 = fs_get()
            sgv = sgw32[:].rearrange("p (g t) -> p g t", g=4)
            P.dma(sgv, sgw_d[l], writes=[sgw32])
            P.tt("dve", sgwb[:], sgv, triU.unsqueeze(1).broadcast_to([128, 4, 128]), ALU.mult,
                 reads=[sgw32, ctf], writes=[sgwb])
            for cc in range(4):
                for tt in range(NTT):
                    sl = slice(tt * TT, (tt + 1) * TT)
                    ps = ps_get()
                    proj_fm(wC[0], cc * 128, 128, tt, ps)
                    P.act(oT[2][:, cc, sl], ps[:, :], AF.Gelu_apprx_tanh, reads=[ps], writes=[TS(oT[2], cc, tt)])
            sgb = pp[:, PP_SGB:PP_SGB + 512]
            for tb in range(NB):
                bs = slice(tb * 128, (tb + 1) * 128)
                ps = ps_get()
                for k in range(KC):
                    P.mm(ps[:, :], hT[:, k, bs], wC[1][:, k, :], start=(k == 0), stop=(k == KC - 1), reads=[TS(hT, k, tb // 4), wC[1]], writes=[ps])
                vg = fs_get()
                P.act(vg[:], ps[:, :], AF.Gelu_apprx_tanh, reads=[ps], writes=[vg])
                junk = fs_get()
                P.act(junk[:], vg[:], AF.Square, reads=[vg], writes=[junk, sm], accum_out=sm[:, 192:193])
                P.act(sm[:, 193:194], sm[:, 192:193], AF.Sqrt, reads=[sm], writes=[sm], bias=EPS, scale=1.0 / 512)
                P.op("dve", lambda e: e.reciprocal(sm[:, 194:195], sm[:, 193:194]), reads=[sm], writes=[sm])
                vn = sq_get()
                P.stt(vn[:], vg[:], sm[:, 194:195], pp[:, PP_SGNG:PP_SGNG + 512], ALU.mult, ALU.mult, reads=[vg, sm, pp], writes=[vn])
                ps2 = ps_get()
                for g in range(4):
                    P.mm(ps2[:, g * 128:(g + 1) * 128], vn[:, g * 128:(g + 1) * 128], sgwb[:, g, :], reads=[vn, sgwb], writes=[ps2])
                t1 = fs_get()
                P.tt("dve", t1[:], ps2[:, :], sgb, ALU.add, reads=[ps2, pp], writes=[t1])
                P.tt("dve", oT[2][:, :, bs], t1[:].rearrange("p (g t) -> p g t", g=4), oT[2][:, :, bs], ALU.mult,
                     reads=[t1, TSA(oT[2], 4, tb // 4)], writes=[TSA(oT[2], 4, tb // 4)])

            if debug and l == 0 and s == 0:
                for i in range(4):
                    for cc in range(4):
                        stg32 = _view(mT, 0, 2, 128, [T], F32)
                        P.copy("dve", stg32, oT[i][:, cc, :], reads=[CS(oT[i], cc)], writes=[CS(mT, (0, 1))])
                        P.dma(dbg_d[i, :, cc, :], stg32, reads=[CS(mT, (0, 1))], final=True)

            if stop == "sgate":
                return
            for j in range(KC):
                js = slice(j * 128, (j + 1) * 128)
                wg = wb_get()
                for i in range(4):
                    wload(wg[:, :, i * 128:(i + 1) * 128], kpn(wg_d[l, i][:, js]), wg)
                wbr = wb_get()
                wbv = wbr[:, 0:4, :].rearrange("p k (i n) -> p i k n", i=4)
                for i in range(4):
                    wload(wbv[:, i, :, :], kpn(wb_d[l, i][:, js]), wbr)
                for tt in range(NTT):
                    sl = slice(tt * TT, (tt + 1) * TT)
                    macc = fl_get()
                    for i in range(4):
                        psg = ps_get()
                        proj_fm(wg, i * 128, 128, tt, psg)
                        psp = ps_get()
                        for kk in range(4):
                            P.mm(psp[:, :], wbv[:, i, kk, :], oT[i][:, kk, sl], start=(kk == 0), stop=(kk == 3),
                                 reads=[wbr, TS(oT[i], kk, tt)], writes=[psp])
                        sg = fs_get()
                        P.act(sg[:], psg[:, :], AF.Sigmoid, reads=[psg], writes=[sg])
                        if i == 0:
                            P.tt("dve", macc[:], sg[:], psp[:, :], ALU.mult, reads=[sg, psp], writes=[macc])
                        else:
                            P.tt("dve", sg[:], sg[:], psp[:, :], ALU.mult, reads=[sg, psp], writes=[sg])
                            if i < 3:
                                P.tt("dve", macc[:], macc[:], sg[:], ALU.add, reads=[macc, sg], writes=[macc])
                            else:
                                P.tt("dve", mT[:, j, sl], macc[:], sg[:], ALU.add, reads=[macc, sg], writes=[TS(mT, j, tt)])

            if stop == "merge":
                return
            for jj in range(2):
                wo = wb_get()
                wload(wo[:], kpn(wo_d[l][:, jj * 512:(jj + 1) * 512]), wo)
                for tt in range(NTT):
                    sl = slice(tt * TT, (tt + 1) * TT)
                    for j4 in range(4):
                        chn = jj * 4 + j4
                        ps = ps_get()
                        for k in range(KC):
                            P.mm(ps[:, :], wo[:, k, j4 * 128:(j4 + 1) * 128], mT[:, k, sl], start=(k == 0), stop=(k == KC - 1),
                                 reads=[wo, TS(mT, k, tt)], writes=[ps])
                        xin = fs_get()
                        sap, sacc = xsrc_ap(src, s, chn, tt)
                        P.dma(xin[:], sap, reads=sacc, writes=[xin])
                        P.stt(xin[:], ps[:, :], gate1(chn), xin[:], ALU.mult, ALU.add, reads=[ps, modT, xin], writes=[xin])
                        P.dma(xb_d[chn * 128:(chn + 1) * 128, sl], xin[:], reads=[xin], writes=[(XB, chn * 4 + tt)])

            if stop == "wout":
                return
            norm_phase(XB, s, l, 1)

            def hid(hc, tt):
                if hc < 16:
                    return oT[hc // 4][:, hc % 4, :], [TS(oT[hc // 4], hc % 4, tt)]
                return mT[:, hc - 16, :], [TS(mT, hc - 16, tt)]

            for hp in range(NHC // 2):
                wf = wb_get()
                wload(wf[:, :, 0:256], kpn(wfi_d[l][:, hp * 256:(hp + 1) * 256]), wf)
                wload(wf[:, :, 256:512], kpn(wfi_d[l][:, D_FF + hp * 256:D_FF + (hp + 1) * 256]), wf)
                for hh in range(2):
                    hc = hp * 2 + hh
                    for tt in range(NTT):
                        hap, hacc = hid(hc, tt)
                        sl = slice(tt * TT, (tt + 1) * TT)
                        psa_ = ps_get()
                        proj_fm(wf, hh * 128, 128, tt, psa_)
                        psb2 = ps_get()
                        proj_fm(wf, 256 + hh * 128, 128, tt, psb2)
                        sa = fs_get()
                        P.act(sa[:], psa_[:, :], AF.Silu, reads=[psa_], writes=[sa])
                        P.tt("dve", hap[:, sl], sa[:], psb2[:, :], ALU.mult, reads=[sa, psb2], writes=hacc)
            for j in range(KC):
                js = slice(j * 128, (j + 1) * 128)
                wf2 = wb_get()
                w2v = wf2[:].rearrange("p k n -> p (k n)")[:, 0:NHC * 128].rearrange("p (c n) -> p c n", c=NHC)
                wload(w2v, wfo_d[l][:, js].rearrange("(c p) n -> p c n", p=128), wf2)
                for tt in range(NTT):
                    sl = slice(tt * TT, (tt + 1) * TT)
                    ps = ps_get()
                    for hc in range(NHC):
                        hap, hacc = hid(hc, tt)
                        P.mm(ps[:, :], w2v[:, hc, :], hap[:, sl], start=(hc == 0), stop=(hc == NHC - 1), reads=[wf2] + hacc, writes=[ps])
                    xin = fs_get()
                    P.dma(xin[:], xb_d[js, sl], reads=[(XB, j * 4 + tt)], writes=[xin])
                    P.stt(xin[:], ps[:, :], gate2(j), xin[:], ALU.mult, ALU.add, reads=[ps, modT, xin], writes=[xin])
                    if dst_is_out:
                        P.dma(outT_d[s][js, sl], xin[:], reads=[xin], writes=[(OUT, j * 4 + tt)], final=True)
                    else:
                        P.dma(xa_d[js, sl], xin[:], reads=[xin], writes=[(XA, j * 4 + tt)])

        for s in range(n_seq):
            for l in range(n_layers):
                layer(s, l, None if l == 0 else XA, XA, l == n_layers - 1)

        stats = P.finalize()
        P.emit()
    return nc, stats


_CACHE = {}


def _host_inputs(inp, n_seq=SEQ_PER_CORE):
    ctf, ctb, lk, lkc, rh = _const_tables()
    pp, pl = _pack_params(inp)
    shared = dict(
        pp=pp, pl=pl, ctf=ctf, ctb=ctb, lk=lk, lkc=lkc, rh=rh,
        sgwT=np.ascontiguousarray(np.asarray(inp["sg_w"], np.float32).transpose(0, 3, 1, 2)),
        ada_w=np.asarray(inp["ada_w"], np.float32), w_in=np.asarray(inp["w_in"], np.float32),
        cmp_w1=np.asarray(inp["nsa_cmp_w1"], np.float32), cmp_w2=np.asarray(inp["nsa_cmp_w2"], np.float32),
        w_branch=np.asarray(inp["w_branch"], np.float32), w_branch_gate=np.asarray(inp["w_branch_gate"], np.float32),
        w_out=np.asarray(inp["w_out"], np.float32), w_ffn_in=np.asarray(inp["w_ffn_in"], np.float32),
        w_ffn_out=np.asarray(inp["w_ffn_out"], np.float32),
    )
    x = np.asarray(inp["x"], np.float32)
    c = np.asarray(inp["c"], np.float32)
    maps = []
    for core in range(NCORES):
        b0 = core * SEQ_PER_CORE
        xs = x[b0:b0 + n_seq]
        cs = c[b0:b0 + n_seq]
        m = dict(shared)
        m["xT"] = np.ascontiguousarray(xs.transpose(0, 2, 1))
        m["cT"] = np.ascontiguousarray(cs.reshape(n_seq, KC, 128).transpose(2, 1, 0))
        maps.append(m)
    return maps


def kernel(**inputs):
    if "nc" not in _CACHE:
        _CACHE["nc"], _CACHE["stats"] = build_program()
    nc = _CACHE["nc"]
    maps = _host_inputs(inputs)
    res = run_bass_kernel_spmd(nc, maps, core_ids=list(range(NCORES)))
    out = np.empty((NCORES * SEQ_PER_CORE, T, D), np.float32)
    for core in range(NCORES):
        o = np.asarray(res.results[core]["outT"])
        out[core * SEQ_PER_CORE:(core + 1) * SEQ_PER_CORE] = o.transpose(0, 2, 1)
    return out
```

```python
import numpy as np
import concourse.bass as bass
import concourse.mybir as mybir
from concourse.bass_utils import run_bass_kernel_spmd
from contextlib import ExitStack

F32 = mybir.dt.float32
BF16 = mybir.dt.bfloat16
AF = mybir.ActivationFunctionType
ALU = mybir.AluOpType
AX = mybir.AxisListType

ENGS = ("pe", "act", "dve", "pool", "sp")
N_DMA_SEMS = 16

D = 1024
T = 2048
DEPTH = 4
NCORES = 8
SEQ_PER_CORE = 2
KC = 8
TT = 512
NTT = 4
NB = 16
D_IN = 5408
D_FF = 2816
NHC = 22
EPS = 1e-6
NEG = -30000.0
C_Z, C_XBC, C_DT = 0, 512, 1536
C_B, C_C, C_HX = 1544, 2056, 2568
C_GU, C_GV = 3080, 3592
C_Q = 4104
C_KCMP, C_VCMP, C_KSLC, C_VSLC, C_KWIN, C_VWIN = 4616, 4744, 4872, 5000, 5128, 5256
C_GATES = 5384


class Buf:
    __slots__ = ("t", "name", "nslots", "lw", "rd")

    def __init__(self, t, name, nslots=1):
        self.t = t
        self.name = name
        self.nslots = nslots
        self.lw = [None] * nslots
        self.rd = [[] for _ in range(nslots)]

    def __getitem__(self, idx):
        return self.t[idx]


class Op:
    __slots__ = ("eng", "fn", "deps", "is_dma", "sig", "sig_idx", "dsem", "dval", "idx", "waits", "prewait")

    def __init__(self, eng, fn, is_dma):
        self.eng = eng
        self.fn = fn
        self.is_dma = is_dma
        self.deps = set()
        self.sig = False
        self.sig_idx = 0
        self.dsem = None
        self.dval = 0
        self.waits = []
        self.prewait = None


class Prog:
    def __init__(self, nc, stack):
        self.nc = nc
        self.stack = stack
        self.ops = []
        self.final_dma = []
        self.nbuf = 0

    def sbuf(self, shape, dtype, name=None, nslots=1):
        self.nbuf += 1
        name = f"sb{self.nbuf}_{name or ""}"
        t = self.stack.enter_context(self.nc.sbuf_tensor(name, list(shape), dtype))
        return Buf(t, name, nslots)

    def psum(self, shape, dtype=F32, name=None, nslots=1):
        self.nbuf += 1
        name = f"ps{self.nbuf}_{name or ""}"
        t = self.stack.enter_context(self.nc.psum_tensor(name, list(shape), dtype))
        return Buf(t, name, nslots)

    @staticmethod
    def _norm(acc):
        out = []
        for a in acc:
            if a is None:
                continue
            if isinstance(a, Buf):
                out.append((a, range(a.nslots)))
            else:
                b, s = a
                if isinstance(s, int):
                    s = (s,)
                out.append((b, s))
        return out

    def op(self, eng, fn, reads=(), writes=(), dma=False, final=False):
        o = Op(eng, fn, dma)
        o.idx = len(self.ops)
        rl = self._norm(reads)
        wl = self._norm(writes)
        for b, slots in rl:
            for s in slots:
                w = b.lw[s]
                if w is not None:
                    o.deps.add(w)
        for b, slots in wl:
            for s in slots:
                w = b.lw[s]
                if w is not None:
                    o.deps.add(w)
                for r in b.rd[s]:
                    o.deps.add(r)
        for b, slots in rl:
            for s in slots:
                b.rd[s].append(o)
        for b, slots in wl:
            for s in slots:
                b.lw[s] = o
                b.rd[s] = []
        o.deps.discard(o)
        self.ops.append(o)
        if final:
            self.final_dma.append(o)
        return o

    def mm(self, out, lhsT, rhs, start=True, stop=True, reads=(), writes=(), **kw):
        return self.op("pe", lambda e: e.matmul(out, lhsT, rhs, start=start, stop=stop, **kw), reads, writes)

    def tr(self, out, in_, ident, reads=(), writes=()):
        return self.op("pe", lambda e: e.transpose(out, in_, ident), reads, writes)

    def act(self, out, in_, func, reads=(), writes=(), **kw):
        return self.op("act", lambda e: e.activation(out, in_, func, **kw), reads, writes)

    def tt(self, eng, out, in0, in1, op, reads=(), writes=()):
        return self.op(eng, lambda e: e.tensor_tensor(out, in0, in1, op), reads, writes)

    def ts(self, eng, out, in0, s1, s2, op0, op1=None, reads=(), writes=()):
        if op1 is None:
            return self.op(eng, lambda e: e.tensor_scalar(out, in0, s1, None, op0), reads, writes)
        return self.op(eng, lambda e: e.tensor_scalar(out, in0, s1, s2, op0, op1), reads, writes)

    def stt(self, out, in0, scalar, in1, op0, op1, reads=(), writes=()):
        return self.op("dve", lambda e: e.scalar_tensor_tensor(out, in0, scalar, in1, op0, op1), reads, writes)

    def copy(self, eng, out, in_, reads=(), writes=()):
        if eng == "act":
            return self.op(eng, lambda e: e.copy(out, in_), reads, writes)
        return self.op(eng, lambda e: e.tensor_copy(out, in_), reads, writes)

    def dma(self, out, in_, reads=(), writes=(), eng="sp", final=False, **kw):
        if eng == "pool":
            kw.setdefault("max_dma_last_dim", 4096)
        return self.op(eng, lambda e: e.dma_start(out, in_, **kw), reads, writes, dma=True, final=final)

    def finalize(self):
        ops = self.ops
        for o in ops:
            need = []
            for d in o.deps:
                if d.is_dma or o.is_dma:
                    need.append(d)
                elif d.eng != o.eng:
                    need.append(d)
                elif o.eng != "pe":
                    need.append(d)
            best = {}
            keep = []
            for d in need:
                if d.is_dma:
                    keep.append(d)
                else:
                    b = best.get(d.eng)
                    if b is None or d.idx > b.idx:
                        best[d.eng] = d
            keep.extend(best.values())
            o.deps = keep
            for d in keep:
                if not d.is_dma:
                    d.sig = True
        cnt = {e: 0 for e in ENGS}
        dcnt = {e: 0 for e in ENGS}
        for o in ops:
            if o.is_dma:
                i = dcnt[o.eng]
                dcnt[o.eng] += 1
                o.dsem = (o.eng, i % N_DMA_SEMS)
                o.dval = 16 * (i // N_DMA_SEMS + 1)
                if i >= N_DMA_SEMS:
                    o.prewait = (o.dsem, o.dval - 16)
            elif o.sig:
                cnt[o.eng] += 1
                o.sig_idx = cnt[o.eng]
        waited = {e: {} for e in ENGS}
        nw = 0
        for o in ops:
            w = waited[o.eng]
            req = {}
            if o.prewait is not None:
                req[o.prewait[0]] = o.prewait[1]
            for d in o.deps:
                if d.is_dma:
                    k, v = d.dsem, d.dval
                else:
                    k, v = d.eng, d.sig_idx
                if req.get(k, 0) < v:
                    req[k] = v
            for k, v in req.items():
                if w.get(k, 0) < v:
                    w[k] = v
                    o.waits.append((k, v))
                    nw += 1
        self.stats = dict(n_ops=len(ops), n_waits=nw, sig=dict(cnt), dma=dict(dcnt),
                          per_eng={e: sum(1 for o in ops if o.eng == e) for e in ENGS})
        return self.stats

    def emit(self):
        nc = self.nc
        st = self.stack
        esem = {e: st.enter_context(nc.semaphore(f"s_{e}")) for e in ENGS}
        dsem = {}
        for e in ENGS:
            if self.stats["dma"][e]:
                for j in range(N_DMA_SEMS):
                    dsem[(e, j)] = st.enter_context(nc.semaphore(f"d_{e}{j}"))

        def semof(k):
            return dsem[k] if isinstance(k, tuple) else esem[k]

        per = {e: [o for o in self.ops if o.eng == e] for e in ENGS}
        finals = self.final_dma
        block = st.enter_context(nc.Block())

        def run(eh, name):
            for o in per[name]:
                for k, v in o.waits:
                    eh.wait_ge(semof(k), v)
                ins = o.fn(eh)
                if o.is_dma:
                    ins.then_inc(dsem[o.dsem], 16)
                elif o.sig:
                    ins.then_inc(esem[name], 1)
            if name == "sp":
                for o in finals:
                    eh.wait_ge(dsem[o.dsem], o.dval)

        @block.tensor
        def _(e):
            run(e, "pe")

        @block.scalar
        def _(e):
            run(e, "act")

        @block.vector
        def _(e):
            run(e, "dve")

        @block.gpsimd
        def _(e):
            run(e, "pool")

        @block.sync
        def _(e):
            run(e, "sp")


CF_IDENT, CF_TRIU, CF_ONES, CF_M1, CF_M2 = 0, 128, 256, 384, 896
NCF = 1408
CB_IDENT, CB_ONES, CB_NEGM, CB_CM, CB_DM, CB_WM, CB_E, CB_OVL = 0, 128, 256, 384, 2432, 4480, 4864, 6912
NCB = 6945


def _const_tables():
    p = np.arange(128)
    ctf = np.zeros((128, NCF), np.float32)
    ctf[:, CF_IDENT:CF_IDENT + 128] = np.eye(128)
    ctf[:, CF_TRIU:CF_TRIU + 128] = (p[:, None] <= p[None, :])
    ctf[:, CF_ONES:CF_ONES + 128] = 1.0
    m1 = np.zeros((128, 16, 32), np.float32)
    m2 = np.zeros((128, 16, 32), np.float32)
    j = np.arange(32)
    for qb in range(16):
        t = qb * 128 + p
        cur = t // 64
        forced = (j[None, :] == 0) | (j[None, :] == cur[:, None]) | (j[None, :] == cur[:, None] - 1)
        future = j[None, :] > cur[:, None]
        m1[:, qb, :] = np.where(forced | future, 0.0, 1.0)
        m2[:, qb, :] = np.where(future, -1e30, np.where(forced, 1e9, 0.0))
    ctf[:, CF_M1:CF_M1 + 512] = m1.reshape(128, 512)
    ctf[:, CF_M2:CF_M2 + 512] = m2.reshape(128, 512)

    ctb = np.zeros((128, NCB), np.float32)
    ctb[:, CB_IDENT:CB_IDENT + 128] = np.eye(128)
    ctb[:, CB_ONES:CB_ONES + 128] = 1.0
    ctb[:, CB_NEGM:CB_NEGM + 128] = np.where(p[None, :] < p[:, None], -10000.0, 0.0)
    tpos = np.arange(T)
    n = np.arange(128)
    ctb[:, CB_CM:CB_CM + T] = np.where(tpos[None, :] >= 16 * n[:, None] + 31, 0.0, NEG)
    dm = np.zeros((128, 4, 512), np.float32)
    tl = np.arange(512)
    for i in range(4):
        dm[:, i, :] = np.where(tl[None, :] >= 128 * i + p[:, None], 0.0, NEG)
    ctb[:, CB_DM:CB_DM + 2048] = dm.reshape(128, 2048)
    wm = np.zeros((128, 3, 128), np.float32)
    b = np.arange(128)
    wm[:, 0, :] = np.where(b[None, :] >= p[:, None], 0.0, NEG)
    wm[:, 2, :] = np.where(b[None, :] < p[:, None], 0.0, NEG)
    ctb[:, CB_WM:CB_WM + 384] = wm.reshape(128, 384)
    e = np.zeros((128, 16, 128), np.float32)
    for c in range(16):
        for key in range(128):
            e[2 * c + key // 64, c, key] = 1.0
    ctb[:, CB_E:CB_E + 2048] = e.reshape(128, 2048)
    ovl = np.zeros((128, 33), np.float32)
    cs = 16 * n
    ss = 64 * np.arange(32)
    ovl[:, 0:32] = ((cs[:, None] < ss[None, :] + 64) & (cs[:, None] + 32 > ss[None, :]))
    ovl[:, 32] = 1.0
    ctb[:, CB_OVL:CB_OVL + 33] = ovl

    lk = np.zeros((4, T), np.float32)
    lk[0] = 128 * (tpos // 128)
    lk[1] = tpos % 128
    lk[2] = 1.0
    lk[3] = 1.0
    lkc = np.zeros((4, 128), np.float32)
    lkc[0] = 16 * n
    lkc[1] = 31
    lkc[2] = 1.0
    lkc[3] = 1.0
    rh = np.zeros((4, 8, T), np.float32)
    for h in range(8):
        s8 = 8.0 * 2.0 ** (-(h + 1))
        rh[0, h] = s8
        rh[1, h] = s8
        rh[2, h] = -s8 * 128 * (tpos // 128)
        rh[3, h] = -s8 * (tpos % 128)
    return ctf, ctb, lk, lkc, rh


PP_CONVW, PP_CONVB, PP_DTB, PP_ALOG, PP_DSKIP, PP_SSDNG = 0, 32, 40, 48, 56, 64
PP_SCW, PP_SGNG, PP_SGB, PP_QG, PP_KG12, PP_KG0, PP_PE = 576, 588, 1100, 1612, 1613, 1615, 1679
NPP = 1711
PL_ADAB, PL_GMIX, PL_GFFN = 0, 48, 56
NPL = 64


def _pack_params(inp):
    L = DEPTH
    rep = lambda v: np.broadcast_to(np.asarray(v, np.float32).reshape(1, -1), (128, v.size))
    pp = np.zeros((L, 128, NPP), np.float32)
    pl = np.zeros((128, L, NPL), np.float32)
    for l in range(L):
        pp[l, :, PP_CONVW:PP_CONVW + 32] = inp["ssd_conv_w"][l].reshape(4, 8, 128).transpose(2, 1, 0).reshape(128, 32)
        pp[l, :, PP_CONVB:PP_CONVB + 8] = inp["ssd_conv_b"][l].reshape(8, 128).T
        pp[l, :, PP_DTB:PP_DTB + 8] = rep(inp["ssd_dt_bias"][l])
        pp[l, :, PP_ALOG:PP_ALOG + 8] = rep(inp["ssd_a_log"][l])
        pp[l, :, PP_DSKIP:PP_DSKIP + 8] = rep(inp["ssd_d"][l])
        pp[l, :, PP_SSDNG:PP_SSDNG + 512] = rep(inp["ssd_norm_g"][l])
        pp[l, :, PP_SCW:PP_SCW + 12] = inp["sc_conv_w"][l].reshape(3, 4, 128).transpose(2, 1, 0).reshape(128, 12)
        pp[l, :, PP_SGNG:PP_SGNG + 512] = rep(inp["sg_norm_g"][l])
        pp[l, :, PP_SGB:PP_SGB + 512] = rep(inp["sg_b"][l])
        pp[l, :, PP_QG] = np.tile(inp["nsa_q_norm_g"][l], 2)
        pp[l, :, PP_KG12] = np.tile(inp["nsa_k_norm_g"][l, 1], 2)
        pp[l, :, PP_KG12 + 1] = np.tile(inp["nsa_k_norm_g"][l, 2], 2)
        pp[l, :, PP_KG0:PP_KG0 + 64] = rep(inp["nsa_k_norm_g"][l, 0])
        pp[l, :, PP_PE:PP_PE + 32] = inp["nsa_cmp_pe"][l].reshape(2, 16, 128).transpose(2, 0, 1).reshape(128, 32)
        pl[:, l, PL_ADAB:PL_ADAB + 48] = inp["ada_b"][l].reshape(48, 128).T
        pl[:, l, PL_GMIX:PL_GMIX + 8] = inp["norm_mix_g"][l].reshape(8, 128).T
        pl[:, l, PL_GFFN:PL_GFFN + 8] = inp["norm_ffn_g"][l].reshape(8, 128).T
    return pp, pl


def build_program(n_layers=DEPTH, n_seq=SEQ_PER_CORE, debug=False, stop=None):
    nc = bass.Bass("TRN2", target_bir_lowering=False)
    L = DEPTH
    din = lambda name, shape: nc.dram_tensor(name, list(shape), F32, kind="ExternalInput").ap()
    xT_d = din("xT", [n_seq, D, T])
    cT_d = din("cT", [128, KC, n_seq])
    pp_d = din("pp", [L, 128, NPP])
    pl_d = din("pl", [128, L, NPL])
    ctf_d = din("ctf", [128, NCF])
    ctb_d = din("ctb", [128, NCB])
    lk_d = din("lk", [4, T])
    lkc_d = din("lkc", [4, 128])
    rh_d = din("rh", [4, 8, T])
    sgw_d = din("sgwT", [L, 128, 4, 128])
    ada_w_d = din("ada_w", [L, D, 6 * D])
    w_in_d = din("w_in", [L, D, D_IN])
    w1_d = din("cmp_w1", [L, 2, 2048, 64])
    w2_d = din("cmp_w2", [L, 2, 64, 64])
    wb_d = din("w_branch", [L, 4, 512, D])
    wg_d = din("w_branch_gate", [L, 4, D, D])
    wo_d = din("w_out", [L, D, D])
    wfi_d = din("w_ffn_in", [L, D, 2 * D_FF])
    wfo_d = din("w_ffn_out", [L, D_FF, D])
    outT_d = nc.dram_tensor("outT", [n_seq, D, T], F32, kind="ExternalOutput").ap()
    xa_d = nc.dram_tensor("xres_a", [D, T], F32, kind="Internal").ap()
    xb_d = nc.dram_tensor("xres_b", [D, T], F32, kind="Internal").ap()
    dbg_d = None
    if debug:
        dbg_d = nc.dram_tensor("dbg", [4, 128, 4, T], F32, kind="ExternalOutput").ap()

    with ExitStack() as st:
        P = Prog(nc, st)
        XA = Buf(xa_d, "xa", 32)
        XB = Buf(xb_d, "xb", 32)
        OUT = Buf(outT_d, "out", 32)

        ctf = P.sbuf([128, NCF], F32, "ctf")
        ctb = P.sbuf([128, NCB], BF16, "ctb")
        pp = P.sbuf([128, NPP], F32, "pp")
        plb = P.sbuf([128, L, NPL], F32, "pl")
        sc = P.sbuf([128, KC, n_seq], F32, "sc")
        modT = P.sbuf([128, L, 48, n_seq], F32, "modT")
        der = P.sbuf([128, 2, 8], F32, "der")
        hT = P.sbuf([128, KC, T], BF16, "hT", nslots=32)
        oT = [P.sbuf([128, 4, T], BF16, f"oT{i}", nslots=16) for i in range(4)]
        mT = P.sbuf([128, KC, T], BF16, "mT", nslots=32)
        wbufs = [P.sbuf([128, KC, 512], BF16, f"wb{i}") for i in range(3)]
        psb = [P.psum([128, 512], F32, f"bank{i}") for i in range(8)]
        sq_b = [P.sbuf([128, 512], BF16, f"sq{i}") for i in range(2)]
        f32s = [P.sbuf([128, 512], F32, f"fs{i}") for i in range(4)]
        f32l = [P.sbuf([128, 512], F32, f"fl{i}") for i in range(2)]
        smalls = P.sbuf([128, 256], F32, "smalls", nslots=1)
        dtb = P.sbuf([128, NB, 8], F32, "dt")
        gsig = P.sbuf([128, NB, 24], F32, "gsig")
        kcaug = P.sbuf([68, 2, 128], BF16, "kcaug")
        VC = P.sbuf([128, 2, 97], BF16, "VC")
        cbias = P.sbuf([64, 2], F32, "cbias")
        GTb = P.sbuf([64, 128], BF16, "GT")
        kcn = P.sbuf([128, 64], BF16, "kcn")
        peb = P.sbuf([128, 32], BF16, "peb")
        sgwb = P.sbuf([128, 4, 128], BF16, "sgwb")
        state = P.sbuf([128, 512], F32, "state")
        stateb = P.sbuf([128, 512], BF16, "stateb")
        tails = P.sbuf([128, 8, 3], F32, "tails")
        raw = [P.sbuf([128, 515], F32, f"raw{i}") for i in range(2)]
        zero_b = P.sbuf([1, 260], BF16, "zerob")
        negA = P.sbuf([128, 8], F32, "negA")

        identF = ctf[:, CF_IDENT:CF_IDENT + 128]
        triU = ctf[:, CF_TRIU:CF_TRIU + 128]
        onesF = ctf[:, CF_ONES:CF_ONES + 128]
        identB = ctb[:, CB_IDENT:CB_IDENT + 128]
        onesB = ctb[:, CB_ONES:CB_ONES + 128]
        NEGM = ctb[:, CB_NEGM:CB_NEGM + 128]

        def CS(buf, chunks):
            if isinstance(chunks, int):
                chunks = (chunks,)
            return (buf, [4 * c + i for c in chunks for i in range(4)])

        def TS(buf, chunk, tt):
            return (buf, 4 * chunk + tt)

        def TSA(buf, nch, tt):
            return (buf, [4 * c + tt for c in range(nch)])

        held = set()
        rr = {"ps": 0, "wb": 0, "sq": 0, "fs": 0, "raw": 0, "fl": 0}

        def ps_get(hold=False):
            for _ in range(16):
                i = rr["ps"] % 8
                rr["ps"] += 1
                if i not in held:
                    if hold:
                        held.add(i)
                    return psb[i]
            raise RuntimeError("no psum bank")

        def ps_release(b):
            held.discard(psb.index(b))

        def psbf(b):
            return b.t[:].bitcast(BF16)

        def wb_get():
            i = rr["wb"] % 3
            rr["wb"] += 1
            return wbufs[i]

        def sq_get():
            i = rr["sq"] % 2
            rr["sq"] += 1
            return sq_b[i]

        def fs_get():
            i = rr["fs"] % 4
            rr["fs"] += 1
            return f32s[i]

        def fl_get():
            i = rr["fl"] % 2
            rr["fl"] += 1
            return f32l[i]

        def wload(dst_ap, src_ap, wbuf):
            P.dma(dst_ap, src_ap, writes=[wbuf], eng="pool")

        def kpn(ap2d):
            return ap2d.rearrange("(k p) n -> p k n", p=128)

        def mt_view(s0, ns, parts, shape_tail, dtype=BF16):
            return _view(mT, s0, ns, parts, shape_tail, dtype)

        def _view(buf, s0, ns, parts, shape_tail, dtype=BF16):
            ap = buf.t[0:parts, s0:s0 + ns, :].rearrange("p a b -> p (a b)")
            if dtype == F32:
                ap = ap.bitcast(F32)
            n = int(np.prod(shape_tail))
            ap = ap[:, 0:n]
            if len(shape_tail) == 1:
                return ap
            if len(shape_tail) == 2:
                return ap.rearrange("p (a b) -> p a b", a=shape_tail[0])
            if len(shape_tail) == 3:
                return ap.rearrange("p (a b c) -> p a b c", a=shape_tail[0], b=shape_tail[1])
            raise ValueError

        P.dma(ctf[:], ctf_d, writes=[ctf])
        P.dma(ctb[:], ctb_d, writes=[ctb], eng="pool")
        P.dma(plb[:], pl_d, writes=[plb])
        P.dma(sc[:], cT_d, writes=[sc])
        P.op("dve", lambda e: e.memset(zero_b[:], 0.0), writes=[zero_b])
        P.dma(kcaug[64:68, 0, :], lkc_d, writes=[kcaug], eng="pool")
        P.dma(kcaug[64:68, 1, :], lkc_d, writes=[kcaug], eng="pool")
        P.copy("dve", VC[:, 0, 64:97], ctb[:, CB_OVL:CB_OVL + 33], reads=[ctb], writes=[VC])
        P.copy("dve", VC[:, 1, 64:97], ctb[:, CB_OVL:CB_OVL + 33], reads=[ctb], writes=[VC])
        P.act(sc[:], sc[:], AF.Silu, reads=[sc], writes=[sc])
        stg = [(_view(hT, 4 * i, 4, 128, [KC, 512], F32), [CS(hT, range(4 * i, 4 * i + 4))]) for i in range(2)]
        nblk = 0
        for l in range(n_layers):
            for cb in range(12):
                sv, sacc = stg[nblk % 2]
                nblk += 1
                P.dma(sv, kpn(ada_w_d[l][:, cb * 512:(cb + 1) * 512]), writes=sacc)
                ps = ps_get()
                for m in range(4):
                    for k in range(KC):
                        P.mm(ps[:, m * n_seq:(m + 1) * n_seq], sv[:, k, m * 128:(m + 1) * 128], sc[:, k, :],
                             start=(k == 0), stop=(k == KC - 1), reads=sacc + [sc], writes=[ps])
                P.tt("dve", modT[:, l, cb * 4:(cb + 1) * 4, :],
                     ps[:, 0:4 * n_seq].rearrange("p (m s) -> p m s", m=4),
                     plb[:, l, PL_ADAB + cb * 4:PL_ADAB + (cb + 1) * 4].unsqueeze(2).broadcast_to([128, 4, n_seq]),
                     ALU.add, reads=[ps, plb], writes=[modT])

        def xsrc_ap(src, s, k, tt):
            if src is XA:
                return xa_d[k * 128:(k + 1) * 128, tt * TT:(tt + 1) * TT], [(XA, k * 4 + tt)]
            if src is XB:
                return xb_d[k * 128:(k + 1) * 128, tt * TT:(tt + 1) * TT], [(XB, k * 4 + tt)]
            return xT_d[s][k * 128:(k + 1) * 128, tt * TT:(tt + 1) * TT], []

        def norm_tile(xt, xacc, s, l, which, tt):
            A = der[:, which, :]
            shift_c0 = 0 if which == 0 else 24
            sl = slice(tt * TT, (tt + 1) * TT)
            ps = ps_get()
            for k in range(KC):
                sq = sq_get()
                P.act(sq[:], xt[:, k, :], AF.Square, reads=xacc, writes=[sq])
                P.mm(ps[:], onesB, sq[:], start=(k == 0), stop=(k == KC - 1), reads=[ctb, sq], writes=[ps])
            r = fl_get()
            P.act(r[:], ps[:], AF.Sqrt, reads=[ps], writes=[r], bias=EPS, scale=1.0 / D)
            P.op("dve", lambda e, r=r: e.reciprocal(r[:], r[:]), reads=[r], writes=[r])
            for k in range(KC):
                tmp = fs_get()
                P.stt(tmp[:], xt[:, k, :], A[:, k:k + 1], r[:], ALU.mult, ALU.mult,
                      reads=xacc + [der, r], writes=[tmp])
                P.act(hT[:, k, sl], tmp[:], AF.Identity, reads=[tmp, modT], writes=[TS(hT, k, tt)],
                      bias=modT[:, l, shift_c0 + k, s:s + 1], scale=1.0)

        def norm_phase(src, s, l, which):
            for tt in range(NTT):
                sl = slice(tt * TT, (tt + 1) * TT)
                half = tt % 2
                xt = _view(mT, 4 * half, 4, 128, [KC, TT], F32)
                xacc = [CS(mT, range(4 * half, 4 * half + 4))]
                if src is None:
                    P.dma(xt, kpn(xT_d[s][:, sl]), writes=xacc)
                else:
                    dsl = [(src, [k * 4 + tt for k in range(KC)])]
                    P.dma(xt, kpn(src.t[:, sl]), reads=dsl, writes=xacc)
                norm_tile(xt, xacc, s, l, which, tt)

        def proj_fm(wbuf, wcol0, m, tt, ps, parts=128):
            sl = slice(tt * TT, (tt + 1) * TT)
            for k in range(KC):
                P.mm(ps[0:m, :], wbuf[:, k, wcol0:wcol0 + m], hT[:, k, sl], start=(k == 0), stop=(k == KC - 1),
                     reads=[wbuf, TS(hT, k, tt)], writes=[ps])

        def layer(s, l, src, dst, dst_is_out):
            ppv = lambda c0, n: pp[:, c0:c0 + n]
            P.dma(pp[:], pp_d[l], writes=[pp])
            P.stt(der[:, 0, :], modT[:, l, 8:16, s], 1.0, plb[:, l, PL_GMIX:PL_GMIX + 8], ALU.add, ALU.mult,
                  reads=[modT, plb], writes=[der])
            P.stt(der[:, 1, :], modT[:, l, 32:40, s], 1.0, plb[:, l, PL_GFFN:PL_GFFN + 8], ALU.add, ALU.mult,
                  reads=[modT, plb], writes=[der])
            gate1 = lambda k: modT[:, l, 16 + k, s:s + 1]
            gate2 = lambda k: modT[:, l, 40 + k, s:s + 1]
            P.act(negA[:], ppv(PP_ALOG, 8), AF.Exp, reads=[pp], writes=[negA])
            P.ts("dve", negA[:], negA[:], -1.0, None, ALU.mult, reads=[negA], writes=[negA])
            P.copy("dve", peb[:], ppv(PP_PE, 32), reads=[pp], writes=[peb])

            norm_phase(None if src is None else src, s, l, 0)

            if stop == "dbgh":
                for i in range(2):
                    for cc in range(4):
                        stg32 = _view(oT[0], 0, 2, 128, [T], F32)
                        P.copy("dve", stg32, hT[:, 4 * i + cc, :], reads=[CS(hT, 4 * i + cc)], writes=[CS(oT[0], (0, 1))])
                        P.dma(dbg_d[i, :, cc, :], stg32, reads=[CS(oT[0], (0, 1))], final=True)
                for cc in range(4):
                    stg32 = _view(oT[0], 0, 2, 128, [T], F32)
                    P.op("dve", lambda e, stg32=stg32: e.memset(stg32, 0.0), writes=[CS(oT[0], (0, 1))])
                    P.copy("dve", stg32[:, 0:48], modT[:, l, :, s], reads=[modT], writes=[CS(oT[0], (0, 1))])
                    P.copy("dve", stg32[:, 48:64], der[:].rearrange("p a b -> p (a b)"), reads=[der], writes=[CS(oT[0], (0, 1))])
                    P.dma(dbg_d[2, :, cc, :], stg32, reads=[CS(oT[0], (0, 1))], final=True)
                return
            if stop == "n1":
                return
            qaug = _view(mT, 0, 2, 68, [8, TT])
            QA = [CS(mT, (0, 1))]
            kaug_s = _view(mT, 2, 2, 68, [2, T])
            KS = [CS(mT, (2, 3))]
            kaug_w = _view(mT, 4, 2, 68, [2, T])
            KW = [CS(mT, (4, 5))]
            MB = _view(mT, 6, 2, 32, [2, T])
            MBA = [CS(mT, (6, 7))]
            vslc = _view(oT[0], 0, 2, 128, [NB, 2, 66])
            VS = [CS(oT[0], (0, 1))]
            vwin = _view(oT[0], 2, 2, 128, [NB, 2, 66])
            VW = [CS(oT[0], (2, 3))]
            cmpk = _view(oT[1], 0, 2, 64, [2, T])
            CK = [CS(oT[1], (0, 1))]
            cmpv = _view(oT[1], 2, 2, 64, [2, T])
            CV = [CS(oT[1], (2, 3))]
            ocomb = _view(oT[2], 0, 2, 128, [4, 8, 64], F32)
            OC = [CS(oT[2], (0, 1))]
            cmpP = _view(oT[2], 2, 1, 128, [4, TT])
            CP = [CS(oT[2], 2)]
            PTs = _view(oT[2], 3, 1, 128, [4, TT])
            PTA = [CS(oT[2], 3)]

            for g in range(2):
                P.dma(kaug_s[64:68, g, :], lk_d, writes=KS, eng="pool")
                P.dma(kaug_w[64:68, g, :], lk_d, writes=KW, eng="pool")
            P.op("dve", lambda e: e.memset(vslc[:, :, :, 64:65], 1.0), writes=VS)
            P.op("dve", lambda e: e.memset(vwin[:, :, :, 64:65], 1.0), writes=VW)

            if stop == "tm0":
                return
            wsm = wb_get()
            W = w_in_d[l]
            wload(wsm[:, :, 0:128], kpn(W[:, C_VSLC:C_VSLC + 128]), wsm)
            wload(wsm[:, :, 128:256], kpn(W[:, C_VWIN:C_VWIN + 128]), wsm)
            wload(wsm[:, :, 256:384], kpn(W[:, C_GATES - 104:C_GATES + 24]), wsm)
            wload(wsm[:, :, 384:512], kpn(W[:, C_DT - 120:C_DT + 8]), wsm)
            if stop == "tm1":
                return
            for tb in range(NB):
                bs = slice(tb * 128, (tb + 1) * 128)
                ps = ps_get()
                for k in range(KC):
                    P.mm(ps[:, 0:512], hT[:, k, bs], wsm[:, k, 0:512], start=(k == 0), stop=(k == KC - 1),
                         reads=[TS(hT, k, tb // 4), wsm], writes=[ps])
                import os as _os
                _sk = _os.environ.get("K_SKIP", "")
                if "a" not in _sk:
                    P.copy("dve", vslc[:, tb, :, 0:64], ps[:, 0:128].rearrange("p (g d) -> p g d", g=2),
                           reads=[ps], writes=VS)
                if "b" not in _sk:
                    P.copy("dve", vwin[:, tb, :, 0:64], ps[:, 128:256].rearrange("p (g d) -> p g d", g=2),
                           reads=[ps], writes=VW)
                if "c" not in _sk:
                    P.copy("dve", gsig[:, tb, :], ps[:, 360:384], reads=[ps], writes=[gsig])
                if "d" not in _sk:
                    P.tt("dve", dtb[:, tb, :], ps[:, 504:512], ppv(PP_DTB, 8), ALU.add, reads=[ps, pp], writes=[dtb])
            if stop == "tm2":
                return
            P.act(gsig[:], gsig[:], AF.Sigmoid, reads=[gsig], writes=[gsig])
            P.act(dtb[:], dtb[:], AF.Exp, reads=[dtb], writes=[dtb])
            P.act(dtb[:], dtb[:], AF.Ln, reads=[dtb], writes=[dtb], bias=1.0, scale=1.0)

            if stop == "tmsmall":
                return
            wk = wb_get()
            wload(wk[:, :, 0:128], kpn(W[:, C_KCMP:C_KCMP + 128]), wk)
            wload(wk[:, :, 128:256], kpn(W[:, C_VCMP:C_VCMP + 128]), wk)
            wload(wk[:, :, 256:384], kpn(W[:, C_KSLC:C_KSLC + 128]), wk)
            wload(wk[:, :, 384:512], kpn(W[:, C_KWIN:C_KWIN + 128]), wk)

            def norm64(ps, gcol, out_ap, out_acc):
                sq = sq_get()
                P.act(sq[0:64, :], ps[0:64, :], AF.Square, reads=[ps], writes=[sq])
                ps2 = ps_get()
                P.mm(ps2[0:64, :], onesB[0:64, 0:64], sq[0:64, :], reads=[ctb, sq], writes=[ps2])
                r = fs_get()
                P.act(r[0:64, :], ps2[0:64, :], AF.Sqrt, reads=[ps2], writes=[r], bias=EPS, scale=1.0 / 64)
                P.op("dve", lambda e, r=r: e.reciprocal(r[0:64, :], r[0:64, :]), reads=[r], writes=[r])
                P.stt(out_ap, ps[0:64, :], pp[0:64, gcol:gcol + 1], r[0:64, :], ALU.mult, ALU.mult,
                      reads=[ps, pp, r], writes=out_acc)

            for which in range(4):
                for g in range(2):
                    for tt in range(NTT):
                        sl = slice(tt * TT, (tt + 1) * TT)
                        ps = ps_get()
                        proj_fm(wk, which * 128 + g * 64, 64, tt, ps)
                        if which == 0:
                            P.copy("act", cmpk[:, g, sl], ps[0:64, :], reads=[ps], writes=CK)
                        elif which == 1:
                            P.copy("act", cmpv[:, g, sl], ps[0:64, :], reads=[ps], writes=CV)
                        elif which == 2:
                            norm64(ps, PP_KG12, kaug_s[0:64, g, sl], KS)
                        else:
                            norm64(ps, PP_KG12 + 1, kaug_w[0:64, g, sl], KW)

            if stop == "kproj":
                return
            for kv in range(2):
                wcm = wb_get()
                wcf = wcm[:].rearrange("p k n -> p (k n)")
                w1b_v = wcf[0:64, 0:2048].rearrange("p (l e) -> p l e", l=32)
                w1f_v = wcf[:, 2048:3072].rearrange("p (c e) -> p c e", c=16)
                w2b_v = wcf[0:64, 3072:3136]
                w1b = w1f = w2b = wcm
                P.dma(w1b_v, w1_d[l, kv].rearrange("(l d) e -> d l e", d=64), writes=[wcm], eng="pool")
                P.dma(w1f_v, w1_d[l, kv].rearrange("(c p) e -> p c e", p=128), writes=[wcm], eng="pool")
                P.dma(w2b_v, w2_d[l, kv], writes=[wcm], eng="pool")
                psc = ps_get()
                for c in range(16):
                    P.mm(psc[0:64, 0:1], w1f_v[:, c, :], peb[:, kv * 16 + c:kv * 16 + c + 1], start=(c == 0), stop=(c == 15),
                         reads=[w1f, peb], writes=[psc])
                P.copy("dve", cbias[:, kv:kv + 1], psc[0:64, 0:1], reads=[psc], writes=[cbias])
                rawb, racc = (cmpk, CK) if kv == 0 else (cmpv, CV)
                for g in range(2):
                    ps = ps_get()
                    for li in range(32):
                        P.mm(ps[0:64, 0:127], w1b_v[:, li, :], rawb[:, g, li:li + 16 * 126 + 1:16], start=(li == 0), stop=(li == 31),
                             reads=[w1b] + racc, writes=[ps])
                    P.act(GTb[:, 0:127], ps[0:64, 0:127], AF.Gelu_apprx_tanh, reads=[ps, cbias], writes=[GTb],
                          bias=cbias[:, kv:kv + 1], scale=1.0)
                    ps2 = ps_get()
                    P.mm(ps2[0:127, 0:64], GTb[:, 0:127], w2b_v, reads=[GTb, w2b], writes=[ps2])
                    if kv == 0:
                        junk = fs_get()
                        P.act(junk[0:127, 0:64], ps2[0:127, 0:64], AF.Square, reads=[ps2], writes=[junk, smalls],
                              accum_out=smalls[0:127, 0:1])
                        P.act(smalls[0:127, 1:2], smalls[0:127, 0:1], AF.Sqrt, reads=[smalls], writes=[smalls],
                              bias=EPS, scale=1.0 / 64)
                        P.op("dve", lambda e: e.reciprocal(smalls[0:127, 2:3], smalls[0:127, 1:2]), reads=[smalls], writes=[smalls])
                        P.stt(kcn[0:127, :], ps2[0:127, 0:64], smalls[0:127, 2:3], pp[0:127, PP_KG0:PP_KG0 + 64],
                              ALU.mult, ALU.mult, reads=[ps2, smalls, pp], writes=[kcn])
                        pst = ps_get()
                        P.tr(psbf(pst)[0:64, 0:127], kcn[0:127, :], identB[0:127, 0:127], reads=[kcn, ctb], writes=[pst])
                        P.copy("dve", kcaug[0:64, g, 0:127], psbf(pst)[0:64, 0:127], reads=[pst], writes=[kcaug])
                    else:
                        P.copy("dve", VC[0:127, g, 0:64], ps2[0:127, 0:64], reads=[ps2], writes=[VC])

            if stop == "compress":
                return
            wq = wb_get()
            wload(wq[:], kpn(W[:, C_Q:C_Q + 512]), wq)
            gs4 = gsig[:].rearrange("p b (h i) -> p b h i", i=3)
            m1v = ctf[:, CF_M1:CF_M1 + 512].rearrange("p (b j) -> p b j", b=16)
            m2v = ctf[:, CF_M2:CF_M2 + 512].rearrange("p (b j) -> p b j", b=16)
            Ev = ctb[0:32, CB_E:CB_E + 2048].rearrange("p (c k) -> p c k", c=16)
            DMv = ctb[:, CB_DM:CB_DM + 2048].rearrange("p (i t) -> p i t", i=4)
            WMv = ctb[:, CB_WM:CB_WM + 384]
            CMv = ctb[:, CB_CM:CB_CM + T]
            sm = smalls

            for tt in range(NTT):
                sl = slice(tt * TT, (tt + 1) * TT)
                P.dma(qaug[64:68, :, :], rh_d[:, :, sl], writes=QA, eng="pool")
                for h in range(8):
                    ps = ps_get()
                    proj_fm(wq, h * 64, 64, tt, ps)
                    norm64(ps, PP_QG, qaug[0:64, h, :], QA)
                nmax = min(127, 32 * tt + 31)
                for g in range(2):
                    for r in range(4):
                        h = 4 * g + r
                        pss = ps_get()
                        P.mm(pss[0:nmax, :], kcaug[0:68, g, 0:nmax], qaug[0:68, h, :], start=True, stop=False,
                             reads=[kcaug] + QA, writes=[pss])
                        P.mm(pss[0:nmax, :], identB[0:nmax, 0:nmax], CMv[0:nmax, sl], start=False, stop=True,
                             reads=[ctb], writes=[pss])
                        P.act(cmpP[0:nmax, r, :], pss[0:nmax, :], AF.Exp, reads=[pss], writes=[(oT[2], 8 + r)], scale=0.125)
                    for qb in range(4):
                        tb = 4 * tt + qb
                        qs = slice(qb * 128, (qb + 1) * 128)
                        pso = ps_get()
                        pso_v = pso[:, 0:388].rearrange("p (r c) -> p r c", r=4)
                        for r in range(4):
                            P.mm(pso_v[:, r, :], cmpP[0:nmax, r, qs], VC[0:nmax, g, :], reads=CP + [VC], writes=[pso])
                        P.ts("dve", sm[:, 0:4], pso_v[:, :, 96], 1e-30, None, ALU.max, reads=[pso], writes=[sm])
                        P.op("dve", lambda e: e.reciprocal(sm[:, 4:8], sm[:, 0:4]), reads=[sm], writes=[sm])
                        tmp = fs_get()
                        tv = tmp[:, 0:128].rearrange("p (r j) -> p r j", r=4)
                        P.tt("dve", tv, pso_v[:, :, 64:96], sm[:, 4:8].unsqueeze(2).broadcast_to([128, 4, 32]), ALU.mult,
                             reads=[pso, sm], writes=[tmp])
                        P.op("dve", lambda e, tv=tv: e.tensor_reduce(sm[:, 8:40], tv.rearrange("p r j -> p j r"), AX.X, ALU.add),
                             reads=[tmp], writes=[sm])
                        P.tt("dve", sm[:, 40:44], sm[:, 4:8], gs4[:, tb, 4 * g:4 * g + 4, 0], ALU.mult, reads=[sm, gsig], writes=[sm])
                        P.tt("dve", ocomb[:, qb, 4 * g:4 * g + 4, :], pso_v[:, :, 0:64],
                             sm[:, 40:44].unsqueeze(2).broadcast_to([128, 4, 64]), ALU.mult, reads=[pso, sm], writes=OC)
                        P.tt("dve", sm[:, 44:76], sm[:, 8:40], m1v[:, tb, :], ALU.mult, reads=[sm, ctf], writes=[sm])
                        P.tt("dve", sm[:, 44:76], sm[:, 44:76], m2v[:, tb, :], ALU.add, reads=[sm, ctf], writes=[sm])
                        P.op("dve", lambda e: e.max(sm[:, 76:84], sm[:, 44:76]), reads=[sm], writes=[sm])
                        nsel = sq_get()
                        P.ts("dve", nsel[:, 0:32], sm[:, 44:76], sm[:, 83:84], None, ALU.is_lt, reads=[sm], writes=[nsel])
                        pst = ps_get()
                        P.tr(psbf(pst)[0:32, 0:128], nsel[:, 0:32], identB, reads=[nsel, ctb], writes=[pst])
                        P.ts("dve", MB[:, g, tb * 128:(tb + 1) * 128], psbf(pst)[0:32, 0:128], NEG, None, ALU.mult,
                             reads=[pst], writes=MBA)
                items = []
                for h in range(8):
                    for branch in (1, 2):
                        if branch == 1:
                            chunks = list(range(0, 4 * tt + 4))
                        else:
                            chunks = list(range(max(0, 4 * tt - 2), 4 * tt + 4))
                        for ci, c in enumerate(chunks):
                            items.append((h, branch, ci, c, ci == len(chunks) - 1))
                accs = {}
                ptn = [0]

                def stage1(it):
                    h, branch, ci, c, last = it
                    g = h // 4
                    if ci == 0:
                        acc = ps_get(hold=True)
                        accs[(h, branch)] = acc
                        P.mm(acc[:, 0:260], zero_b[0:1, 0:128], zero_b[0:1, 0:260], start=True, stop=False,
                             reads=[zero_b], writes=[acc], skip_group_check=True)
                    ks = slice(c * 128, (c + 1) * 128)
                    i = c - 4 * tt
                    pss = ps_get()
                    pi = ptn[0] % 4
                    ptn[0] += 1
                    pt = PTs[:, pi, :]
                    pta = [(oT[2], 12 + pi)]
                    if branch == 1:
                        P.mm(pss[:, :], kaug_s[0:68, g, ks], qaug[0:68, h, :], start=True, stop=False,
                             reads=KS + QA, writes=[pss])
                        P.mm(pss[:, :], Ev[:, c, :], MB[:, g, sl], start=False, stop=(i < 0),
                             reads=[ctb] + MBA, writes=[pss])
                        if i >= 0:
                            P.mm(pss[:, :], identB, DMv[:, i, :], start=False, stop=True, reads=[ctb], writes=[pss])
                        P.act(pt, pss[:, :], AF.Exp, reads=[pss], writes=pta, scale=0.125)
                        return (pt, pta, None)
                    qlo = max(i, 0)
                    qhi = min(i + 2, 3)
                    nq = qhi - qlo + 1
                    rel_lo = qlo - i
                    N = nq * 128
                    P.mm(pss[:, 0:N], kaug_w[0:68, g, ks], qaug[0:68, h, qlo * 128:(qhi + 1) * 128], start=True, stop=False,
                         reads=KW + QA, writes=[pss])
                    P.mm(pss[:, 0:N], identB, WMv[:, rel_lo * 128:(rel_lo + nq) * 128], start=False, stop=True,
                         reads=[ctb], writes=[pss])
                    P.act(pt[:, 0:N], pss[:, 0:N], AF.Exp, reads=[pss], writes=pta, scale=0.125)
                    return (pt, pta, (qlo, nq))

                def stage2(it, st1):
                    h, branch, ci, c, last = it
                    g = h // 4
                    pt, pta, wq_ = st1
                    acc = accs[(h, branch)]
                    acc_v = acc[:, 0:260].rearrange("p (q c) -> p q c", q=4)
                    i = c - 4 * tt
                    if branch == 1:
                        for qb in range(max(i, 0), 4):
                            P.mm(acc_v[:, qb, :], pt[:, qb * 128:(qb + 1) * 128], vslc[:, c, g, 0:65], start=False, stop=False,
                                 reads=pta + VS, writes=[acc], skip_group_check=True)
                    else:
                        qlo, nq = wq_
                        for qi in range(nq):
                            qb = qlo + qi
                            P.mm(acc_v[:, qb, :], pt[:, qi * 128:(qi + 1) * 128], vwin[:, c, g, 0:65], start=False, stop=False,
                                 reads=pta + VW, writes=[acc], skip_group_check=True)
                    if last:
                        o_ = 100 + 12 * ((h * 2 + branch) % 2)
                        P.ts("dve", sm[:, o_:o_ + 4], acc_v[:, :, 64], 1e-30, None, ALU.max, reads=[acc], writes=[sm])
                        P.op("dve", lambda e: e.reciprocal(sm[:, o_ + 4:o_ + 8], sm[:, o_:o_ + 4]), reads=[sm], writes=[sm])
                        P.tt("dve", sm[:, o_ + 8:o_ + 12], sm[:, o_ + 4:o_ + 8], gs4[:, 4 * tt:4 * tt + 4, h, branch], ALU.mult,
                             reads=[sm, gsig], writes=[sm])
                        tmp = fs_get()
                        tv = tmp[:, 0:256].rearrange("p (q d) -> p q d", q=4)
                        P.tt("dve", tv, acc_v[:, :, 0:64], sm[:, o_ + 8:o_ + 12].unsqueeze(2).broadcast_to([128, 4, 64]), ALU.mult,
                             reads=[acc, sm], writes=[tmp])
                        P.tt("dve", ocomb[:, :, h, :], ocomb[:, :, h, :], tv, ALU.add, reads=OC + [tmp], writes=OC)
                        ps_release(acc)

                st_next = stage1(items[0])
                for ii, it in enumerate(items):
                    st_cur = st_next
                    if ii + 1 < len(items):
                        st_next = stage1(items[ii + 1])
                    stage2(it, st_cur)
                for qb in range(4):
                    tb = 4 * tt + qb
                    pst = ps_get()
                    oc2 = ocomb[:, qb, :, :].rearrange("p h d -> p (h d)")
                    for cc in range(4):
                        P.tr(pst[:, cc * 128:(cc + 1) * 128], oc2[:, cc * 128:(cc + 1) * 128], identF, reads=OC + [ctf], writes=[pst])
                    P.copy("act", oT[3][:, :, tb * 128:(tb + 1) * 128], pst[:, :].rearrange("p (c t) -> p c t", c=4),
                           reads=[pst], writes=[TSA(oT[3], 4, tt)])

            if stop == "nsa":
                return
            wz = wb_get()
            wload(wz[:], kpn(W[:, C_Z:C_Z + 512]), wz)
            wx = [wb_get(), wb_get()]
            wload(wx[0][:], kpn(W[:, C_XBC:C_XBC + 512]), wx[0])
            wload(wx[1][:], kpn(W[:, C_XBC + 512:C_XBC + 1024]), wx[1])
            xbcT = _view(mT, 0, 2, 128, [8, TT])
            XBC = [CS(mT, (0, 1))]
            dtab = _view(mT, 2, 2, 128, [8, 128], F32)
            DTA = [CS(mT, (2, 3))]
            LT = _view(mT, 4, 2, 128, [8, 128], F32)
            LTA = [CS(mT, (4, 5))]
            Mb = _view(mT, 6, 1, 128, [8, 128])
            MA = [CS(mT, 6)]
            s7 = _view(mT, 7, 1, 128, [2048])
            S7 = [CS(mT, 7)]
            xdt = s7[:, 0:512].rearrange("p (h d) -> p h d", h=8)
            xdd = s7[:, 512:1024].rearrange("p (h d) -> p h d", h=8)
            xs_sb = s7[:, 1024:1536].rearrange("p (h d) -> p h d", h=8)
            Btm = s7[:, 1536:1792]
            P.op("dve", lambda e: e.memset(state[:], 0.0), writes=[state])
            P.op("dve", lambda e: e.memset(stateb[:], 0.0), writes=[stateb])
            P.op("dve", lambda e: e.memset(tails[:], 0.0), writes=[tails])
            convw = pp[:, PP_CONVW:PP_CONVW + 32].rearrange("p (c k) -> p c k", c=8)
            dsk = pp[:, PP_DSKIP:PP_DSKIP + 8]
            for tt in range(NTT):
                for ch in range(8):
                    ps = ps_get()
                    proj_fm(wx[ch // 4], (ch % 4) * 128, 128, tt, ps)
                    rw = raw[rr["raw"] % 2]
                    rr["raw"] += 1
                    P.copy("dve", rw[:, 0:3], tails[:, ch, :], reads=[tails], writes=[rw])
                    P.copy("act", rw[:, 3:515], ps[:, :], reads=[ps], writes=[rw])
                    P.copy("dve", tails[:, ch, :], rw[:, 512:515], reads=[rw], writes=[tails])
                    acc = fs_get()
                    P.ts("dve", acc[:], rw[:, 0:512], convw[:, ch, 0:1], None, ALU.mult, reads=[rw, pp], writes=[acc])
                    for kk in range(1, 4):
                        P.stt(acc[:], rw[:, kk:kk + 512], convw[:, ch, kk:kk + 1], acc[:], ALU.mult, ALU.add,
                              reads=[rw, pp, acc], writes=[acc])
                    P.act(xbcT[:, ch, :], acc[:], AF.Silu, reads=[acc, pp], writes=XBC,
                          bias=pp[:, PP_CONVB + ch:PP_CONVB + ch + 1], scale=1.0)
                for lt in range(4):
                    tb = 4 * tt + lt
                    cs = slice(lt * 128, (lt + 1) * 128)
                    bs = slice(tb * 128, (tb + 1) * 128)
                    psx = ps_get()
                    pxb = psbf(psx)
                    for ch in range(4):
                        P.tr(pxb[:, ch * 128:(ch + 1) * 128], xbcT[:, ch, cs], identB, reads=XBC + [ctb], writes=[psx])
                    psB = ps_get()
                    pBb = psbf(psB)
                    for g in range(2):
                        P.tr(pBb[:, g * 128:(g + 1) * 128], xbcT[:, 4 + g, cs], identB, reads=XBC + [ctb], writes=[psB])
                    P.tt("dve", sm[:, 120:128], dtb[:, tb, :], negA[:], ALU.mult, reads=[dtb, negA], writes=[sm])
                    P.copy("dve", dtab, sm[:, 120:128].unsqueeze(2).broadcast_to([128, 8, 128]), reads=[sm], writes=DTA)
                    psa = ps_get()
                    P.mm(psa[:, 0:8], triU, sm[:, 120:128], reads=[ctf, sm], writes=[psa])
                    P.mm(psa[:, 8:16], onesF, sm[:, 120:128], reads=[ctf, sm], writes=[psa])
                    P.copy("dve", sm[:, 128:136], psa[:, 0:8], reads=[psa], writes=[sm])
                    P.tt("dve", sm[:, 136:144], psa[:, 8:16], sm[:, 128:136], ALU.subtract, reads=[psa, sm], writes=[sm])
                    P.copy("dve", sm[:, 144:152], psa[:, 8:16], reads=[psa], writes=[sm])
                    P.ts("dve", sm[:, 176:184], sm[:, 128:136], -1.0, None, ALU.mult, reads=[sm], writes=[sm])
                    P.act(sm[:, 152:176], sm[:, 128:152], AF.Exp, reads=[sm], writes=[sm])
                    ea = sm[:, 152:160]
                    dec = sm[:, 160:168]
                    cdec = sm[:, 168:176]
                    psl = [ps_get(), ps_get()]
                    for h in range(8):
                        pl_ = psl[h // 4]
                        o_ = pl_[:, (h % 4) * 128:(h % 4 + 1) * 128]
                        P.mm(o_, dtab[:, h, :], triU, start=True, stop=False, reads=DTA + [ctf], writes=[pl_])
                        P.mm(o_, identB, NEGM, start=False, stop=True, reads=[ctb], writes=[pl_])
                    for h in range(8):
                        pl_ = psl[h // 4]
                        o_ = pl_[:, (h % 4) * 128:(h % 4 + 1) * 128]
                        P.act(LT[:, h, :], o_, AF.Exp, reads=[pl_, sm], writes=LTA, bias=sm[:, 176 + h:177 + h], scale=1.0)
                    psc_ = ps_get()
                    for g in range(2):
                        P.mm(psc_[:, g * 128:(g + 1) * 128], xbcT[:, 4 + g, cs], xbcT[:, 6 + g, cs], reads=XBC, writes=[psc_])
                    cbT = fs_get()
                    P.copy("act", cbT[:, 0:256], psc_[:, 0:256], reads=[psc_], writes=[cbT])
                    P.tt("dve", Mb.rearrange("p (g r) l -> p g r l", g=2),
                         LT.rearrange("p (g r) l -> p g r l", g=2),
                         cbT[:, 0:256].rearrange("p (g l) -> p g l", g=2).unsqueeze(2).broadcast_to([128, 2, 4, 128]),
                         ALU.mult, reads=LTA + [cbT], writes=MA)
                    pxv = pxb[:, 0:512].rearrange("p (h d) -> p h d", h=8)
                    P.tt("dve", xdt, pxv, dtb[:, tb, :].unsqueeze(2).broadcast_to([128, 8, 64]), ALU.mult,
                         reads=[psx, dtb], writes=S7)
                    P.copy("act", Btm, pBb[:, 0:256], reads=[psB], writes=S7)
                    P.tt("dve", xdd, xdt, dec.unsqueeze(2).broadcast_to([128, 8, 64]), ALU.mult, reads=S7 + [sm], writes=S7)
                    psy = ps_get()
                    for h in range(8):
                        P.mm(psy[:, h * 64:(h + 1) * 64], Mb[:, h, :], xdt[:, h, :], reads=MA + S7, writes=[psy])
                    pso = ps_get()
                    for g in range(2):
                        P.mm(pso[:, g * 256:(g + 1) * 256], xbcT[:, 6 + g, cs], stateb[:, g * 256:(g + 1) * 256],
                             reads=XBC + [stateb], writes=[pso])
                    y1 = fs_get()
                    y1v = y1[:].rearrange("p (h d) -> p h d", h=8)
                    P.tt("dve", y1v, pso[:, :].rearrange("p (h d) -> p h d", h=8),
                         ea.unsqueeze(2).broadcast_to([128, 8, 64]), ALU.mult, reads=[pso, sm], writes=[y1])
                    P.tt("dve", y1[:], y1[:], psy[:, :], ALU.add, reads=[y1, psy], writes=[y1])
                    y2 = fs_get()
                    P.tt("dve", y2[:].rearrange("p (h d) -> p h d", h=8), pxv,
                         dsk.unsqueeze(2).broadcast_to([128, 8, 64]), ALU.mult, reads=[psx, pp], writes=[y2])
                    P.tt("dve", y1[:], y1[:], y2[:], ALU.add, reads=[y1, y2], writes=[y1])
                    pst_ = ps_get()
                    for g in range(2):
                        P.mm(pst_[:, g * 256:(g + 1) * 256], Btm[:, g * 128:(g + 1) * 128],
                             xdd[:, 4 * g:4 * g + 4, :].rearrange("p h d -> p (h d)"), reads=S7, writes=[pst_])
                    stv = state[:].rearrange("p (h d) -> p h d", h=8)
                    P.tt("dve", stv, stv, cdec.unsqueeze(2).broadcast_to([128, 8, 64]), ALU.mult, reads=[state, sm], writes=[state])
                    P.tt("dve", state[:], state[:], pst_[:, :], ALU.add, reads=[state, pst_], writes=[state])
                    P.copy("act", stateb[:], state[:], reads=[state], writes=[stateb])
                    psz = ps_get()
                    for k in range(KC):
                        P.mm(psz[:, :], hT[:, k, bs], wz[:, k, :], start=(k == 0), stop=(k == KC - 1), reads=[TS(hT, k, tt), wz], writes=[psz])
                    zs = fs_get()
                    P.act(zs[:], psz[:, :], AF.Silu, reads=[psz], writes=[zs])
                    P.tt("dve", y1[:], y1[:], zs[:], ALU.mult, reads=[y1, zs], writes=[y1])
                    for g in range(2):
                        P.act(zs[:, g * 256:(g + 1) * 256], y1[:, g * 256:(g + 1) * 256], AF.Square, reads=[y1], writes=[zs, sm],
                              accum_out=sm[:, 184 + g:185 + g])
                    P.act(sm[:, 186:188], sm[:, 184:186], AF.Sqrt, reads=[sm], writes=[sm], bias=EPS, scale=1.0 / 256)
                    P.op("dve", lambda e: e.reciprocal(sm[:, 188:190], sm[:, 186:188]), reads=[sm], writes=[sm])
                    oa = sq_get()
                    for g in range(2):
                        P.stt(oa[:, g * 256:(g + 1) * 256], y1[:, g * 256:(g + 1) * 256], sm[:, 188 + g:189 + g],
                              pp[:, PP_SSDNG + g * 256:PP_SSDNG + (g + 1) * 256], ALU.mult, ALU.mult,
                              reads=[y1, sm, pp], writes=[oa])
                    pso2 = ps_get()
                    po2 = psbf(pso2)
                    for ch in range(4):
                        P.tr(po2[:, ch * 128:(ch + 1) * 128], oa[:, ch * 128:(ch + 1) * 128], identB, reads=[oa, ctb], writes=[pso2])
                    P.copy("act", oT[0][:, :, bs], po2[:, 0:512].rearrange("p (c t) -> p c t", c=4), reads=[pso2], writes=[TSA(oT[0], 4, tt)])

            if stop == "ssd":
                return
            wB = [wb_get(), wb_get(), wb_get()]
            for i_, c0 in enumerate((C_B, C_C, C_HX)):
                wload(wB[i_][:], kpn(W[:, c0:c0 + 512]), wB[i_])
            chx = _view(mT, 0, 3, 128, [2050], F32)
            CHX = [CS(mT, (0, 1, 2))]
            accB = _view(mT, 3, 2, 128, [2048], F32)
            ACB = [CS(mT, (3, 4))]
            bsb = _view(mT, 5, 1, 128, [2048])
            BSB = [CS(mT, 5)]
            P.op("dve", lambda e: e.memset(chx[:, 0:2], 0.0), writes=CHX)
            scw = pp[:, PP_SCW:PP_SCW + 12].rearrange("p (c k) -> p c k", c=4)
            for cc in range(4):
                for tt in range(NTT):
                    sl = slice(tt * TT, (tt + 1) * TT)
                    psb_ = ps_get()
                    proj_fm(wB[0], cc * 128, 128, tt, psb_)
                    psc_ = ps_get()
                    proj_fm(wB[1], cc * 128, 128, tt, psc_)
                    psh = ps_get()
                    proj_fm(wB[2], cc * 128, 128, tt, psh)
                    P.copy("act", bsb[:, sl], psb_[:, :], reads=[psb_], writes=BSB)
                    hx = fs_get()
                    P.copy("act", hx[:], psh[:, :], reads=[psh], writes=[hx])
                    P.tt("dve", chx[:, 2 + tt * TT:2 + (tt + 1) * TT], psc_[:, :], hx[:], ALU.mult, reads=[psc_, hx], writes=CHX)
                P.ts("dve", accB, chx[:, 0:2048], scw[:, cc, 0:1], None, ALU.mult, reads=CHX + [pp], writes=ACB)
                P.stt(accB, chx[:, 1:2049], scw[:, cc, 1:2], accB, ALU.mult, ALU.add, reads=CHX + [pp] + ACB, writes=ACB)
                P.stt(accB, chx[:, 2:2050], scw[:, cc, 2:3], accB, ALU.mult, ALU.add, reads=CHX + [pp] + ACB, writes=ACB)
                P.tt("dve", oT[1][:, cc, :], accB, bsb, ALU.mult, reads=ACB + BSB, writes=[CS(oT[1], cc)])

            if stop == "sconv":
                return
            wC = [wb_get(), wb_get()]
            wload(wC[0][:], kpn(W[:, C_GU:C_GU + 512]), wC[0])
            wload(wC[1][:], kpn(W[:, C_GV:C_GV + 512]), wC[1])
            sgw32 = fs_get()
            sgv = sgw32[:].rearrange("p (g t) -> p g t", g=4)
            P.dma(sgv, sgw_d[l], writes=[sgw32])
            P.tt("dve", sgwb[:], sgv, triU.unsqueeze(1).broadcast_to([128, 4, 128]), ALU.mult,
                 reads=[sgw32, ctf], writes=[sgwb])
            for cc in range(4):
                for tt in range(NTT):
                    sl = slice(tt * TT, (tt + 1) * TT)
                    ps = ps_get()
                    proj_fm(wC[0], cc * 128, 128, tt, ps)
                    P.act(oT[2][:, cc, sl], ps[:, :], AF.Gelu_apprx_tanh, reads=[ps], writes=[TS(oT[2], cc, tt)])
            sgb = pp[:, PP_SGB:PP_SGB + 512]
            for tb in range(NB):
                bs = slice(tb * 128, (tb + 1) * 128)
                ps = ps_get()
                for k in range(KC):
                    P.mm(ps[:, :], hT[:, k, bs], wC[1][:, k, :], start=(k == 0), stop=(k == KC - 1), reads=[TS(hT, k, tb // 4), wC[1]], writes=[ps])
                vg = fs_get()
                P.act(vg[:], ps[:, :], AF.Gelu_apprx_tanh, reads=[ps], writes=[vg])
                junk = fs_get()
                P.act(junk[:], vg[:], AF.Square, reads=[vg], writes=[junk, sm], accum_out=sm[:, 192:193])
                P.act(sm[:, 193:194], sm[:, 192:193], AF.Sqrt, reads=[sm], writes=[sm], bias=EPS, scale=1.0 / 512)
                P.op("dve", lambda e: e.reciprocal(sm[:, 194:195], sm[:, 193:194]), reads=[sm], writes=[sm])
                vn = sq_get()
                P.stt(vn[:], vg[:], sm[:, 194:195], pp[:, PP_SGNG:PP_SGNG + 512], ALU.mult, ALU.mult, reads=[vg, sm, pp], writes=[vn])
                ps2 = ps_get()
                for g in range(4):
                    P.mm(ps2[:, g * 128:(g + 1) * 128], vn[:, g * 128:(g + 1) * 128], sgwb[:, g, :], reads=[vn, sgwb], writes=[ps2])
                t1 = fs_get()
                P.tt("dve", t1[:], ps2[:, :], sgb, ALU.add, reads=[ps2, pp], writes=[t1])
                P.tt("dve", oT[2][:, :, bs], t1[:].rearrange("p (g t) -> p g t", g=4), oT[2][:, :, bs], ALU.mult,
                     reads=[t1, TSA(oT[2], 4, tb // 4)], writes=[TSA(oT[2], 4, tb // 4)])

            if debug and l == 0 and s == 0:
                for i in range(4):
                    for cc in range(4):
                        stg32 = _view(mT, 0, 2, 128, [T], F32)
                        P.copy("dve", stg32, oT[i][:, cc, :], reads=[CS(oT[i], cc)], writes=[CS(mT, (0, 1))])
                        P.dma(dbg_d[i, :, cc, :], stg32, reads=[CS(mT, (0, 1))], final=True)

            if stop == "sgate":
                return
            for j in range(KC):
                js = slice(j * 128, (j + 1) * 128)
                wg = wb_get()
                for i in range(4):
                    wload(wg[:, :, i * 128:(i + 1) * 128], kpn(wg_d[l, i][:, js]), wg)
                wbr = wb_get()
                wbv = wbr[:, 0:4, :].rearrange("p k (i n) -> p i k n", i=4)
                for i in range(4):
                    wload(wbv[:, i, :, :], kpn(wb_d[l, i][:, js]), wbr)
                for tt in range(NTT):
                    sl = slice(tt * TT, (tt + 1) * TT)
                    macc = fl_get()
                    for i in range(4):
                        psg = ps_get()
                        proj_fm(wg, i * 128, 128, tt, psg)
                        psp = ps_get()
                        for kk in range(4):
                            P.mm(psp[:, :], wbv[:, i, kk, :], oT[i][:, kk, sl], start=(kk == 0), stop=(kk == 3),
                                 reads=[wbr, TS(oT[i], kk, tt)], writes=[psp])
                        sg = fs_get()
                        P.act(sg[:], psg[:, :], AF.Sigmoid, reads=[psg], writes=[sg])
                        if i == 0:
                            P.tt("dve", macc[:], sg[:], psp[:, :], ALU.mult, reads=[sg, psp], writes=[macc])
                        else:
                            P.tt("dve", sg[:], sg[:], psp[:, :], ALU.mult, reads=[sg, psp], writes=[sg])
                            if i < 3:
                                P.tt("dve", macc[:], macc[:], sg[:], ALU.add, reads=[macc, sg], writes=[macc])
                            else:
                                P.tt("dve", mT[:, j, sl], macc[:], sg[:], ALU.add, reads=[macc, sg], writes=[TS(mT, j, tt)])

            if stop == "merge":
                return
            wo2 = [wb_get(), wb_get()]
            for jj in range(2):
                wload(wo2[jj][:], kpn(wo_d[l][:, jj * 512:(jj + 1) * 512]), wo2[jj])
            for tt in range(NTT):
                sl = slice(tt * TT, (tt + 1) * TT)
                half = tt % 2
                xt = _view(oT[half], 0, 4, 128, [KC, TT], F32)
                xacc = [CS(oT[half], range(4))]
                sap = kpn(xT_d[s][:, sl]) if src is None else kpn(src.t[:, sl])
                sacc = [] if src is None else [(src, [k * 4 + tt for k in range(KC)])]
                P.dma(xt, sap, reads=sacc, writes=xacc)
                for chn in range(KC):
                    wo = wo2[chn // 4]
                    j4 = chn % 4
                    ps = ps_get()
                    for k in range(KC):
                        P.mm(ps[:, :], wo[:, k, j4 * 128:(j4 + 1) * 128], mT[:, k, sl], start=(k == 0), stop=(k == KC - 1),
                             reads=[wo, TS(mT, k, tt)], writes=[ps])
                    P.stt(xt[:, chn, :], ps[:, :], gate1(chn), xt[:, chn, :], ALU.mult, ALU.add, reads=[ps, modT] + xacc, writes=xacc)
                P.dma(kpn(xb_d[:, sl]), xt, reads=xacc, writes=[(XB, [k * 4 + tt for k in range(KC)])])
                norm_tile(xt, xacc, s, l, 1, tt)

            def hid(hc, tt):
                if hc < 16:
                    return oT[hc // 4][:, hc % 4, :], [TS(oT[hc // 4], hc % 4, tt)]
                return mT[:, hc - 16, :], [TS(mT, hc - 16, tt)]

            for hp in range(NHC // 2):
                wf = wb_get()
                wload(wf[:, :, 0:256], kpn(wfi_d[l][:, hp * 256:(hp + 1) * 256]), wf)
                wload(wf[:, :, 256:512], kpn(wfi_d[l][:, D_FF + hp * 256:D_FF + (hp + 1) * 256]), wf)
                for hh in range(2):
                    hc = hp * 2 + hh
                    for tt in range(NTT):
                        hap, hacc = hid(hc, tt)
                        sl = slice(tt * TT, (tt + 1) * TT)
                        psa_ = ps_get()
                        proj_fm(wf, hh * 128, 128, tt, psa_)
                        psb2 = ps_get()
                        proj_fm(wf, 256 + hh * 128, 128, tt, psb2)
                        sa = fs_get()
                        P.act(sa[:], psa_[:, :], AF.Silu, reads=[psa_], writes=[sa])
                        P.tt("dve", hap[:, sl], sa[:], psb2[:, :], ALU.mult, reads=[sa, psb2], writes=hacc)
            for j in range(KC):
                js = slice(j * 128, (j + 1) * 128)
                wf2 = wb_get()
                w2v = wf2[:].rearrange("p k n -> p (k n)")[:, 0:NHC * 128].rearrange("p (c n) -> p c n", c=NHC)
                wload(w2v, wfo_d[l][:, js].rearrange("(c p) n -> p c n", p=128), wf2)
                for tt in range(NTT):
                    sl = slice(tt * TT, (tt + 1) * TT)
                    ps = ps_get()
                    for hc in range(NHC):
                        hap, hacc = hid(hc, tt)
                        P.mm(ps[:, :], w2v[:, hc, :], hap[:, sl], start=(hc == 0), stop=(hc == NHC - 1), reads=[wf2] + hacc, writes=[ps])
                    xin = fs_get()
                    P.dma(xin[:], xb_d[js, sl], reads=[(XB, j * 4 + tt)], writes=[xin])
                    P.stt(xin[:], ps[:, :], gate2(j), xin[:], ALU.mult, ALU.add, reads=[ps, modT, xin], writes=[xin])
                    if dst_is_out:
                        P.dma(outT_d[s][js, sl], xin[:], reads=[xin], writes=[(OUT, j * 4 + tt)], final=True)
                    else:
                        P.dma(xa_d[js, sl], xin[:], reads=[xin], writes=[(XA, j * 4 + tt)])

        for s in range(n_seq):
            for l in range(n_layers):
                layer(s, l, None if l == 0 else XA, XA, l == n_layers - 1)

        stats = P.finalize()
        P.emit()
    return nc, stats


_CACHE = {}


def _host_inputs(inp, n_seq=SEQ_PER_CORE):
    ctf, ctb, lk, lkc, rh = _const_tables()
    pp, pl = _pack_params(inp)
    shared = dict(
        pp=pp, pl=pl, ctf=ctf, ctb=ctb, lk=lk, lkc=lkc, rh=rh,
        sgwT=np.ascontiguousarray(np.asarray(inp["sg_w"], np.float32).transpose(0, 3, 1, 2)),
        ada_w=np.asarray(inp["ada_w"], np.float32), w_in=np.asarray(inp["w_in"], np.float32),
        cmp_w1=np.asarray(inp["nsa_cmp_w1"], np.float32), cmp_w2=np.asarray(inp["nsa_cmp_w2"], np.float32),
        w_branch=np.asarray(inp["w_branch"], np.float32), w_branch_gate=np.asarray(inp["w_branch_gate"], np.float32),
        w_out=np.asarray(inp["w_out"], np.float32), w_ffn_in=np.asarray(inp["w_ffn_in"], np.float32),
        w_ffn_out=np.asarray(inp["w_ffn_out"], np.float32),
    )
    x = np.asarray(inp["x"], np.float32)
    c = np.asarray(inp["c"], np.float32)
    maps = []
    for core in range(NCORES):
        b0 = core * SEQ_PER_CORE
        xs = x[b0:b0 + n_seq]
        cs = c[b0:b0 + n_seq]
        m = dict(shared)
        m["xT"] = np.ascontiguousarray(xs.transpose(0, 2, 1))
        m["cT"] = np.ascontiguousarray(cs.reshape(n_seq, KC, 128).transpose(2, 1, 0))
        maps.append(m)
    return maps


def kernel(**inputs):
    if "nc" not in _CACHE:
        _CACHE["nc"], _CACHE["stats"] = build_program()
    nc = _CACHE["nc"]
    maps = _host_inputs(inputs)
    res = run_bass_kernel_spmd(nc, maps, core_ids=list(range(NCORES)))
    out = np.empty((NCORES * SEQ_PER_CORE, T, D), np.float32)
    for core in range(NCORES):
        o = np.asarray(res.results[core]["outT"])
        out[core * SEQ_PER_CORE:(core + 1) * SEQ_PER_CORE] = o.transpose(0, 2, 1)
    return out
```

```python
import numpy as np
import concourse.bass as bass
import concourse.mybir as mybir
from concourse.bass_utils import run_bass_kernel_spmd
from contextlib import ExitStack

F32 = mybir.dt.float32
BF16 = mybir.dt.bfloat16
AF = mybir.ActivationFunctionType
ALU = mybir.AluOpType
AX = mybir.AxisListType

ENGS = ("pe", "act", "dve", "pool", "sp")
N_DMA_SEMS = 16

D = 1024
T = 2048
DEPTH = 4
NCORES = 8
SEQ_PER_CORE = 2
KC = 8
TT = 512
NTT = 4
NB = 16
D_IN = 5408
D_FF = 2816
NHC = 22
EPS = 1e-6
NEG = -30000.0
C_Z, C_XBC, C_DT = 0, 512, 1536
C_B, C_C, C_HX = 1544, 2056, 2568
C_GU, C_GV = 3080, 3592
C_Q = 4104
C_KCMP, C_VCMP, C_KSLC, C_VSLC, C_KWIN, C_VWIN = 4616, 4744, 4872, 5000, 5128, 5256
C_GATES = 5384


class Buf:
    __slots__ = ("t", "name", "nslots", "lw", "rd")

    def __init__(self, t, name, nslots=1):
        self.t = t
        self.name = name
        self.nslots = nslots
        self.lw = [None] * nslots
        self.rd = [[] for _ in range(nslots)]

    def __getitem__(self, idx):
        return self.t[idx]


class Op:
    __slots__ = ("eng", "fn", "deps", "is_dma", "sig", "sig_idx", "dsem", "dval", "idx", "waits", "prewait")

    def __init__(self, eng, fn, is_dma):
        self.eng = eng
        self.fn = fn
        self.is_dma = is_dma
        self.deps = set()
        self.sig = False
        self.sig_idx = 0
        self.dsem = None
        self.dval = 0
        self.waits = []
        self.prewait = None


class Prog:
    def __init__(self, nc, stack):
        self.nc = nc
        self.stack = stack
        self.ops = []
        self.final_dma = []
        self.nbuf = 0

    def sbuf(self, shape, dtype, name=None, nslots=1):
        self.nbuf += 1
        name = f"sb{self.nbuf}_{name or ""}"
        t = self.stack.enter_context(self.nc.sbuf_tensor(name, list(shape), dtype))
        return Buf(t, name, nslots)

    def psum(self, shape, dtype=F32, name=None, nslots=1):
        self.nbuf += 1
        name = f"ps{self.nbuf}_{name or ""}"
        t = self.stack.enter_context(self.nc.psum_tensor(name, list(shape), dtype))
        return Buf(t, name, nslots)

    @staticmethod
    def _norm(acc):
        out = []
        for a in acc:
            if a is None:
                continue
            if isinstance(a, Buf):
                out.append((a, range(a.nslots)))
            else:
                b, s = a
                if isinstance(s, int):
                    s = (s,)
                out.append((b, s))
        return out

    def op(self, eng, fn, reads=(), writes=(), dma=False, final=False):
        o = Op(eng, fn, dma)
        o.idx = len(self.ops)
        rl = self._norm(reads)
        wl = self._norm(writes)
        for b, slots in rl:
            for s in slots:
                w = b.lw[s]
                if w is not None:
                    o.deps.add(w)
        for b, slots in wl:
            for s in slots:
                w = b.lw[s]
                if w is not None:
                    o.deps.add(w)
                for r in b.rd[s]:
                    o.deps.add(r)
        for b, slots in rl:
            for s in slots:
                b.rd[s].append(o)
        for b, slots in wl:
            for s in slots:
                b.lw[s] = o
                b.rd[s] = []
        o.deps.discard(o)
        self.ops.append(o)
        if final:
            self.final_dma.append(o)
        return o

    def mm(self, out, lhsT, rhs, start=True, stop=True, reads=(), writes=(), **kw):
        return self.op("pe", lambda e: e.matmul(out, lhsT, rhs, start=start, stop=stop, **kw), reads, writes)

    def tr(self, out, in_, ident, reads=(), writes=()):
        return self.op("pe", lambda e: e.transpose(out, in_, ident), reads, writes)

    def act(self, out, in_, func, reads=(), writes=(), **kw):
        return self.op("act", lambda e: e.activation(out, in_, func, **kw), reads, writes)

    def tt(self, eng, out, in0, in1, op, reads=(), writes=()):
        return self.op(eng, lambda e: e.tensor_tensor(out, in0, in1, op), reads, writes)

    def ts(self, eng, out, in0, s1, s2, op0, op1=None, reads=(), writes=()):
        if op1 is None:
            return self.op(eng, lambda e: e.tensor_scalar(out, in0, s1, None, op0), reads, writes)
        return self.op(eng, lambda e: e.tensor_scalar(out, in0, s1, s2, op0, op1), reads, writes)

    def stt(self, out, in0, scalar, in1, op0, op1, reads=(), writes=()):
        return self.op("dve", lambda e: e.scalar_tensor_tensor(out, in0, scalar, in1, op0, op1), reads, writes)

    def copy(self, eng, out, in_, reads=(), writes=()):
        if eng == "act":
            return self.op(eng, lambda e: e.copy(out, in_), reads, writes)
        return self.op(eng, lambda e: e.tensor_copy(out, in_), reads, writes)

    def dma(self, out, in_, reads=(), writes=(), eng="sp", final=False, **kw):
        if eng == "pool":
            kw.setdefault("max_dma_last_dim", 4096)
        return self.op(eng, lambda e: e.dma_start(out, in_, **kw), reads, writes, dma=True, final=final)

    def finalize(self):
        ops = self.ops
        for o in ops:
            need = []
            for d in o.deps:
                if d.is_dma or o.is_dma:
                    need.append(d)
                elif d.eng != o.eng:
                    need.append(d)
                elif o.eng != "pe":
                    need.append(d)
            best = {}
            keep = []
            for d in need:
                if d.is_dma:
                    keep.append(d)
                else:
                    b = best.get(d.eng)
                    if b is None or d.idx > b.idx:
                        best[d.eng] = d
            keep.extend(best.values())
            o.deps = keep
            for d in keep:
                if not d.is_dma:
                    d.sig = True
        cnt = {e: 0 for e in ENGS}
        dcnt = {e: 0 for e in ENGS}
        for o in ops:
            if o.is_dma:
                i = dcnt[o.eng]
                dcnt[o.eng] += 1
                o.dsem = (o.eng, i % N_DMA_SEMS)
                o.dval = 16 * (i // N_DMA_SEMS + 1)
                if i >= N_DMA_SEMS:
                    o.prewait = (o.dsem, o.dval - 16)
            elif o.sig:
                cnt[o.eng] += 1
                o.sig_idx = cnt[o.eng]
        waited = {e: {} for e in ENGS}
        nw = 0
        for o in ops:
            w = waited[o.eng]
            req = {}
            if o.prewait is not None:
                req[o.prewait[0]] = o.prewait[1]
            for d in o.deps:
                if d.is_dma:
                    k, v = d.dsem, d.dval
                else:
                    k, v = d.eng, d.sig_idx
                if req.get(k, 0) < v:
                    req[k] = v
            for k, v in req.items():
                if w.get(k, 0) < v:
                    w[k] = v
                    o.waits.append((k, v))
                    nw += 1
        self.stats = dict(n_ops=len(ops), n_waits=nw, sig=dict(cnt), dma=dict(dcnt),
                          per_eng={e: sum(1 for o in ops if o.eng == e) for e in ENGS})
        return self.stats

    def emit(self):
        nc = self.nc
        st = self.stack
        esem = {e: st.enter_context(nc.semaphore(f"s_{e}")) for e in ENGS}
        dsem = {}
        for e in ENGS:
            if self.stats["dma"][e]:
                for j in range(N_DMA_SEMS):
                    dsem[(e, j)] = st.enter_context(nc.semaphore(f"d_{e}{j}"))

        def semof(k):
            return dsem[k] if isinstance(k, tuple) else esem[k]

        per = {e: [o for o in self.ops if o.eng == e] for e in ENGS}
        finals = self.final_dma
        block = st.enter_context(nc.Block())

        def run(eh, name):
            for o in per[name]:
                for k, v in o.waits:
                    eh.wait_ge(semof(k), v)
                ins = o.fn(eh)
                if o.is_dma:
                    ins.then_inc(dsem[o.dsem], 16)
                elif o.sig:
                    ins.then_inc(esem[name], 1)
            if name == "sp":
                for o in finals:
                    eh.wait_ge(dsem[o.dsem], o.dval)

        @block.tensor
        def _(e):
            run(e, "pe")

        @block.scalar
        def _(e):
            run(e, "act")

        @block.vector
        def _(e):
            run(e, "dve")

        @block.gpsimd
        def _(e):
            run(e, "pool")

        @block.sync
        def _(e):
            run(e, "sp")


CF_IDENT, CF_TRIU, CF_ONES, CF_M2 = 0, 128, 256, 384
NCF = 896
CB_IDENT, CB_ONES, CB_NEGM, CB_CM, CB_DM, CB_WM, CB_E, CB_OVL, CB_ONES64, CB_ZERO = 0, 128, 256, 384, 2432, 4480, 4864, 6912, 6946, 7074
NCB = 7334


def _const_tables():
    p = np.arange(128)
    ctf = np.zeros((128, NCF), np.float32)
    ctf[:, CF_IDENT:CF_IDENT + 128] = np.eye(128)
    ctf[:, CF_TRIU:CF_TRIU + 128] = (p[:, None] <= p[None, :])
    ctf[:, CF_ONES:CF_ONES + 128] = 1.0
    m1 = np.zeros((128, 16, 32), np.float32)
    m2 = np.zeros((128, 16, 32), np.float32)
    j = np.arange(32)
    for qb in range(16):
        t = qb * 128 + p
        cur = t // 64
        forced = (j[None, :] == 0) | (j[None, :] == cur[:, None]) | (j[None, :] == cur[:, None] - 1)
        future = j[None, :] > cur[:, None]
        m1[:, qb, :] = np.where(forced | future, 0.0, 1.0)
        m2[:, qb, :] = np.where(future, -1e30, np.where(forced, 1e9, 0.0))
    ctf[:, CF_M2:CF_M2 + 512] = m2.reshape(128, 512)

    ctb = np.zeros((128, NCB), np.float32)
    ctb[:, CB_IDENT:CB_IDENT + 128] = np.eye(128)
    ctb[:, CB_ONES:CB_ONES + 128] = 1.0
    ctb[:, CB_NEGM:CB_NEGM + 128] = np.where(p[None, :] < p[:, None], -10000.0, 0.0)
    tpos = np.arange(T)
    n = np.arange(128)
    ctb[:, CB_CM:CB_CM + T] = np.where(tpos[None, :] >= 16 * n[:, None] + 31, 0.0, NEG)
    dm = np.zeros((128, 4, 512), np.float32)
    tl = np.arange(512)
    for i in range(4):
        dm[:, i, :] = np.where(tl[None, :] >= 128 * i + p[:, None], 0.0, NEG)
    ctb[:, CB_DM:CB_DM + 2048] = dm.reshape(128, 2048)
    wm = np.zeros((128, 3, 128), np.float32)
    b = np.arange(128)
    wm[:, 0, :] = np.where(b[None, :] >= p[:, None], 0.0, NEG)
    wm[:, 2, :] = np.where(b[None, :] < p[:, None], 0.0, NEG)
    ctb[:, CB_WM:CB_WM + 384] = wm.reshape(128, 384)
    e = np.zeros((128, 16, 128), np.float32)
    for c in range(16):
        for key in range(128):
            e[2 * c + key // 64, c, key] = 1.0
    ctb[:, CB_E:CB_E + 2048] = e.reshape(128, 2048)
    ovl = np.zeros((128, 33), np.float32)
    cs = 16 * n
    ss = 64 * np.arange(32)
    ovl[:, 0:32] = ((cs[:, None] < ss[None, :] + 64) & (cs[:, None] + 32 > ss[None, :]))
    ovl[:, 32] = 1.0
    ctb[:, CB_OVL:CB_OVL + 33] = ovl
    ctb[0:64, CB_ONES64:CB_ONES64 + 64] = 1.0

    lk = np.zeros((4, T), np.float32)
    lk[0] = 128 * (tpos // 128)
    lk[1] = tpos % 128
    lk[2] = 1.0
    lk[3] = 1.0
    lkc = np.zeros((4, 128), np.float32)
    lkc[0] = 16 * n
    lkc[1] = 31
    lkc[2] = 1.0
    lkc[3] = 1.0
    rh = np.zeros((4, 8, T), np.float32)
    for h in range(8):
        s8 = 8.0 * 2.0 ** (-(h + 1))
        rh[0, h] = s8
        rh[1, h] = s8
        rh[2, h] = -s8 * 128 * (tpos // 128)
        rh[3, h] = -s8 * (tpos % 128)
    return ctf, ctb, lk, lkc, rh


PP_CONVW, PP_CONVB, PP_DTB, PP_ALOG, PP_DSKIP, PP_SSDNG = 0, 32, 40, 48, 56, 64
PP_SCW, PP_SGNG, PP_SGB, PP_QG, PP_KG12, PP_KG0, PP_PE = 576, 588, 1100, 1612, 1613, 1615, 1679
NPP = 1711
PL_ADAB, PL_GMIX, PL_GFFN = 0, 48, 56
NPL = 64


def _pack_params(inp):
    L = DEPTH
    rep = lambda v: np.broadcast_to(np.asarray(v, np.float32).reshape(1, -1), (128, v.size))
    pp = np.zeros((L, 128, NPP), np.float32)
    pl = np.zeros((128, L, NPL), np.float32)
    for l in range(L):
        pp[l, :, PP_CONVW:PP_CONVW + 32] = inp["ssd_conv_w"][l].reshape(4, 8, 128).transpose(2, 1, 0).reshape(128, 32)
        pp[l, :, PP_CONVB:PP_CONVB + 8] = inp["ssd_conv_b"][l].reshape(8, 128).T
        pp[l, :, PP_DTB:PP_DTB + 8] = rep(inp["ssd_dt_bias"][l])
        pp[l, :, PP_ALOG:PP_ALOG + 8] = rep(inp["ssd_a_log"][l])
        pp[l, :, PP_DSKIP:PP_DSKIP + 8] = rep(inp["ssd_d"][l])
        pp[l, :, PP_SSDNG:PP_SSDNG + 512] = rep(inp["ssd_norm_g"][l])
        pp[l, :, PP_SCW:PP_SCW + 12] = inp["sc_conv_w"][l].reshape(3, 4, 128).transpose(2, 1, 0).reshape(128, 12)
        pp[l, :, PP_SGNG:PP_SGNG + 512] = rep(inp["sg_norm_g"][l])
        pp[l, :, PP_SGB:PP_SGB + 512] = rep(inp["sg_b"][l])
        pp[l, :, PP_QG] = np.tile(inp["nsa_q_norm_g"][l], 2)
        pp[l, :, PP_KG12] = np.tile(inp["nsa_k_norm_g"][l, 1], 2)
        pp[l, :, PP_KG12 + 1] = np.tile(inp["nsa_k_norm_g"][l, 2], 2)
        pp[l, :, PP_KG0:PP_KG0 + 64] = rep(inp["nsa_k_norm_g"][l, 0])
        pp[l, :, PP_PE:PP_PE + 32] = inp["nsa_cmp_pe"][l].reshape(2, 16, 128).transpose(2, 0, 1).reshape(128, 32)
        pl[:, l, PL_ADAB:PL_ADAB + 48] = inp["ada_b"][l].reshape(48, 128).T
        pl[:, l, PL_GMIX:PL_GMIX + 8] = inp["norm_mix_g"][l].reshape(8, 128).T
        pl[:, l, PL_GFFN:PL_GFFN + 8] = inp["norm_ffn_g"][l].reshape(8, 128).T
    return pp, pl


def build_program(n_layers=DEPTH, n_seq=SEQ_PER_CORE, debug=False, stop=None):
    nc = bass.Bass("TRN2", target_bir_lowering=False)
    L = DEPTH
    din = lambda name, shape: nc.dram_tensor(name, list(shape), F32, kind="ExternalInput").ap()
    xT_d = din("xT", [n_seq, D, T])
    cT_d = din("cT", [128, KC, n_seq])
    pp_d = din("pp", [L, 128, NPP])
    pl_d = din("pl", [128, L, NPL])
    ctf_d = din("ctf", [128, NCF])
    ctb_d = din("ctb", [128, NCB])
    lk_d = din("lk", [4, T])
    lkc_d = din("lkc", [4, 128])
    rh_d = din("rh", [4, 8, T])
    sgw_d = din("sgwT", [L, 128, 4, 128])
    ada_w_d = din("ada_w", [L, D, 6 * D])
    w_in_d = din("w_in", [L, D, D_IN])
    w1_d = din("cmp_w1", [L, 2, 2048, 64])
    w2_d = din("cmp_w2", [L, 2, 64, 64])
    wb_d = din("w_branch", [L, 4, 512, D])
    wg_d = din("w_branch_gate", [L, 4, D, D])
    wo_d = din("w_out", [L, D, D])
    wfi_d = din("w_ffn_in", [L, D, 2 * D_FF])
    wfo_d = din("w_ffn_out", [L, D_FF, D])
    outT_d = nc.dram_tensor("outT", [n_seq, D, T], F32, kind="ExternalOutput").ap()
    xa_d = nc.dram_tensor("xres_a", [D, T], F32, kind="Internal").ap()
    xb_d = nc.dram_tensor("xres_b", [D, T], F32, kind="Internal").ap()
    dbg_d = None
    if debug:
        dbg_d = nc.dram_tensor("dbg", [4, 128, 4, T], F32, kind="ExternalOutput").ap()

    with ExitStack() as st:
        P = Prog(nc, st)
        XA = Buf(xa_d, "xa", 32)
        XB = Buf(xb_d, "xb", 32)
        OUT = Buf(outT_d, "out", 32)

        ctf = P.sbuf([128, NCF], F32, "ctf")
        ctb = P.sbuf([128, NCB], BF16, "ctb")
        pp = P.sbuf([128, NPP], F32, "pp")
        plb = P.sbuf([128, L, NPL], F32, "pl")
        sc = P.sbuf([128, KC, n_seq], F32, "sc")
        modT = P.sbuf([128, L, 48, n_seq], F32, "modT")
        der = P.sbuf([128, 2, 8], F32, "der")
        hT = P.sbuf([128, KC, T], BF16, "hT", nslots=32)
        oT = [P.sbuf([128, 4, T], BF16, f"oT{i}", nslots=16) for i in range(4)]
        mT = P.sbuf([128, KC, T], BF16, "mT", nslots=32)
        wbufs = [P.sbuf([128, KC, 512], BF16, f"wb{i}") for i in range(3)]
        psb = [P.psum([128, 512], F32, f"bank{i}") for i in range(8)]
        sq_b = [P.sbuf([128, 512], BF16, f"sq{i}") for i in range(2)]
        f32s = [P.sbuf([128, 512], F32, f"fs{i}") for i in range(4)]
        f32l = [P.sbuf([128, 512], F32, f"fl{i}") for i in range(2)]
        smalls = P.sbuf([128, 256], F32, "smalls", nslots=1)
        smsel = P.sbuf([128, 84], F32, "smsel")
        smcmb = P.sbuf([128, 12], F32, "smcmb")
        dtb = P.sbuf([128, NB, 8], F32, "dt")
        gsig = P.sbuf([128, NB, 24], F32, "gsig")
        kcaug = P.sbuf([128, 2, 128], BF16, "kcaug")
        nselb = P.sbuf([128, 128], BF16, "nselb")
        VC = P.sbuf([128, 2, 97], BF16, "VC")
        cbias = P.sbuf([64, 2], F32, "cbias")
        GTb = P.sbuf([64, 128], BF16, "GT")
        kcn = P.sbuf([128, 64], BF16, "kcn")
        peb = P.sbuf([128, 32], BF16, "peb")
        sgwb = P.sbuf([128, 4, 128], BF16, "sgwb")
        state = P.sbuf([128, 512], F32, "state")
        stateb = P.sbuf([128, 512], BF16, "stateb")
        tails = P.sbuf([128, 8, 3], F32, "tails")
        raw = [P.sbuf([128, 515], F32, f"raw{i}") for i in range(2)]
        zero_b = P.sbuf([1, 260], BF16, "zerob")
        negA = P.sbuf([128, 8], F32, "negA")

        identF = ctf[:, CF_IDENT:CF_IDENT + 128]
        triU = ctf[:, CF_TRIU:CF_TRIU + 128]
        onesF = ctf[:, CF_ONES:CF_ONES + 128]
        identB = ctb[:, CB_IDENT:CB_IDENT + 128]
        onesB = ctb[:, CB_ONES:CB_ONES + 128]
        NEGM = ctb[:, CB_NEGM:CB_NEGM + 128]
        ONES64 = ctb[:, CB_ONES64:CB_ONES64 + 128]
        ZEROL = ctb[:, CB_ZERO:CB_ZERO + 128]
        ZEROR = ctb[:, CB_ZERO:CB_ZERO + 260]

        def CS(buf, chunks):
            if isinstance(chunks, int):
                chunks = (chunks,)
            return (buf, [4 * c + i for c in chunks for i in range(4)])

        def TS(buf, chunk, tt):
            return (buf, 4 * chunk + tt)

        def TSA(buf, nch, tt):
            return (buf, [4 * c + tt for c in range(nch)])

        held = set()
        rr = {"ps": 0, "wb": 0, "sq": 0, "fs": 0, "raw": 0, "fl": 0}

        def ps_get(hold=False):
            for _ in range(16):
                i = rr["ps"] % 8
                rr["ps"] += 1
                if i not in held:
                    if hold:
                        held.add(i)
                    return psb[i]
            raise RuntimeError("no psum bank")

        def ps_release(b):
            held.discard(psb.index(b))

        def psbf(b):
            return b.t[:].bitcast(BF16)

        def wb_get():
            i = rr["wb"] % 3
            rr["wb"] += 1
            return wbufs[i]

        def sq_get():
            i = rr["sq"] % 2
            rr["sq"] += 1
            return sq_b[i]

        def fs_get():
            i = rr["fs"] % 4
            rr["fs"] += 1
            return f32s[i]

        def fl_get():
            i = rr["fl"] % 2
            rr["fl"] += 1
            return f32l[i]

        def wload(dst_ap, src_ap, wbuf):
            P.dma(dst_ap, src_ap, writes=[wbuf], eng="pool")

        def kpn(ap2d):
            return ap2d.rearrange("(k p) n -> p k n", p=128)

        def mt_view(s0, ns, parts, shape_tail, dtype=BF16):
            return _view(mT, s0, ns, parts, shape_tail, dtype)

        def _view(buf, s0, ns, parts, shape_tail, dtype=BF16):
            ap = buf.t[0:parts, s0:s0 + ns, :].rearrange("p a b -> p (a b)")
            if dtype == F32:
                ap = ap.bitcast(F32)
            n = int(np.prod(shape_tail))
            ap = ap[:, 0:n]
            if len(shape_tail) == 1:
                return ap
            if len(shape_tail) == 2:
                return ap.rearrange("p (a b) -> p a b", a=shape_tail[0])
            if len(shape_tail) == 3:
                return ap.rearrange("p (a b c) -> p a b c", a=shape_tail[0], b=shape_tail[1])
            raise ValueError

        P.dma(ctf[:], ctf_d, writes=[ctf])
        P.dma(ctb[:], ctb_d, writes=[ctb], eng="pool")
        P.dma(plb[:], pl_d, writes=[plb])
        P.dma(sc[:], cT_d, writes=[sc])
        P.op("dve", lambda e: e.memset(zero_b[:], 0.0), writes=[zero_b])
        P.op("dve", lambda e: e.memset(kcaug[:], 0.0), writes=[kcaug])
        P.op("dve", lambda e: e.memset(VC[:], 0.0), writes=[VC])
        P.op("dve", lambda e: e.memset(nselb[:], 0.0), writes=[nselb])
        P.dma(kcaug[64:68, 0, :], lkc_d, writes=[kcaug], eng="pool")
        P.dma(kcaug[64:68, 1, :], lkc_d, writes=[kcaug], eng="pool")
        P.copy("dve", VC[:, 0, 64:97], ctb[:, CB_OVL:CB_OVL + 33], reads=[ctb], writes=[VC])
        P.copy("dve", VC[:, 1, 64:97], ctb[:, CB_OVL:CB_OVL + 33], reads=[ctb], writes=[VC])
        P.act(sc[:], sc[:], AF.Silu, reads=[sc], writes=[sc])
        stg = [(_view(hT, 4 * i, 4, 128, [KC, 512], F32), [CS(hT, range(4 * i, 4 * i + 4))]) for i in range(2)]
        nblk = 0
        for l in range(n_layers):
            for cb in range(12):
                sv, sacc = stg[nblk % 2]
                nblk += 1
                P.dma(sv, kpn(ada_w_d[l][:, cb * 512:(cb + 1) * 512]), writes=sacc)
                ps = ps_get()
                for m in range(4):
                    for k in range(KC):
                        P.mm(ps[:, m * n_seq:(m + 1) * n_seq], sv[:, k, m * 128:(m + 1) * 128], sc[:, k, :],
                             start=(k == 0), stop=(k == KC - 1), reads=sacc + [sc], writes=[ps])
                P.tt("dve", modT[:, l, cb * 4:(cb + 1) * 4, :],
                     ps[:, 0:4 * n_seq].rearrange("p (m s) -> p m s", m=4),
                     plb[:, l, PL_ADAB + cb * 4:PL_ADAB + (cb + 1) * 4].unsqueeze(2).broadcast_to([128, 4, n_seq]),
                     ALU.add, reads=[ps, plb], writes=[modT])

        def xsrc_ap(src, s, k, tt):
            if src is XA:
                return xa_d[k * 128:(k + 1) * 128, tt * TT:(tt + 1) * TT], [(XA, k * 4 + tt)]
            if src is XB:
                return xb_d[k * 128:(k + 1) * 128, tt * TT:(tt + 1) * TT], [(XB, k * 4 + tt)]
            return xT_d[s][k * 128:(k + 1) * 128, tt * TT:(tt + 1) * TT], []

        def norm_tile(xt, xacc, s, l, which, tt):
            A = der[:, which, :]
            shift_c0 = 0 if which == 0 else 24
            sl = slice(tt * TT, (tt + 1) * TT)
            ps = ps_get()
            for k in range(KC):
                sq = sq_get()
                P.act(sq[:], xt[:, k, :], AF.Square, reads=xacc, writes=[sq])
                P.mm(ps[:], onesB, sq[:], start=(k == 0), stop=(k == KC - 1), reads=[ctb, sq], writes=[ps])
            r = fl_get()
            P.act(r[:], ps[:], AF.Sqrt, reads=[ps], writes=[r], bias=EPS, scale=1.0 / D)
            P.op("dve", lambda e, r=r: e.reciprocal(r[:], r[:]), reads=[r], writes=[r])
            for k in range(KC):
                tmp = fs_get()
                P.stt(tmp[:], xt[:, k, :], A[:, k:k + 1], r[:], ALU.mult, ALU.mult,
                      reads=xacc + [der, r], writes=[tmp])
                P.act(hT[:, k, sl], tmp[:], AF.Identity, reads=[tmp, modT], writes=[TS(hT, k, tt)],
                      bias=modT[:, l, shift_c0 + k, s:s + 1], scale=1.0)

        def norm_phase(src, s, l, which):
            for tt in range(NTT):
                sl = slice(tt * TT, (tt + 1) * TT)
                half = tt % 2
                xt = _view(mT, 4 * half, 4, 128, [KC, TT], F32)
                xacc = [CS(mT, range(4 * half, 4 * half + 4))]
                if src is None:
                    P.dma(xt, kpn(xT_d[s][:, sl]), writes=xacc)
                else:
                    dsl = [(src, [k * 4 + tt for k in range(KC)])]
                    P.dma(xt, kpn(src.t[:, sl]), reads=dsl, writes=xacc)
                norm_tile(xt, xacc, s, l, which, tt)

        def proj_fm(wbuf, wcol0, m, tt, ps, parts=128):
            sl = slice(tt * TT, (tt + 1) * TT)
            for k in range(KC):
                P.mm(ps[0:m, :], wbuf[:, k, wcol0:wcol0 + m], hT[:, k, sl], start=(k == 0), stop=(k == KC - 1),
                     reads=[wbuf, TS(hT, k, tt)], writes=[ps])

        def layer(s, l, src, dst, dst_is_out):
            ppv = lambda c0, n: pp[:, c0:c0 + n]
            P.dma(pp[:], pp_d[l], writes=[pp])
            P.stt(der[:, 0, :], modT[:, l, 8:16, s], 1.0, plb[:, l, PL_GMIX:PL_GMIX + 8], ALU.add, ALU.mult,
                  reads=[modT, plb], writes=[der])
            P.stt(der[:, 1, :], modT[:, l, 32:40, s], 1.0, plb[:, l, PL_GFFN:PL_GFFN + 8], ALU.add, ALU.mult,
                  reads=[modT, plb], writes=[der])
            gate1 = lambda k: modT[:, l, 16 + k, s:s + 1]
            gate2 = lambda k: modT[:, l, 40 + k, s:s + 1]
            P.act(negA[:], ppv(PP_ALOG, 8), AF.Exp, reads=[pp], writes=[negA])
            P.ts("dve", negA[:], negA[:], -1.0, None, ALU.mult, reads=[negA], writes=[negA])
            P.copy("dve", peb[:], ppv(PP_PE, 32), reads=[pp], writes=[peb])

            norm_phase(None if src is None else src, s, l, 0)

            if stop == "dbgh":
                for i in range(2):
                    for cc in range(4):
                        stg32 = _view(oT[0], 0, 2, 128, [T], F32)
                        P.copy("dve", stg32, hT[:, 4 * i + cc, :], reads=[CS(hT, 4 * i + cc)], writes=[CS(oT[0], (0, 1))])
                        P.dma(dbg_d[i, :, cc, :], stg32, reads=[CS(oT[0], (0, 1))], final=True)
                for cc in range(4):
                    stg32 = _view(oT[0], 0, 2, 128, [T], F32)
                    P.op("dve", lambda e, stg32=stg32: e.memset(stg32, 0.0), writes=[CS(oT[0], (0, 1))])
                    P.copy("dve", stg32[:, 0:48], modT[:, l, :, s], reads=[modT], writes=[CS(oT[0], (0, 1))])
                    P.copy("dve", stg32[:, 48:64], der[:].rearrange("p a b -> p (a b)"), reads=[der], writes=[CS(oT[0], (0, 1))])
                    P.dma(dbg_d[2, :, cc, :], stg32, reads=[CS(oT[0], (0, 1))], final=True)
                return
            if stop == "n1":
                return
            qaug = _view(mT, 0, 2, 128, [8, TT])
            QA = [CS(mT, (0, 1))]
            kaug_s = _view(mT, 2, 2, 128, [2, T])
            KS = [CS(mT, (2, 3))]
            kaug_w = _view(mT, 4, 2, 128, [2, T])
            KW = [CS(mT, (4, 5))]
            MB = _view(mT, 6, 2, 128, [2, T])
            MBA = [CS(mT, (6, 7))]
            vslc = _view(oT[0], 0, 2, 128, [NB, 2, 66])
            VS = [CS(oT[0], (0, 1))]
            vwin = _view(oT[0], 2, 2, 128, [NB, 2, 66])
            VW = [CS(oT[0], (2, 3))]
            cmpk = _view(oT[1], 0, 2, 64, [2, T])
            CK = [CS(oT[1], (0, 1))]
            cmpv = _view(oT[1], 2, 2, 64, [2, T])
            CV = [CS(oT[1], (2, 3))]
            ocomb = _view(oT[2], 0, 2, 128, [4, 8, 64], F32)
            OC = [CS(oT[2], (0, 1))]
            cmpP = _view(oT[2], 2, 1, 128, [4, TT])
            CP = [CS(oT[2], 2)]
            PTs = _view(oT[2], 3, 1, 128, [4, TT])
            PTA = [CS(oT[2], 3)]

            P.op("dve", lambda e: e.memset(kaug_s[64:128, :, :], 0.0), writes=KS)
            P.op("dve", lambda e: e.memset(kaug_w[64:128, :, :], 0.0), writes=KW)
            for g in range(2):
                P.dma(kaug_s[64:68, g, :], lk_d, writes=KS, eng="pool")
                P.dma(kaug_w[64:68, g, :], lk_d, writes=KW, eng="pool")
            P.op("dve", lambda e: e.memset(vslc[:, :, :, 64:65], 1.0), writes=VS)
            P.op("dve", lambda e: e.memset(vwin[:, :, :, 64:65], 1.0), writes=VW)

            if stop == "tm0":
                return
            wsm = wb_get()
            W = w_in_d[l]
            wload(wsm[:, :, 0:128], kpn(W[:, C_VSLC:C_VSLC + 128]), wsm)
            wload(wsm[:, :, 128:256], kpn(W[:, C_VWIN:C_VWIN + 128]), wsm)
            wload(wsm[:, :, 256:384], kpn(W[:, C_GATES - 104:C_GATES + 24]), wsm)
            wload(wsm[:, :, 384:512], kpn(W[:, C_DT - 120:C_DT + 8]), wsm)
            if stop == "tm1":
                return
            for tb in range(NB):
                bs = slice(tb * 128, (tb + 1) * 128)
                ps = ps_get()
                for k in range(KC):
                    P.mm(ps[:, 0:512], hT[:, k, bs], wsm[:, k, 0:512], start=(k == 0), stop=(k == KC - 1),
                         reads=[TS(hT, k, tb // 4), wsm], writes=[ps])
                import os as _os
                _sk = _os.environ.get("K_SKIP", "")
                if "a" not in _sk:
                    P.copy("dve", vslc[:, tb, :, 0:64], ps[:, 0:128].rearrange("p (g d) -> p g d", g=2),
                           reads=[ps], writes=VS)
                if "b" not in _sk:
                    P.copy("dve", vwin[:, tb, :, 0:64], ps[:, 128:256].rearrange("p (g d) -> p g d", g=2),
                           reads=[ps], writes=VW)
                if "c" not in _sk:
                    P.copy("dve", gsig[:, tb, :], ps[:, 360:384], reads=[ps], writes=[gsig])
                if "d" not in _sk:
                    P.tt("dve", dtb[:, tb, :], ps[:, 504:512], ppv(PP_DTB, 8), ALU.add, reads=[ps, pp], writes=[dtb])
            if stop == "tm2":
                return
            P.act(gsig[:], gsig[:], AF.Sigmoid, reads=[gsig], writes=[gsig])
            P.act(dtb[:], dtb[:], AF.Exp, reads=[dtb], writes=[dtb])
            P.act(dtb[:], dtb[:], AF.Ln, reads=[dtb], writes=[dtb], bias=1.0, scale=1.0)

            if stop == "tmsmall":
                return
            wk = wb_get()
            wload(wk[:, :, 0:128], kpn(W[:, C_KCMP:C_KCMP + 128]), wk)
            wload(wk[:, :, 128:256], kpn(W[:, C_VCMP:C_VCMP + 128]), wk)
            wload(wk[:, :, 256:384], kpn(W[:, C_KSLC:C_KSLC + 128]), wk)
            wload(wk[:, :, 384:512], kpn(W[:, C_KWIN:C_KWIN + 128]), wk)
            wx7 = wb_get()
            wload(wx7[:, :, 0:128], kpn(W[:, C_Q + 448:C_Q + 576]), wx7)
            wload(wx7[:, :, 128:256], kpn(W[:, C_KWIN + 64:C_KWIN + 192]), wx7)

            def norm64(ps, gcol, out_ap, out_acc):
                sq = sq_get()
                P.act(sq[:, :], ps[:, :], AF.Square, reads=[ps], writes=[sq])
                ps2 = ps_get()
                P.mm(ps2[:, :], ONES64, sq[:, :], reads=[ctb, sq], writes=[ps2])
                r = fs_get()
                P.act(r[0:64, :], ps2[0:64, :], AF.Sqrt, reads=[ps2], writes=[r], bias=EPS, scale=1.0 / 64)
                P.op("dve", lambda e, r=r: e.reciprocal(r[0:64, :], r[0:64, :]), reads=[r], writes=[r])
                P.stt(out_ap, ps[0:64, :], pp[0:64, gcol:gcol + 1], r[0:64, :], ALU.mult, ALU.mult,
                      reads=[ps, pp, r], writes=out_acc)

            for which in range(4):
                for g in range(2):
                    for tt in range(NTT):
                        sl = slice(tt * TT, (tt + 1) * TT)
                        ps = ps_get()
                        if which == 3 and g == 1:
                            proj_fm(wx7, 128, 128, tt, ps)
                        else:
                            proj_fm(wk, which * 128 + g * 64, 128, tt, ps)
                        if which == 0:
                            P.copy("act", cmpk[:, g, sl], ps[0:64, :], reads=[ps], writes=CK)
                        elif which == 1:
                            P.copy("act", cmpv[:, g, sl], ps[0:64, :], reads=[ps], writes=CV)
                        elif which == 2:
                            norm64(ps, PP_KG12, kaug_s[0:64, g, sl], KS)
                        else:
                            norm64(ps, PP_KG12 + 1, kaug_w[0:64, g, sl], KW)

            if stop == "kproj":
                return
            for kv in range(2):
                wcm = wb_get()
                wcf = wcm[:].rearrange("p k n -> p (k n)")
                w1b_v = wcf[0:64, 0:2048].rearrange("p (l e) -> p l e", l=32)
                w1f_v = wcf[:, 2048:3072].rearrange("p (c e) -> p c e", c=16)
                w2b_v = wcf[0:64, 3072:3136]
                w1b = w1f = w2b = wcm
                P.dma(w1b_v, w1_d[l, kv].rearrange("(l d) e -> d l e", d=64), writes=[wcm], eng="pool")
                P.dma(w1f_v, w1_d[l, kv].rearrange("(c p) e -> p c e", p=128), writes=[wcm], eng="pool")
                P.dma(w2b_v, w2_d[l, kv], writes=[wcm], eng="pool")
                psc = ps_get()
                for c in range(16):
                    P.mm(psc[0:64, 0:1], w1f_v[:, c, :], peb[:, kv * 16 + c:kv * 16 + c + 1], start=(c == 0), stop=(c == 15),
                         reads=[w1f, peb], writes=[psc])
                P.copy("dve", cbias[:, kv:kv + 1], psc[0:64, 0:1], reads=[psc], writes=[cbias])
                rawb, racc = (cmpk, CK) if kv == 0 else (cmpv, CV)
                for g in range(2):
                    ps = ps_get()
                    for li in range(32):
                        P.mm(ps[0:64, 0:127], w1b_v[:, li, :], rawb[:, g, li:li + 16 * 126 + 1:16], start=(li == 0), stop=(li == 31),
                             reads=[w1b] + racc, writes=[ps])
                    P.act(GTb[:, 0:127], ps[0:64, 0:127], AF.Gelu_apprx_tanh, reads=[ps, cbias], writes=[GTb],
                          bias=cbias[:, kv:kv + 1], scale=1.0)
                    ps2 = ps_get()
                    P.mm(ps2[0:127, 0:64], GTb[:, 0:127], w2b_v, reads=[GTb, w2b], writes=[ps2])
                    if kv == 0:
                        junk = fs_get()
                        P.act(junk[0:127, 0:64], ps2[0:127, 0:64], AF.Square, reads=[ps2], writes=[junk, smalls],
                              accum_out=smalls[0:127, 0:1])
                        P.act(smalls[0:127, 1:2], smalls[0:127, 0:1], AF.Sqrt, reads=[smalls], writes=[smalls],
                              bias=EPS, scale=1.0 / 64)
                        P.op("dve", lambda e: e.reciprocal(smalls[0:127, 2:3], smalls[0:127, 1:2]), reads=[smalls], writes=[smalls])
                        P.stt(kcn[0:127, :], ps2[0:127, 0:64], smalls[0:127, 2:3], pp[0:127, PP_KG0:PP_KG0 + 64],
                              ALU.mult, ALU.mult, reads=[ps2, smalls, pp], writes=[kcn])
                        pst = ps_get()
                        P.tr(psbf(pst)[0:64, 0:127], kcn[0:127, :], identB[0:127, 0:127], reads=[kcn, ctb], writes=[pst])
                        P.copy("dve", kcaug[0:64, g, 0:127], psbf(pst)[0:64, 0:127], reads=[pst], writes=[kcaug])
                    else:
                        P.copy("dve", VC[0:127, g, 0:64], ps2[0:127, 0:64], reads=[ps2], writes=[VC])

            if stop == "compress":
                return
            wb_get()
            wq = wb_get()
            wload(wq[:], kpn(W[:, C_Q:C_Q + 512]), wq)
            gs4 = gsig[:].rearrange("p b (h i) -> p b h i", i=3)
            m2v = ctf[:, CF_M2:CF_M2 + 512].rearrange("p (b j) -> p b j", b=16)
            Ev = ctb[:, CB_E:CB_E + 2048].rearrange("p (c k) -> p c k", c=16)
            DMv = ctb[:, CB_DM:CB_DM + 2048].rearrange("p (i t) -> p i t", i=4)
            WMv = ctb[:, CB_WM:CB_WM + 384]
            CMv = ctb[:, CB_CM:CB_CM + T]
            sm = smalls
            qaugs = [qaug, _view(oT[1], 0, 2, 128, [8, TT])]
            QAs = [QA, [CS(oT[1], (0, 1))]]
            ocombs = [ocomb, _view(oT[1], 2, 2, 128, [4, 8, 64], F32)]
            OCs = [OC, [CS(oT[1], (2, 3))]]

            for qq in range(2):
                P.op("dve", lambda e, qq=qq: e.memset(qaugs[qq][64:128, :, :], 0.0), writes=QAs[qq])

            def MBacc(g, tt_):
                return [(mT, 4 * (6 + g) + tt_)]

            def prep_steps(tt):
                sl = slice(tt * TT, (tt + 1) * TT)
                qa, QAa = qaugs[tt % 2], QAs[tt % 2]
                oc, OCa = ocombs[tt % 2], OCs[tt % 2]
                nmax = min(127, 32 * tt + 31)
                steps = []
                steps.append(lambda: P.dma(qa[64:68, :, :], rh_d[:, :, sl], writes=QAa, eng="pool"))

                def qstep(h):
                    ps = ps_get()
                    if h < 7:
                        proj_fm(wq, h * 64, 128, tt, ps)
                    else:
                        proj_fm(wx7, 0, 128, tt, ps)
                    norm64(ps, PP_QG, qa[0:64, h, :], QAa)
                for h in range(8):
                    steps.append(lambda h=h: qstep(h))

                def cstep(g, r):
                    h = 4 * g + r
                    pss = ps_get()
                    P.mm(pss[:, :], kcaug[:, g, :], qa[:, h, :], start=True, stop=False,
                         reads=[kcaug] + QAa, writes=[pss])
                    P.mm(pss[:, :], identB, CMv[:, sl], start=False, stop=True,
                         reads=[ctb], writes=[pss])
                    P.act(cmpP[:, r, :], pss[:, :], AF.Exp, reads=[pss], writes=[(oT[2], 8 + r)], scale=0.125)

                def sstep(g, qb):
                    tb = 4 * tt + qb
                    qs = slice(qb * 128, (qb + 1) * 128)
                    pso = ps_get()
                    pso_v = pso[:, 0:388].rearrange("p (r c) -> p r c", r=4)
                    for r in range(4):
                        P.mm(pso_v[:, r, :], cmpP[:, r, qs], VC[:, g, :], reads=CP + [VC], writes=[pso])
                    sp_ = smsel
                    P.ts("dve", sp_[:, 0:4], pso_v[:, :, 96], 1e-30, None, ALU.max, reads=[pso], writes=[sp_])
                    P.op("dve", lambda e: e.reciprocal(sp_[:, 4:8], sp_[:, 0:4]), reads=[sp_], writes=[sp_])
                    tmp = fs_get()
                    tv = tmp[:, 0:128].rearrange("p (r j) -> p r j", r=4)
                    P.tt("dve", tv, pso_v[:, :, 64:96], sp_[:, 4:8].unsqueeze(2).broadcast_to([128, 4, 32]), ALU.mult,
                         reads=[pso, sp_], writes=[tmp])
                    P.op("dve", lambda e, tv=tv: e.tensor_reduce(sp_[:, 8:40], tv.rearrange("p r j -> p j r"), AX.X, ALU.add),
                         reads=[tmp], writes=[sp_])
                    P.tt("dve", sp_[:, 40:44], sp_[:, 4:8], gs4[:, tb, 4 * g:4 * g + 4, 0], ALU.mult, reads=[sp_, gsig], writes=[sp_])
                    P.tt("dve", oc[:, qb, 4 * g:4 * g + 4, :], pso_v[:, :, 0:64],
                         sp_[:, 40:44].unsqueeze(2).broadcast_to([128, 4, 64]), ALU.mult, reads=[pso, sp_], writes=OCa)
                    P.stt(sp_[:, 44:76], m2v[:, tb, :], 0.0, sp_[:, 8:40], ALU.is_equal, ALU.mult, reads=[sp_, ctf], writes=[sp_])
                    P.tt("dve", sp_[:, 44:76], sp_[:, 44:76], m2v[:, tb, :], ALU.add, reads=[sp_, ctf], writes=[sp_])
                    P.op("dve", lambda e: e.max(sp_[:, 76:84], sp_[:, 44:76]), reads=[sp_], writes=[sp_])
                    nsel = nselb
                    P.ts("dve", nsel[:, 0:32], sp_[:, 44:76], sp_[:, 83:84], None, ALU.is_lt, reads=[sp_], writes=[nsel])
                    pst = ps_get()
                    P.tr(psbf(pst)[:, 0:128], nsel[:, :], identB, reads=[nsel, ctb], writes=[pst])
                    P.ts("dve", MB[:, g, tb * 128:(tb + 1) * 128], psbf(pst)[:, 0:128], NEG, None, ALU.mult,
                         reads=[pst], writes=MBacc(g, tt))
                for g in range(2):
                    for r in range(4):
                        steps.append(lambda g=g, r=r: cstep(g, r))
                    for qb in range(4):
                        steps.append(lambda g=g, qb=qb: sstep(g, qb))
                return steps

            for st_ in prep_steps(0):
                st_()
            for tt in range(NTT):
                sl = slice(tt * TT, (tt + 1) * TT)
                qaug, QA = qaugs[tt % 2], QAs[tt % 2]
                ocomb, OC = ocombs[tt % 2], OCs[tt % 2]
                nxt_steps = prep_steps(tt + 1) if tt + 1 < NTT else []
                items = []
                for h in range(8):
                    for branch in (1, 2):
                        if branch == 1:
                            chunks = list(range(0, 4 * tt + 4))
                        else:
                            chunks = list(range(max(0, 4 * tt - 2), 4 * tt + 4))
                        for ci, c in enumerate(chunks):
                            items.append((h, branch, ci, c, ci == len(chunks) - 1))
                accs = {}
                ptn = [0]

                def stage1(it):
                    h, branch, ci, c, last = it
                    g = h // 4
                    if ci == 0:
                        acc = ps_get(hold=True)
                        accs[(h, branch)] = acc
                        P.mm(acc[:, 0:260], ZEROL, ZEROR, start=True, stop=False,
                             reads=[ctb], writes=[acc], skip_group_check=True)
                    ks = slice(c * 128, (c + 1) * 128)
                    i = c - 4 * tt
                    pss = ps_get()
                    pi = ptn[0] % 4
                    ptn[0] += 1
                    pt = PTs[:, pi, :]
                    pta = [(oT[2], 12 + pi)]
                    if branch == 1:
                        P.mm(pss[:, :], kaug_s[:, g, ks], qaug[:, h, :], start=True, stop=False,
                             reads=KS + QA, writes=[pss])
                        P.mm(pss[:, :], Ev[:, c, :], MB[:, g, sl], start=False, stop=(i < 0),
                             reads=[ctb] + MBacc(g, tt), writes=[pss])
                        if i >= 0:
                            P.mm(pss[:, :], identB, DMv[:, i, :], start=False, stop=True, reads=[ctb], writes=[pss])
                        P.act(pt, pss[:, :], AF.Exp, reads=[pss], writes=pta, scale=0.125)
                        return (pt, pta, None)
                    qlo = max(i, 0)
                    qhi = min(i + 2, 3)
                    nq = qhi - qlo + 1
                    rel_lo = qlo - i
                    N = nq * 128
                    P.mm(pss[:, 0:N], kaug_w[:, g, ks], qaug[:, h, qlo * 128:(qhi + 1) * 128], start=True, stop=False,
                         reads=KW + QA, writes=[pss])
                    P.mm(pss[:, 0:N], identB, WMv[:, rel_lo * 128:(rel_lo + nq) * 128], start=False, stop=True,
                         reads=[ctb], writes=[pss])
                    P.act(pt[:, 0:N], pss[:, 0:N], AF.Exp, reads=[pss], writes=pta, scale=0.125)
                    return (pt, pta, (qlo, nq))

                def stage2(it, st1):
                    h, branch, ci, c, last = it
                    g = h // 4
                    pt, pta, wq_ = st1
                    acc = accs[(h, branch)]
                    acc_v = acc[:, 0:260].rearrange("p (q c) -> p q c", q=4)
                    i = c - 4 * tt
                    if branch == 1:
                        for qb in range(max(i, 0), 4):
                            P.mm(acc_v[:, qb, :], pt[:, qb * 128:(qb + 1) * 128], vslc[:, c, g, 0:65], start=False, stop=False,
                                 reads=pta + VS, writes=[acc], skip_group_check=True)
                    else:
                        qlo, nq = wq_
                        for qi in range(nq):
                            qb = qlo + qi
                            P.mm(acc_v[:, qb, :], pt[:, qi * 128:(qi + 1) * 128], vwin[:, c, g, 0:65], start=False, stop=False,
                                 reads=pta + VW, writes=[acc], skip_group_check=True)
                    if last:
                        o_ = 0
                        sc_ = smcmb
                        P.ts("dve", sc_[:, o_:o_ + 4], acc_v[:, :, 64], 1e-30, None, ALU.max, reads=[acc], writes=[sc_])
                        P.op("dve", lambda e: e.reciprocal(sc_[:, o_ + 4:o_ + 8], sc_[:, o_:o_ + 4]), reads=[sc_], writes=[sc_])
                        P.tt("dve", sc_[:, o_ + 8:o_ + 12], sc_[:, o_ + 4:o_ + 8], gs4[:, 4 * tt:4 * tt + 4, h, branch], ALU.mult,
                             reads=[sc_, gsig], writes=[sc_])
                        tmp = fs_get()
                        tv = tmp[:, 0:256].rearrange("p (q d) -> p q d", q=4)
                        P.tt("dve", tv, acc_v[:, :, 0:64], sc_[:, o_ + 8:o_ + 12].unsqueeze(2).broadcast_to([128, 4, 64]), ALU.mult,
                             reads=[acc, sc_], writes=[tmp])
                        P.tt("dve", ocomb[:, :, h, :], ocomb[:, :, h, :], tv, ALU.add, reads=OC + [tmp], writes=OC)
                        ps_release(acc)

                LOOK = 2
                pend = [stage1(items[q]) for q in range(min(LOOK, len(items)))]
                for ii, it in enumerate(items):
                    if ii + LOOK < len(items):
                        pend.append(stage1(items[ii + LOOK]))
                    stage2(it, pend.pop(0))
                    if nxt_steps and ii % 2 == 1:
                        nxt_steps.pop(0)()
                while nxt_steps:
                    nxt_steps.pop(0)()
                for qb in range(4):
                    tb = 4 * tt + qb
                    pst = ps_get()
                    oc2 = ocomb[:, qb, :, :].rearrange("p h d -> p (h d)")
                    for cc in range(4):
                        P.tr(pst[:, cc * 128:(cc + 1) * 128], oc2[:, cc * 128:(cc + 1) * 128], identF, reads=OC + [ctf], writes=[pst])
                    P.copy("act", oT[3][:, :, tb * 128:(tb + 1) * 128], pst[:, :].rearrange("p (c t) -> p c t", c=4),
                           reads=[pst], writes=[TSA(oT[3], 4, tt)])

            if stop == "nsa":
                return
            wz = wb_get()
            wload(wz[:], kpn(W[:, C_Z:C_Z + 512]), wz)
            wx = [wb_get(), wb_get()]
            wload(wx[0][:], kpn(W[:, C_XBC:C_XBC + 512]), wx[0])
            wload(wx[1][:], kpn(W[:, C_XBC + 512:C_XBC + 1024]), wx[1])
            xbcT = _view(mT, 0, 2, 128, [8, TT])
            XBC = [CS(mT, (0, 1))]
            dtab = _view(mT, 2, 2, 128, [8, 128], F32)
            DTA = [CS(mT, (2, 3))]
            LT = _view(mT, 4, 2, 128, [8, 128], F32)
            LTA = [CS(mT, (4, 5))]
            Mb = _view(mT, 6, 1, 128, [8, 128])
            MA = [CS(mT, 6)]
            s7 = _view(mT, 7, 1, 128, [2048])
            S7 = [CS(mT, 7)]
            xdt = s7[:, 0:512].rearrange("p (h d) -> p h d", h=8)
            xdd = s7[:, 512:1024].rearrange("p (h d) -> p h d", h=8)
            xs_sb = s7[:, 1024:1536].rearrange("p (h d) -> p h d", h=8)
            Btm = s7[:, 1536:1792]
            P.op("dve", lambda e: e.memset(state[:], 0.0), writes=[state])
            P.op("dve", lambda e: e.memset(stateb[:], 0.0), writes=[stateb])
            P.op("dve", lambda e: e.memset(tails[:], 0.0), writes=[tails])
            convw = pp[:, PP_CONVW:PP_CONVW + 32].rearrange("p (c k) -> p c k", c=8)
            dsk = pp[:, PP_DSKIP:PP_DSKIP + 8]
            for tt in range(NTT):
                for ch in range(8):
                    ps = ps_get()
                    proj_fm(wx[ch // 4], (ch % 4) * 128, 128, tt, ps)
                    rw = raw[rr["raw"] % 2]
                    rr["raw"] += 1
                    P.copy("dve", rw[:, 0:3], tails[:, ch, :], reads=[tails], writes=[rw])
                    P.copy("act", rw[:, 3:515], ps[:, :], reads=[ps], writes=[rw])
                    P.copy("dve", tails[:, ch, :], rw[:, 512:515], reads=[rw], writes=[tails])
                    acc = fs_get()
                    P.ts("dve", acc[:], rw[:, 0:512], convw[:, ch, 0:1], None, ALU.mult, reads=[rw, pp], writes=[acc])
                    for kk in range(1, 4):
                        P.stt(acc[:], rw[:, kk:kk + 512], convw[:, ch, kk:kk + 1], acc[:], ALU.mult, ALU.add,
                              reads=[rw, pp, acc], writes=[acc])
                    P.act(xbcT[:, ch, :], acc[:], AF.Silu, reads=[acc, pp], writes=XBC,
                          bias=pp[:, PP_CONVB + ch:PP_CONVB + ch + 1], scale=1.0)
                for lt in range(4):
                    tb = 4 * tt + lt
                    cs = slice(lt * 128, (lt + 1) * 128)
                    bs = slice(tb * 128, (tb + 1) * 128)
                    psx = ps_get()
                    pxb = psbf(psx)
                    for ch in range(4):
                        P.tr(pxb[:, ch * 128:(ch + 1) * 128], xbcT[:, ch, cs], identB, reads=XBC + [ctb], writes=[psx])
                    psB = ps_get()
                    pBb = psbf(psB)
                    for g in range(2):
                        P.tr(pBb[:, g * 128:(g + 1) * 128], xbcT[:, 4 + g, cs], identB, reads=XBC + [ctb], writes=[psB])
                    P.tt("dve", sm[:, 120:128], dtb[:, tb, :], negA[:], ALU.mult, reads=[dtb, negA], writes=[sm])
                    P.copy("dve", dtab, sm[:, 120:128].unsqueeze(2).broadcast_to([128, 8, 128]), reads=[sm], writes=DTA)
                    psa = ps_get()
                    P.mm(psa[:, 0:8], triU, sm[:, 120:128], reads=[ctf, sm], writes=[psa])
                    P.mm(psa[:, 8:16], onesF, sm[:, 120:128], reads=[ctf, sm], writes=[psa])
                    P.copy("dve", sm[:, 128:136], psa[:, 0:8], reads=[psa], writes=[sm])
                    P.tt("dve", sm[:, 136:144], psa[:, 8:16], sm[:, 128:136], ALU.subtract, reads=[psa, sm], writes=[sm])
                    P.copy("dve", sm[:, 144:152], psa[:, 8:16], reads=[psa], writes=[sm])
                    P.ts("dve", sm[:, 176:184], sm[:, 128:136], -1.0, None, ALU.mult, reads=[sm], writes=[sm])
                    P.act(sm[:, 152:176], sm[:, 128:152], AF.Exp, reads=[sm], writes=[sm])
                    ea = sm[:, 152:160]
                    dec = sm[:, 160:168]
                    cdec = sm[:, 168:176]
                    psl = [ps_get(), ps_get()]
                    for h in range(8):
                        pl_ = psl[h // 4]
                        o_ = pl_[:, (h % 4) * 128:(h % 4 + 1) * 128]
                        P.mm(o_, dtab[:, h, :], triU, start=True, stop=False, reads=DTA + [ctf], writes=[pl_])
                        P.mm(o_, identB, NEGM, start=False, stop=True, reads=[ctb], writes=[pl_])
                    for h in range(8):
                        pl_ = psl[h // 4]
                        o_ = pl_[:, (h % 4) * 128:(h % 4 + 1) * 128]
                        P.act(LT[:, h, :], o_, AF.Exp, reads=[pl_, sm], writes=LTA, bias=sm[:, 176 + h:177 + h], scale=1.0)
                    psc_ = ps_get()
                    for g in range(2):
                        P.mm(psc_[:, g * 128:(g + 1) * 128], xbcT[:, 4 + g, cs], xbcT[:, 6 + g, cs], reads=XBC, writes=[psc_])
                    cbT = fs_get()
                    P.copy("act", cbT[:, 0:256], psc_[:, 0:256], reads=[psc_], writes=[cbT])
                    P.tt("dve", Mb.rearrange("p (g r) l -> p g r l", g=2),
                         LT.rearrange("p (g r) l -> p g r l", g=2),
                         cbT[:, 0:256].rearrange("p (g l) -> p g l", g=2).unsqueeze(2).broadcast_to([128, 2, 4, 128]),
                         ALU.mult, reads=LTA + [cbT], writes=MA)
                    pxv = pxb[:, 0:512].rearrange("p (h d) -> p h d", h=8)
                    P.tt("dve", xdt, pxv, dtb[:, tb, :].unsqueeze(2).broadcast_to([128, 8, 64]), ALU.mult,
                         reads=[psx, dtb], writes=S7)
                    P.copy("act", Btm, pBb[:, 0:256], reads=[psB], writes=S7)
                    P.tt("dve", xdd, xdt, dec.unsqueeze(2).broadcast_to([128, 8, 64]), ALU.mult, reads=S7 + [sm], writes=S7)
                    psy = ps_get()
                    for h in range(8):
                        P.mm(psy[:, h * 64:(h + 1) * 64], Mb[:, h, :], xdt[:, h, :], reads=MA + S7, writes=[psy])
                    pso = ps_get()
                    for g in range(2):
                        P.mm(pso[:, g * 256:(g + 1) * 256], xbcT[:, 6 + g, cs], stateb[:, g * 256:(g + 1) * 256],
                             reads=XBC + [stateb], writes=[pso])
                    y1 = fs_get()
                    y1v = y1[:].rearrange("p (h d) -> p h d", h=8)
                    P.tt("dve", y1v, pso[:, :].rearrange("p (h d) -> p h d", h=8),
                         ea.unsqueeze(2).broadcast_to([128, 8, 64]), ALU.mult, reads=[pso, sm], writes=[y1])
                    P.tt("dve", y1[:], y1[:], psy[:, :], ALU.add, reads=[y1, psy], writes=[y1])
                    y2 = fs_get()
                    P.tt("dve", y2[:].rearrange("p (h d) -> p h d", h=8), pxv,
                         dsk.unsqueeze(2).broadcast_to([128, 8, 64]), ALU.mult, reads=[psx, pp], writes=[y2])
                    P.tt("dve", y1[:], y1[:], y2[:], ALU.add, reads=[y1, y2], writes=[y1])
                    pst_ = ps_get()
                    for g in range(2):
                        P.mm(pst_[:, g * 256:(g + 1) * 256], Btm[:, g * 128:(g + 1) * 128],
                             xdd[:, 4 * g:4 * g + 4, :].rearrange("p h d -> p (h d)"), reads=S7, writes=[pst_])
                    stv = state[:].rearrange("p (h d) -> p h d", h=8)
                    P.tt("dve", stv, stv, cdec.unsqueeze(2).broadcast_to([128, 8, 64]), ALU.mult, reads=[state, sm], writes=[state])
                    P.tt("dve", state[:], state[:], pst_[:, :], ALU.add, reads=[state, pst_], writes=[state])
                    P.copy("act", stateb[:], state[:], reads=[state], writes=[stateb])
                    psz = ps_get()
                    for k in range(KC):
                        P.mm(psz[:, :], hT[:, k, bs], wz[:, k, :], start=(k == 0), stop=(k == KC - 1), reads=[TS(hT, k, tt), wz], writes=[psz])
                    zs = fs_get()
                    P.act(zs[:], psz[:, :], AF.Silu, reads=[psz], writes=[zs])
                    P.tt("dve", y1[:], y1[:], zs[:], ALU.mult, reads=[y1, zs], writes=[y1])
                    for g in range(2):
                        P.act(zs[:, g * 256:(g + 1) * 256], y1[:, g * 256:(g + 1) * 256], AF.Square, reads=[y1], writes=[zs, sm],
                              accum_out=sm[:, 184 + g:185 + g])
                    P.act(sm[:, 186:188], sm[:, 184:186], AF.Sqrt, reads=[sm], writes=[sm], bias=EPS, scale=1.0 / 256)
                    P.op("dve", lambda e: e.reciprocal(sm[:, 188:190], sm[:, 186:188]), reads=[sm], writes=[sm])
                    oa = sq_get()
                    for g in range(2):
                        P.stt(oa[:, g * 256:(g + 1) * 256], y1[:, g * 256:(g + 1) * 256], sm[:, 188 + g:189 + g],
                              pp[:, PP_SSDNG + g * 256:PP_SSDNG + (g + 1) * 256], ALU.mult, ALU.mult,
                              reads=[y1, sm, pp], writes=[oa])
                    pso2 = ps_get()
                    po2 = psbf(pso2)
                    for ch in range(4):
                        P.tr(po2[:, ch * 128:(ch + 1) * 128], oa[:, ch * 128:(ch + 1) * 128], identB, reads=[oa, ctb], writes=[pso2])
                    P.copy("act", oT[0][:, :, bs], po2[:, 0:512].rearrange("p (c t) -> p c t", c=4), reads=[pso2], writes=[TSA(oT[0], 4, tt)])

            if stop == "ssd":
                return
            wB = [wb_get(), wb_get(), wb_get()]
            for i_, c0 in enumerate((C_B, C_C, C_HX)):
                wload(wB[i_][:], kpn(W[:, c0:c0 + 512]), wB[i_])
            chx = _view(mT, 0, 3, 128, [2050], F32)
            CHX = [CS(mT, (0, 1, 2))]
            accB = _view(mT, 3, 2, 128, [2048], F32)
            ACB = [CS(mT, (3, 4))]
            bsb = _view(mT, 5, 1, 128, [2048])
            BSB = [CS(mT, 5)]
            P.op("dve", lambda e: e.memset(chx[:, 0:2], 0.0), writes=CHX)
            scw = pp[:, PP_SCW:PP_SCW + 12].rearrange("p (c k) -> p c k", c=4)
            for cc in range(4):
                for tt in range(NTT):
                    sl = slice(tt * TT, (tt + 1) * TT)
                    psb_ = ps_get()
                    proj_fm(wB[0], cc * 128, 128, tt, psb_)
                    psc_ = ps_get()
                    proj_fm(wB[1], cc * 128, 128, tt, psc_)
                    psh = ps_get()
                    proj_fm(wB[2], cc * 128, 128, tt, psh)
                    P.copy("act", bsb[:, sl], psb_[:, :], reads=[psb_], writes=BSB)
                    hx = fs_get()
                    P.copy("act", hx[:], psh[:, :], reads=[psh], writes=[hx])
                    P.tt("dve", chx[:, 2 + tt * TT:2 + (tt + 1) * TT], psc_[:, :], hx[:], ALU.mult, reads=[psc_, hx], writes=CHX)
                P.ts("dve", accB, chx[:, 0:2048], scw[:, cc, 0:1], None, ALU.mult, reads=CHX + [pp], writes=ACB)
                P.stt(accB, chx[:, 1:2049], scw[:, cc, 1:2], accB, ALU.mult, ALU.add, reads=CHX + [pp] + ACB, writes=ACB)
                P.stt(accB, chx[:, 2:2050], scw[:, cc, 2:3], accB, ALU.mult, ALU.add, reads=CHX + [pp] + ACB, writes=ACB)
                P.tt("dve", oT[1][:, cc, :], accB, bsb, ALU.mult, reads=ACB + BSB, writes=[CS(oT[1], cc)])

            if stop == "sconv":
                return
            wC = [wb_get(), wb_get()]
            wload(wC[0][:], kpn(W[:, C_GU:C_GU + 512]), wC[0])
            wload(wC[1][:], kpn(W[:, C_GV:C_GV + 512]), wC[1])
            sgw32 = fs_get()
            sgv = sgw32[:].rearrange("p (g t) -> p g t", g=4)
            P.dma(sgv, sgw_d[l], writes=[sgw32])
            P.tt("dve", sgwb[:], sgv, triU.unsqueeze(1).broadcast_to([128, 4, 128]), ALU.mult,
                 reads=[sgw32, ctf], writes=[sgwb])
            for cc in range(4):
                for tt in range(NTT):
                    sl = slice(tt * TT, (tt + 1) * TT)
                    ps = ps_get()
                    proj_fm(wC[0], cc * 128, 128, tt, ps)
                    P.act(oT[2][:, cc, sl], ps[:, :], AF.Gelu_apprx_tanh, reads=[ps], writes=[TS(oT[2], cc, tt)])
            sgb = pp[:, PP_SGB:PP_SGB + 512]
            for tb in range(NB):
                bs = slice(tb * 128, (tb + 1) * 128)
                ps = ps_get()
                for k in range(KC):
                    P.mm(ps[:, :], hT[:, k, bs], wC[1][:, k, :], start=(k == 0), stop=(k == KC - 1), reads=[TS(hT, k, tb // 4), wC[1]], writes=[ps])
                vg = fs_get()
                P.act(vg[:], ps[:, :], AF.Gelu_apprx_tanh, reads=[ps], writes=[vg])
                junk = fs_get()
                P.act(junk[:], vg[:], AF.Square, reads=[vg], writes=[junk, sm], accum_out=sm[:, 192:193])
                P.act(sm[:, 193:194], sm[:, 192:193], AF.Sqrt, reads=[sm], writes=[sm], bias=EPS, scale=1.0 / 512)
                P.op("dve", lambda e: e.reciprocal(sm[:, 194:195], sm[:, 193:194]), reads=[sm], writes=[sm])
                vn = sq_get()
                P.stt(vn[:], vg[:], sm[:, 194:195], pp[:, PP_SGNG:PP_SGNG + 512], ALU.mult, ALU.mult, reads=[vg, sm, pp], writes=[vn])
                ps2 = ps_get()
                for g in range(4):
                    P.mm(ps2[:, g * 128:(g + 1) * 128], vn[:, g * 128:(g + 1) * 128], sgwb[:, g, :], reads=[vn, sgwb], writes=[ps2])
                t1 = fs_get()
                P.tt("dve", t1[:], ps2[:, :], sgb, ALU.add, reads=[ps2, pp], writes=[t1])
                P.tt("dve", oT[2][:, :, bs], t1[:].rearrange("p (g t) -> p g t", g=4), oT[2][:, :, bs], ALU.mult,
                     reads=[t1, TSA(oT[2], 4, tb // 4)], writes=[TSA(oT[2], 4, tb // 4)])

            if debug and l == 0 and s == 0:
                for i in range(4):
                    for cc in range(4):
                        stg32 = _view(mT, 0, 2, 128, [T], F32)
                        P.copy("dve", stg32, oT[i][:, cc, :], reads=[CS(oT[i], cc)], writes=[CS(mT, (0, 1))])
                        P.dma(dbg_d[i, :, cc, :], stg32, reads=[CS(mT, (0, 1))], final=True)

            if stop == "sgate":
                return
            for j in range(KC):
                js = slice(j * 128, (j + 1) * 128)
                wg = wb_get()
                for i in range(4):
                    wload(wg[:, :, i * 128:(i + 1) * 128], kpn(wg_d[l, i][:, js]), wg)
                wbr = wb_get()
                wbv = wbr[:, 0:4, :].rearrange("p k (i n) -> p i k n", i=4)
                for i in range(4):
                    wload(wbv[:, i, :, :], kpn(wb_d[l, i][:, js]), wbr)
                for tt in range(NTT):
                    sl = slice(tt * TT, (tt + 1) * TT)
                    macc = fl_get()
                    for i in range(4):
                        psg = ps_get()
                        proj_fm(wg, i * 128, 128, tt, psg)
                        psp = ps_get()
                        for kk in range(4):
                            P.mm(psp[:, :], wbv[:, i, kk, :], oT[i][:, kk, sl], start=(kk == 0), stop=(kk == 3),
                                 reads=[wbr, TS(oT[i], kk, tt)], writes=[psp])
                        sg = fs_get()
                        P.act(sg[:], psg[:, :], AF.Sigmoid, reads=[psg], writes=[sg])
                        if i == 0:
                            P.tt("dve", macc[:], sg[:], psp[:, :], ALU.mult, reads=[sg, psp], writes=[macc])
                        else:
                            P.tt("dve", sg[:], sg[:], psp[:, :], ALU.mult, reads=[sg, psp], writes=[sg])
                            if i < 3:
                                P.tt("dve", macc[:], macc[:], sg[:], ALU.add, reads=[macc, sg], writes=[macc])
                            else:
                                P.tt("dve", mT[:, j, sl], macc[:], sg[:], ALU.add, reads=[macc, sg], writes=[TS(mT, j, tt)])

            if stop == "merge":
                return
            wo2 = [wb_get(), wb_get()]
            for jj in range(2):
                wload(wo2[jj][:], kpn(wo_d[l][:, jj * 512:(jj + 1) * 512]), wo2[jj])
            for tt in range(NTT):
                sl = slice(tt * TT, (tt + 1) * TT)
                half = tt % 2
                xt = _view(oT[half], 0, 4, 128, [KC, TT], F32)
                xacc = [CS(oT[half], range(4))]
                sap = kpn(xT_d[s][:, sl]) if src is None else kpn(src.t[:, sl])
                sacc = [] if src is None else [(src, [k * 4 + tt for k in range(KC)])]
                P.dma(xt, sap, reads=sacc, writes=xacc)
                for chn in range(KC):
                    wo = wo2[chn // 4]
                    j4 = chn % 4
                    ps = ps_get()
                    for k in range(KC):
                        P.mm(ps[:, :], wo[:, k, j4 * 128:(j4 + 1) * 128], mT[:, k, sl], start=(k == 0), stop=(k == KC - 1),
                             reads=[wo, TS(mT, k, tt)], writes=[ps])
                    P.stt(xt[:, chn, :], ps[:, :], gate1(chn), xt[:, chn, :], ALU.mult, ALU.add, reads=[ps, modT] + xacc, writes=xacc)
                P.dma(kpn(xb_d[:, sl]), xt, reads=xacc, writes=[(XB, [k * 4 + tt for k in range(KC)])])
                norm_tile(xt, xacc, s, l, 1, tt)

            def hid(hc, tt):
                if hc < 16:
                    return oT[hc // 4][:, hc % 4, :], [TS(oT[hc // 4], hc % 4, tt)]
                return mT[:, hc - 16, :], [TS(mT, hc - 16, tt)]

            for hp in range(NHC // 2):
                wf = wb_get()
                wload(wf[:, :, 0:256], kpn(wfi_d[l][:, hp * 256:(hp + 1) * 256]), wf)
                wload(wf[:, :, 256:512], kpn(wfi_d[l][:, D_FF + hp * 256:D_FF + (hp + 1) * 256]), wf)
                for hh in range(2):
                    hc = hp * 2 + hh
                    for tt in range(NTT):
                        hap, hacc = hid(hc, tt)
                        sl = slice(tt * TT, (tt + 1) * TT)
                        psa_ = ps_get()
                        proj_fm(wf, hh * 128, 128, tt, psa_)
                        psb2 = ps_get()
                        proj_fm(wf, 256 + hh * 128, 128, tt, psb2)
                        sa = fs_get()
                        P.act(sa[:], psa_[:, :], AF.Silu, reads=[psa_], writes=[sa])
                        P.tt("dve", hap[:, sl], sa[:], psb2[:, :], ALU.mult, reads=[sa, psb2], writes=hacc)
            for j in range(KC):
                js = slice(j * 128, (j + 1) * 128)
                wf2 = wb_get()
                w2v = wf2[:].rearrange("p k n -> p (k n)")[:, 0:NHC * 128].rearrange("p (c n) -> p c n", c=NHC)
                wload(w2v, wfo_d[l][:, js].rearrange("(c p) n -> p c n", p=128), wf2)
                for tt in range(NTT):
                    sl = slice(tt * TT, (tt + 1) * TT)
                    ps = ps_get()
                    for hc in range(NHC):
                        hap, hacc = hid(hc, tt)
                        P.mm(ps[:, :], w2v[:, hc, :], hap[:, sl], start=(hc == 0), stop=(hc == NHC - 1), reads=[wf2] + hacc, writes=[ps])
                    xin = fs_get()
                    P.dma(xin[:], xb_d[js, sl], reads=[(XB, j * 4 + tt)], writes=[xin])
                    P.stt(xin[:], ps[:, :], gate2(j), xin[:], ALU.mult, ALU.add, reads=[ps, modT, xin], writes=[xin])
                    if dst_is_out:
                        P.dma(outT_d[s][js, sl], xin[:], reads=[xin], writes=[(OUT, j * 4 + tt)], final=True)
                    else:
                        P.dma(xa_d[js, sl], xin[:], reads=[xin], writes=[(XA, j * 4 + tt)])

        for s in range(n_seq):
            for l in range(n_layers):
                layer(s, l, None if l == 0 else XA, XA, l == n_layers - 1)

        stats = P.finalize()
        P.emit()
    return nc, stats


_CACHE = {}


def _host_inputs(inp, n_seq=SEQ_PER_CORE):
    ctf, ctb, lk, lkc, rh = _const_tables()
    pp, pl = _pack_params(inp)
    shared = dict(
        pp=pp, pl=pl, ctf=ctf, ctb=ctb, lk=lk, lkc=lkc, rh=rh,
        sgwT=np.ascontiguousarray(np.asarray(inp["sg_w"], np.float32).transpose(0, 3, 1, 2)),
        ada_w=np.asarray(inp["ada_w"], np.float32), w_in=np.asarray(inp["w_in"], np.float32),
        cmp_w1=np.asarray(inp["nsa_cmp_w1"], np.float32), cmp_w2=np.asarray(inp["nsa_cmp_w2"], np.float32),
        w_branch=np.asarray(inp["w_branch"], np.float32), w_branch_gate=np.asarray(inp["w_branch_gate"], np.float32),
        w_out=np.asarray(inp["w_out"], np.float32), w_ffn_in=np.asarray(inp["w_ffn_in"], np.float32),
        w_ffn_out=np.asarray(inp["w_ffn_out"], np.float32),
    )
    x = np.asarray(inp["x"], np.float32)
    c = np.asarray(inp["c"], np.float32)
    maps = []
    for core in range(NCORES):
        b0 = core * SEQ_PER_CORE
        xs = x[b0:b0 + n_seq]
        cs = c[b0:b0 + n_seq]
        m = dict(shared)
        m["xT"] = np.ascontiguousarray(xs.transpose(0, 2, 1))
        m["cT"] = np.ascontiguousarray(cs.reshape(n_seq, KC, 128).transpose(2, 1, 0))
        maps.append(m)
    return maps


def kernel(**inputs):
    if "nc" not in _CACHE:
        _CACHE["nc"], _CACHE["stats"] = build_program()
    nc = _CACHE["nc"]
    maps = _host_inputs(inputs)
    res = run_bass_kernel_spmd(nc, maps, core_ids=list(range(NCORES)))
    out = np.empty((NCORES * SEQ_PER_CORE, T, D), np.float32)
    for core in range(NCORES):
        o = np.asarray(res.results[core]["outT"])
        out[core * SEQ_PER_CORE:(core + 1) * SEQ_PER_CORE] = o.transpose(0, 2, 1)
    return out
```

```python
import numpy as np
import concourse.bass as bass
import concourse.mybir as mybir
from concourse.bass_utils import run_bass_kernel_spmd
from contextlib import ExitStack

F32 = mybir.dt.float32
BF16 = mybir.dt.bfloat16
AF = mybir.ActivationFunctionType
ALU = mybir.AluOpType
AX = mybir.AxisListType

ENGS = ("pe", "act", "dve", "pool", "sp")
N_DMA_SEMS = 16

D = 1024
T = 2048
DEPTH = 4
NCORES = 8
SEQ_PER_CORE = 2
KC = 8
TT = 512
NTT = 4
NB = 16
D_IN = 5408
D_FF = 2816
NHC = 22
EPS = 1e-6
NEG = -30000.0
C_Z, C_XBC, C_DT = 0, 512, 1536
C_B, C_C, C_HX = 1544, 2056, 2568
C_GU, C_GV = 3080, 3592
C_Q = 4104
C_KCMP, C_VCMP, C_KSLC, C_VSLC, C_KWIN, C_VWIN = 4616, 4744, 4872, 5000, 5128, 5256
C_GATES = 5384


class Buf:
    __slots__ = ("t", "name", "nslots", "lw", "rd")

    def __init__(self, t, name, nslots=1):
        self.t = t
        self.name = name
        self.nslots = nslots
        self.lw = [None] * nslots
        self.rd = [[] for _ in range(nslots)]

    def __getitem__(self, idx):
        return self.t[idx]


class Op:
    __slots__ = ("eng", "fn", "deps", "is_dma", "sig", "sig_idx", "dsem", "dval", "idx", "waits", "prewait")

    def __init__(self, eng, fn, is_dma):
        self.eng = eng
        self.fn = fn
        self.is_dma = is_dma
        self.deps = set()
        self.sig = False
        self.sig_idx = 0
        self.dsem = None
        self.dval = 0
        self.waits = []
        self.prewait = None


class Prog:
    def __init__(self, nc, stack):
        self.nc = nc
        self.stack = stack
        self.ops = []
        self.final_dma = []
        self.nbuf = 0

    def sbuf(self, shape, dtype, name=None, nslots=1):
        self.nbuf += 1
        name = f"sb{self.nbuf}_{name or ""}"
        t = self.stack.enter_context(self.nc.sbuf_tensor(name, list(shape), dtype))
        return Buf(t, name, nslots)

    def psum(self, shape, dtype=F32, name=None, nslots=1):
        self.nbuf += 1
        name = f"ps{self.nbuf}_{name or ""}"
        t = self.stack.enter_context(self.nc.psum_tensor(name, list(shape), dtype))
        return Buf(t, name, nslots)

    @staticmethod
    def _norm(acc):
        out = []
        for a in acc:
            if a is None:
                continue
            if isinstance(a, Buf):
                out.append((a, range(a.nslots)))
            else:
                b, s = a
                if isinstance(s, int):
                    s = (s,)
                out.append((b, s))
        return out

    def op(self, eng, fn, reads=(), writes=(), dma=False, final=False):
        o = Op(eng, fn, dma)
        o.idx = len(self.ops)
        rl = self._norm(reads)
        wl = self._norm(writes)
        for b, slots in rl:
            for s in slots:
                w = b.lw[s]
                if w is not None:
                    o.deps.add(w)
        for b, slots in wl:
            for s in slots:
                w = b.lw[s]
                if w is not None:
                    o.deps.add(w)
                for r in b.rd[s]:
                    o.deps.add(r)
        for b, slots in rl:
            for s in slots:
                b.rd[s].append(o)
        for b, slots in wl:
            for s in slots:
                b.lw[s] = o
                b.rd[s] = []
        o.deps.discard(o)
        self.ops.append(o)
        if final:
            self.final_dma.append(o)
        return o

    def mm(self, out, lhsT, rhs, start=True, stop=True, reads=(), writes=(), **kw):
        return self.op("pe", lambda e: e.matmul(out, lhsT, rhs, start=start, stop=stop, **kw), reads, writes)

    def tr(self, out, in_, ident, reads=(), writes=()):
        return self.op("pe", lambda e: e.transpose(out, in_, ident), reads, writes)

    def act(self, out, in_, func, reads=(), writes=(), **kw):
        return self.op("act", lambda e: e.activation(out, in_, func, **kw), reads, writes)

    def tt(self, eng, out, in0, in1, op, reads=(), writes=()):
        return self.op(eng, lambda e: e.tensor_tensor(out, in0, in1, op), reads, writes)

    def ts(self, eng, out, in0, s1, s2, op0, op1=None, reads=(), writes=()):
        if op1 is None:
            return self.op(eng, lambda e: e.tensor_scalar(out, in0, s1, None, op0), reads, writes)
        return self.op(eng, lambda e: e.tensor_scalar(out, in0, s1, s2, op0, op1), reads, writes)

    def stt(self, out, in0, scalar, in1, op0, op1, reads=(), writes=()):
        return self.op("dve", lambda e: e.scalar_tensor_tensor(out, in0, scalar, in1, op0, op1), reads, writes)

    def copy(self, eng, out, in_, reads=(), writes=()):
        if eng == "act":
            return self.op(eng, lambda e: e.copy(out, in_), reads, writes)
        return self.op(eng, lambda e: e.tensor_copy(out, in_), reads, writes)

    def dma(self, out, in_, reads=(), writes=(), eng="sp", final=False, **kw):
        if eng == "pool":
            kw.setdefault("max_dma_last_dim", 4096)
        return self.op(eng, lambda e: e.dma_start(out, in_, **kw), reads, writes, dma=True, final=final)

    def finalize(self):
        ops = self.ops
        for o in ops:
            need = []
            for d in o.deps:
                if d.is_dma or o.is_dma:
                    need.append(d)
                elif d.eng != o.eng:
                    need.append(d)
                elif o.eng != "pe":
                    need.append(d)
            best = {}
            keep = []
            for d in need:
                if d.is_dma:
                    keep.append(d)
                else:
                    b = best.get(d.eng)
                    if b is None or d.idx > b.idx:
                        best[d.eng] = d
            keep.extend(best.values())
            o.deps = keep
            for d in keep:
                if not d.is_dma:
                    d.sig = True
        cnt = {e: 0 for e in ENGS}
        dcnt = {e: 0 for e in ENGS}
        for o in ops:
            if o.is_dma:
                i = dcnt[o.eng]
                dcnt[o.eng] += 1
                o.dsem = (o.eng, i % N_DMA_SEMS)
                o.dval = 16 * (i // N_DMA_SEMS + 1)
                if i >= N_DMA_SEMS:
                    o.prewait = (o.dsem, o.dval - 16)
            elif o.sig:
                cnt[o.eng] += 1
                o.sig_idx = cnt[o.eng]
        waited = {e: {} for e in ENGS}
        nw = 0
        for o in ops:
            w = waited[o.eng]
            req = {}
            if o.prewait is not None:
                req[o.prewait[0]] = o.prewait[1]
            for d in o.deps:
                if d.is_dma:
                    k, v = d.dsem, d.dval
                else:
                    k, v = d.eng, d.sig_idx
                if req.get(k, 0) < v:
                    req[k] = v
            for k, v in req.items():
                if w.get(k, 0) < v:
                    w[k] = v
                    o.waits.append((k, v))
                    nw += 1
        self.stats = dict(n_ops=len(ops), n_waits=nw, sig=dict(cnt), dma=dict(dcnt),
                          per_eng={e: sum(1 for o in ops if o.eng == e) for e in ENGS})
        return self.stats

    def emit(self):
        nc = self.nc
        st = self.stack
        esem = {e: st.enter_context(nc.semaphore(f"s_{e}")) for e in ENGS}
        dsem = {}
        for e in ENGS:
            if self.stats["dma"][e]:
                for j in range(N_DMA_SEMS):
                    dsem[(e, j)] = st.enter_context(nc.semaphore(f"d_{e}{j}"))

        def semof(k):
            return dsem[k] if isinstance(k, tuple) else esem[k]

        per = {e: [o for o in self.ops if o.eng == e] for e in ENGS}
        finals = self.final_dma
        block = st.enter_context(nc.Block())

        def run(eh, name):
            for o in per[name]:
                for k, v in o.waits:
                    eh.wait_ge(semof(k), v)
                ins = o.fn(eh)
                if o.is_dma:
                    ins.then_inc(dsem[o.dsem], 16)
                elif o.sig:
                    ins.then_inc(esem[name], 1)
            if name == "sp":
                for o in finals:
                    eh.wait_ge(dsem[o.dsem], o.dval)

        @block.tensor
        def _(e):
            run(e, "pe")

        @block.scalar
        def _(e):
            run(e, "act")

        @block.vector
        def _(e):
            run(e, "dve")

        @block.gpsimd
        def _(e):
            run(e, "pool")

        @block.sync
        def _(e):
            run(e, "sp")


CF_IDENT, CF_TRIU, CF_ONES, CF_M2 = 0, 128, 256, 384
NCF = 896
CB_IDENT, CB_ONES, CB_NEGM, CB_CM, CB_DM, CB_WM, CB_E, CB_OVL, CB_ONES64, CB_ZERO = 0, 128, 256, 384, 2432, 4480, 4864, 6912, 6946, 7074
NCB = 7334


def _const_tables():
    p = np.arange(128)
    ctf = np.zeros((128, NCF), np.float32)
    ctf[:, CF_IDENT:CF_IDENT + 128] = np.eye(128)
    ctf[:, CF_TRIU:CF_TRIU + 128] = (p[:, None] <= p[None, :])
    ctf[:, CF_ONES:CF_ONES + 128] = 1.0
    m1 = np.zeros((128, 16, 32), np.float32)
    m2 = np.zeros((128, 16, 32), np.float32)
    j = np.arange(32)
    for qb in range(16):
        t = qb * 128 + p
        cur = t // 64
        forced = (j[None, :] == 0) | (j[None, :] == cur[:, None]) | (j[None, :] == cur[:, None] - 1)
        future = j[None, :] > cur[:, None]
        m1[:, qb, :] = np.where(forced | future, 0.0, 1.0)
        m2[:, qb, :] = np.where(future, -1e30, np.where(forced, 1e9, 0.0))
    ctf[:, CF_M2:CF_M2 + 512] = m2.reshape(128, 512)

    ctb = np.zeros((128, NCB), np.float32)
    ctb[:, CB_IDENT:CB_IDENT + 128] = np.eye(128)
    ctb[:, CB_ONES:CB_ONES + 128] = 1.0
    ctb[:, CB_NEGM:CB_NEGM + 128] = np.where(p[None, :] < p[:, None], -10000.0, 0.0)
    tpos = np.arange(T)
    n = np.arange(128)
    ctb[:, CB_CM:CB_CM + T] = np.where(tpos[None, :] >= 16 * n[:, None] + 31, 0.0, NEG)
    dm = np.zeros((128, 4, 512), np.float32)
    tl = np.arange(512)
    for i in range(4):
        dm[:, i, :] = np.where(tl[None, :] >= 128 * i + p[:, None], 0.0, NEG)
    ctb[:, CB_DM:CB_DM + 2048] = dm.reshape(128, 2048)
    wm = np.zeros((128, 3, 128), np.float32)
    b = np.arange(128)
    wm[:, 0, :] = np.where(b[None, :] >= p[:, None], 0.0, NEG)
    wm[:, 2, :] = np.where(b[None, :] < p[:, None], 0.0, NEG)
    ctb[:, CB_WM:CB_WM + 384] = wm.reshape(128, 384)
    e = np.zeros((128, 16, 128), np.float32)
    for c in range(16):
        for key in range(128):
            e[2 * c + key // 64, c, key] = 1.0
    ctb[:, CB_E:CB_E + 2048] = e.reshape(128, 2048)
    ovl = np.zeros((128, 33), np.float32)
    cs = 16 * n
    ss = 64 * np.arange(32)
    ovl[:, 0:32] = ((cs[:, None] < ss[None, :] + 64) & (cs[:, None] + 32 > ss[None, :]))
    ovl[:, 32] = 1.0
    ctb[:, CB_OVL:CB_OVL + 33] = ovl
    ctb[0:64, CB_ONES64:CB_ONES64 + 64] = 1.0

    lk = np.zeros((4, T), np.float32)
    lk[0] = 128 * (tpos // 128)
    lk[1] = tpos % 128
    lk[2] = 1.0
    lk[3] = 1.0
    lkc = np.zeros((4, 128), np.float32)
    lkc[0] = 16 * n
    lkc[1] = 31
    lkc[2] = 1.0
    lkc[3] = 1.0
    rh = np.zeros((4, 8, T), np.float32)
    for h in range(8):
        s8 = 8.0 * 2.0 ** (-(h + 1))
        rh[0, h] = s8
        rh[1, h] = s8
        rh[2, h] = -s8 * 128 * (tpos // 128)
        rh[3, h] = -s8 * (tpos % 128)
    return ctf, ctb, lk, lkc, rh


PP_CONVW, PP_CONVB, PP_DTB, PP_ALOG, PP_DSKIP, PP_SSDNG = 0, 32, 40, 48, 56, 64
PP_SCW, PP_SGNG, PP_SGB, PP_QG, PP_KG12, PP_KG0, PP_PE = 576, 588, 1100, 1612, 1613, 1615, 1679
NPP = 1711
PL_ADAB, PL_GMIX, PL_GFFN = 0, 48, 56
NPL = 64


def _pack_params(inp):
    L = DEPTH
    rep = lambda v: np.broadcast_to(np.asarray(v, np.float32).reshape(1, -1), (128, v.size))
    pp = np.zeros((L, 128, NPP), np.float32)
    pl = np.zeros((128, L, NPL), np.float32)
    for l in range(L):
        pp[l, :, PP_CONVW:PP_CONVW + 32] = inp["ssd_conv_w"][l].reshape(4, 8, 128).transpose(2, 1, 0).reshape(128, 32)
        pp[l, :, PP_CONVB:PP_CONVB + 8] = inp["ssd_conv_b"][l].reshape(8, 128).T
        pp[l, :, PP_DTB:PP_DTB + 8] = rep(inp["ssd_dt_bias"][l])
        pp[l, :, PP_ALOG:PP_ALOG + 8] = rep(inp["ssd_a_log"][l])
        pp[l, :, PP_DSKIP:PP_DSKIP + 8] = rep(inp["ssd_d"][l])
        pp[l, :, PP_SSDNG:PP_SSDNG + 512] = rep(inp["ssd_norm_g"][l])
        pp[l, :, PP_SCW:PP_SCW + 12] = inp["sc_conv_w"][l].reshape(3, 4, 128).transpose(2, 1, 0).reshape(128, 12)
        pp[l, :, PP_SGNG:PP_SGNG + 512] = rep(inp["sg_norm_g"][l])
        pp[l, :, PP_SGB:PP_SGB + 512] = rep(inp["sg_b"][l])
        pp[l, :, PP_QG] = np.tile(inp["nsa_q_norm_g"][l], 2)
        pp[l, :, PP_KG12] = np.tile(inp["nsa_k_norm_g"][l, 1], 2)
        pp[l, :, PP_KG12 + 1] = np.tile(inp["nsa_k_norm_g"][l, 2], 2)
        pp[l, :, PP_KG0:PP_KG0 + 64] = rep(inp["nsa_k_norm_g"][l, 0])
        pp[l, :, PP_PE:PP_PE + 32] = inp["nsa_cmp_pe"][l].reshape(2, 16, 128).transpose(2, 0, 1).reshape(128, 32)
        pl[:, l, PL_ADAB:PL_ADAB + 48] = inp["ada_b"][l].reshape(48, 128).T
        pl[:, l, PL_GMIX:PL_GMIX + 8] = inp["norm_mix_g"][l].reshape(8, 128).T
        pl[:, l, PL_GFFN:PL_GFFN + 8] = inp["norm_ffn_g"][l].reshape(8, 128).T
    return pp, pl


def build_program(n_layers=DEPTH, n_seq=SEQ_PER_CORE, debug=False, stop=None):
    nc = bass.Bass("TRN2", target_bir_lowering=False)
    L = DEPTH
    din = lambda name, shape: nc.dram_tensor(name, list(shape), F32, kind="ExternalInput").ap()
    xT_d = din("xT", [n_seq, D, T])
    cT_d = din("cT", [128, KC, n_seq])
    pp_d = din("pp", [L, 128, NPP])
    pl_d = din("pl", [128, L, NPL])
    ctf_d = din("ctf", [128, NCF])
    ctb_d = din("ctb", [128, NCB])
    lk_d = din("lk", [4, T])
    lkc_d = din("lkc", [4, 128])
    rh_d = din("rh", [4, 8, T])
    sgw_d = din("sgwT", [L, 128, 4, 128])
    ada_w_d = din("ada_w", [L, D, 6 * D])
    w_in_d = din("w_in", [L, D, D_IN])
    w1_d = din("cmp_w1", [L, 2, 2048, 64])
    w2_d = din("cmp_w2", [L, 2, 64, 64])
    wb_d = din("w_branch", [L, 4, 512, D])
    wg_d = din("w_branch_gate", [L, 4, D, D])
    wo_d = din("w_out", [L, D, D])
    wfi_d = din("w_ffn_in", [L, D, 2 * D_FF])
    wfo_d = din("w_ffn_out", [L, D_FF, D])
    outT_d = nc.dram_tensor("outT", [n_seq, D, T], F32, kind="ExternalOutput").ap()
    xa_d = nc.dram_tensor("xres_a", [D, T], F32, kind="Internal").ap()
    xb_d = nc.dram_tensor("xres_b", [D, T], F32, kind="Internal").ap()
    dbg_d = None
    if debug:
        dbg_d = nc.dram_tensor("dbg", [4, 128, 4, T], F32, kind="ExternalOutput").ap()

    with ExitStack() as st:
        P = Prog(nc, st)
        XA = Buf(xa_d, "xa", 32)
        XB = Buf(xb_d, "xb", 32)
        OUT = Buf(outT_d, "out", 32)

        ctf = P.sbuf([128, NCF], F32, "ctf")
        ctb = P.sbuf([128, NCB], BF16, "ctb")
        pp = P.sbuf([128, NPP], F32, "pp")
        plb = P.sbuf([128, L, NPL], F32, "pl")
        sc = P.sbuf([128, KC, n_seq], F32, "sc")
        modT = P.sbuf([128, L, 48, n_seq], F32, "modT")
        der = P.sbuf([128, 2, 8], F32, "der")
        hT = P.sbuf([128, KC, T], BF16, "hT", nslots=32)
        oT = [P.sbuf([128, 4, T], BF16, f"oT{i}", nslots=16) for i in range(4)]
        mT = P.sbuf([128, KC, T], BF16, "mT", nslots=32)
        wbufs = [P.sbuf([128, KC, 512], BF16, f"wb{i}") for i in range(3)]
        psb = [P.psum([128, 512], F32, f"bank{i}") for i in range(8)]
        sq_b = [P.sbuf([128, 512], BF16, f"sq{i}") for i in range(2)]
        f32s = [P.sbuf([128, 512], F32, f"fs{i}") for i in range(4)]
        f32l = [P.sbuf([128, 512], F32, f"fl{i}") for i in range(2)]
        smalls = P.sbuf([128, 256], F32, "smalls", nslots=1)
        smsel = P.sbuf([128, 84], F32, "smsel")
        smcmb = P.sbuf([128, 12], F32, "smcmb")
        dtb = P.sbuf([128, NB, 8], F32, "dt")
        gsig = P.sbuf([128, NB, 24], F32, "gsig")
        kcaug = P.sbuf([128, 2, 128], BF16, "kcaug")
        nselb = P.sbuf([128, 128], BF16, "nselb")
        VC = P.sbuf([128, 2, 97], BF16, "VC")
        cbias = P.sbuf([64, 2], F32, "cbias")
        GTb = P.sbuf([64, 128], BF16, "GT")
        kcn = P.sbuf([128, 64], BF16, "kcn")
        peb = P.sbuf([128, 32], BF16, "peb")
        sgwb = P.sbuf([128, 4, 128], BF16, "sgwb")
        state = P.sbuf([128, 512], F32, "state")
        stateb = P.sbuf([128, 512], BF16, "stateb")
        tails = P.sbuf([128, 8, 3], F32, "tails")
        raw = [P.sbuf([128, 515], F32, f"raw{i}") for i in range(2)]
        zero_b = P.sbuf([1, 260], BF16, "zerob")
        negA = P.sbuf([128, 8], F32, "negA")

        identF = ctf[:, CF_IDENT:CF_IDENT + 128]
        triU = ctf[:, CF_TRIU:CF_TRIU + 128]
        onesF = ctf[:, CF_ONES:CF_ONES + 128]
        identB = ctb[:, CB_IDENT:CB_IDENT + 128]
        onesB = ctb[:, CB_ONES:CB_ONES + 128]
        NEGM = ctb[:, CB_NEGM:CB_NEGM + 128]
        ONES64 = ctb[:, CB_ONES64:CB_ONES64 + 128]
        ZEROL = ctb[:, CB_ZERO:CB_ZERO + 128]
        ZEROR = ctb[:, CB_ZERO:CB_ZERO + 260]

        def CS(buf, chunks):
            if isinstance(chunks, int):
                chunks = (chunks,)
            return (buf, [4 * c + i for c in chunks for i in range(4)])

        def TS(buf, chunk, tt):
            return (buf, 4 * chunk + tt)

        def TSA(buf, nch, tt):
            return (buf, [4 * c + tt for c in range(nch)])

        held = set()
        rr = {"ps": 0, "wb": 0, "sq": 0, "fs": 0, "raw": 0, "fl": 0}

        def ps_get(hold=False):
            for _ in range(16):
                i = rr["ps"] % 8
                rr["ps"] += 1
                if i not in held:
                    if hold:
                        held.add(i)
                    return psb[i]
            raise RuntimeError("no psum bank")

        def ps_release(b):
            held.discard(psb.index(b))

        def psbf(b):
            return b.t[:].bitcast(BF16)

        def wb_get():
            i = rr["wb"] % 3
            rr["wb"] += 1
            return wbufs[i]

        def sq_get():
            i = rr["sq"] % 2
            rr["sq"] += 1
            return sq_b[i]

        def fs_get():
            i = rr["fs"] % 4
            rr["fs"] += 1
            return f32s[i]

        def fl_get():
            i = rr["fl"] % 2
            rr["fl"] += 1
            return f32l[i]

        def wload(dst_ap, src_ap, wbuf):
            P.dma(dst_ap, src_ap, writes=[wbuf], eng="pool")

        def kpn(ap2d):
            return ap2d.rearrange("(k p) n -> p k n", p=128)

        def mt_view(s0, ns, parts, shape_tail, dtype=BF16):
            return _view(mT, s0, ns, parts, shape_tail, dtype)

        def _view(buf, s0, ns, parts, shape_tail, dtype=BF16):
            ap = buf.t[0:parts, s0:s0 + ns, :].rearrange("p a b -> p (a b)")
            if dtype == F32:
                ap = ap.bitcast(F32)
            n = int(np.prod(shape_tail))
            ap = ap[:, 0:n]
            if len(shape_tail) == 1:
                return ap
            if len(shape_tail) == 2:
                return ap.rearrange("p (a b) -> p a b", a=shape_tail[0])
            if len(shape_tail) == 3:
                return ap.rearrange("p (a b c) -> p a b c", a=shape_tail[0], b=shape_tail[1])
            raise ValueError

        P.dma(ctf[:], ctf_d, writes=[ctf])
        P.dma(ctb[:], ctb_d, writes=[ctb], eng="pool")
        P.dma(plb[:], pl_d, writes=[plb])
        P.dma(sc[:], cT_d, writes=[sc])
        P.op("dve", lambda e: e.memset(zero_b[:], 0.0), writes=[zero_b])
        P.op("dve", lambda e: e.memset(kcaug[:], 0.0), writes=[kcaug])
        P.op("dve", lambda e: e.memset(VC[:], 0.0), writes=[VC])
        P.op("dve", lambda e: e.memset(nselb[:], 0.0), writes=[nselb])
        P.dma(kcaug[64:68, 0, :], lkc_d, writes=[kcaug], eng="pool")
        P.dma(kcaug[64:68, 1, :], lkc_d, writes=[kcaug], eng="pool")
        P.copy("dve", VC[:, 0, 64:97], ctb[:, CB_OVL:CB_OVL + 33], reads=[ctb], writes=[VC])
        P.copy("dve", VC[:, 1, 64:97], ctb[:, CB_OVL:CB_OVL + 33], reads=[ctb], writes=[VC])
        P.act(sc[:], sc[:], AF.Silu, reads=[sc], writes=[sc])
        stg = [(_view(hT, 4 * i, 4, 128, [KC, 512], F32), [CS(hT, range(4 * i, 4 * i + 4))]) for i in range(2)]
        nblk = 0
        for l in range(n_layers):
            for cb in range(12):
                sv, sacc = stg[nblk % 2]
                nblk += 1
                P.dma(sv, kpn(ada_w_d[l][:, cb * 512:(cb + 1) * 512]), writes=sacc)
                ps = ps_get()
                for m in range(4):
                    for k in range(KC):
                        P.mm(ps[:, m * n_seq:(m + 1) * n_seq], sv[:, k, m * 128:(m + 1) * 128], sc[:, k, :],
                             start=(k == 0), stop=(k == KC - 1), reads=sacc + [sc], writes=[ps])
                P.tt("dve", modT[:, l, cb * 4:(cb + 1) * 4, :],
                     ps[:, 0:4 * n_seq].rearrange("p (m s) -> p m s", m=4),
                     plb[:, l, PL_ADAB + cb * 4:PL_ADAB + (cb + 1) * 4].unsqueeze(2).broadcast_to([128, 4, n_seq]),
                     ALU.add, reads=[ps, plb], writes=[modT])

        def xsrc_ap(src, s, k, tt):
            if src is XA:
                return xa_d[k * 128:(k + 1) * 128, tt * TT:(tt + 1) * TT], [(XA, k * 4 + tt)]
            if src is XB:
                return xb_d[k * 128:(k + 1) * 128, tt * TT:(tt + 1) * TT], [(XB, k * 4 + tt)]
            return xT_d[s][k * 128:(k + 1) * 128, tt * TT:(tt + 1) * TT], []

        def norm_tile(xt, xacc, s, l, which, tt):
            A = der[:, which, :]
            shift_c0 = 0 if which == 0 else 24
            sl = slice(tt * TT, (tt + 1) * TT)
            ps = ps_get()
            for k in range(KC):
                sq = sq_get()
                P.act(sq[:], xt[:, k, :], AF.Square, reads=xacc, writes=[sq])
                P.mm(ps[:], onesB, sq[:], start=(k == 0), stop=(k == KC - 1), reads=[ctb, sq], writes=[ps])
            r = fl_get()
            P.act(r[:], ps[:], AF.Sqrt, reads=[ps], writes=[r], bias=EPS, scale=1.0 / D)
            P.op("dve", lambda e, r=r: e.reciprocal(r[:], r[:]), reads=[r], writes=[r])
            for k in range(KC):
                tmp = fs_get()
                P.stt(tmp[:], xt[:, k, :], A[:, k:k + 1], r[:], ALU.mult, ALU.mult,
                      reads=xacc + [der, r], writes=[tmp])
                P.act(hT[:, k, sl], tmp[:], AF.Identity, reads=[tmp, modT], writes=[TS(hT, k, tt)],
                      bias=modT[:, l, shift_c0 + k, s:s + 1], scale=1.0)

        def norm_phase(src, s, l, which, after_tile=None, stage=None):
            for tt in range(NTT):
                sl = slice(tt * TT, (tt + 1) * TT)
                half = tt % 2
                if stage is None:
                    xt = _view(mT, 4 * half, 4, 128, [KC, TT], F32)
                    xacc = [CS(mT, range(4 * half, 4 * half + 4))]
                else:
                    xt = _view(stage[half], 0, 4, 128, [KC, TT], F32)
                    xacc = [CS(stage[half], range(4))]
                if src is None:
                    P.dma(xt, kpn(xT_d[s][:, sl]), writes=xacc)
                else:
                    dsl = [(src, [k * 4 + tt for k in range(KC)])]
                    P.dma(xt, kpn(src.t[:, sl]), reads=dsl, writes=xacc)
                norm_tile(xt, xacc, s, l, which, tt)
                if after_tile is not None:
                    after_tile(tt)

        def proj_fm(wbuf, wcol0, m, tt, ps, parts=128):
            sl = slice(tt * TT, (tt + 1) * TT)
            for k in range(KC):
                P.mm(ps[0:m, :], wbuf[:, k, wcol0:wcol0 + m], hT[:, k, sl], start=(k == 0), stop=(k == KC - 1),
                     reads=[wbuf, TS(hT, k, tt)], writes=[ps])

        def layer(s, l, src, dst, dst_is_out):
            ppv = lambda c0, n: pp[:, c0:c0 + n]
            P.dma(pp[:], pp_d[l], writes=[pp])
            P.stt(der[:, 0, :], modT[:, l, 8:16, s], 1.0, plb[:, l, PL_GMIX:PL_GMIX + 8], ALU.add, ALU.mult,
                  reads=[modT, plb], writes=[der])
            P.stt(der[:, 1, :], modT[:, l, 32:40, s], 1.0, plb[:, l, PL_GFFN:PL_GFFN + 8], ALU.add, ALU.mult,
                  reads=[modT, plb], writes=[der])
            gate1 = lambda k: modT[:, l, 16 + k, s:s + 1]
            gate2 = lambda k: modT[:, l, 40 + k, s:s + 1]
            P.act(negA[:], ppv(PP_ALOG, 8), AF.Exp, reads=[pp], writes=[negA])
            P.ts("dve", negA[:], negA[:], -1.0, None, ALU.mult, reads=[negA], writes=[negA])
            P.copy("dve", peb[:], ppv(PP_PE, 32), reads=[pp], writes=[peb])

            qaug = _view(mT, 0, 2, 128, [8, TT])
            QA = [CS(mT, (0, 1))]
            kaug_s = _view(mT, 2, 2, 128, [2, T])
            KS = [CS(mT, (2, 3))]
            kaug_w = _view(mT, 4, 2, 128, [2, T])
            KW = [CS(mT, (4, 5))]
            MB = _view(mT, 6, 2, 128, [2, T])
            MBA = [CS(mT, (6, 7))]
            vslc = _view(oT[0], 0, 2, 128, [NB, 2, 66])
            VS = [CS(oT[0], (0, 1))]
            vwin = _view(oT[0], 2, 2, 128, [NB, 2, 66])
            VW = [CS(oT[0], (2, 3))]
            cmpk = _view(oT[1], 0, 2, 64, [2, T])
            CK = [CS(oT[1], (0, 1))]
            cmpv = _view(oT[1], 2, 2, 64, [2, T])
            CV = [CS(oT[1], (2, 3))]
            ocomb = _view(oT[2], 0, 2, 128, [4, 8, 64], F32)
            OC = [CS(oT[2], (0, 1))]
            cmpP = _view(oT[2], 2, 1, 128, [4, TT])
            CP = [CS(oT[2], 2)]
            PTs = _view(oT[2], 3, 1, 128, [4, TT])
            PTA = [CS(oT[2], 3)]

            P.op("dve", lambda e: e.memset(kaug_s[64:128, :, :], 0.0), writes=KS)
            P.op("dve", lambda e: e.memset(kaug_w[64:128, :, :], 0.0), writes=KW)
            for g in range(2):
                P.dma(kaug_s[64:68, g, :], lk_d, writes=KS, eng="pool")
                P.dma(kaug_w[64:68, g, :], lk_d, writes=KW, eng="pool")
            P.op("dve", lambda e: e.memset(vslc[:, :, :, 64:65], 1.0), writes=VS)
            P.op("dve", lambda e: e.memset(vwin[:, :, :, 64:65], 1.0), writes=VW)

            if stop == "tm0":
                return
            wsm = wb_get()
            W = w_in_d[l]
            wload(wsm[:, :, 0:128], kpn(W[:, C_VSLC:C_VSLC + 128]), wsm)
            wload(wsm[:, :, 128:256], kpn(W[:, C_VWIN:C_VWIN + 128]), wsm)
            wload(wsm[:, :, 256:384], kpn(W[:, C_GATES - 104:C_GATES + 24]), wsm)
            wload(wsm[:, :, 384:512], kpn(W[:, C_DT - 120:C_DT + 8]), wsm)
            def tm_tile(tt):
                for tb in range(4 * tt, 4 * tt + 4):
                    bs = slice(tb * 128, (tb + 1) * 128)
                    ps = ps_get()
                    for k in range(KC):
                        P.mm(ps[:, 0:512], hT[:, k, bs], wsm[:, k, 0:512], start=(k == 0), stop=(k == KC - 1),
                             reads=[TS(hT, k, tb // 4), wsm], writes=[ps])
                    import os as _os
                    _sk = _os.environ.get("K_SKIP", "")
                    if "a" not in _sk:
                        P.copy("dve", vslc[:, tb, :, 0:64], ps[:, 0:128].rearrange("p (g d) -> p g d", g=2),
                               reads=[ps], writes=VS)
                    if "b" not in _sk:
                        P.copy("dve", vwin[:, tb, :, 0:64], ps[:, 128:256].rearrange("p (g d) -> p g d", g=2),
                               reads=[ps], writes=VW)
                    if "c" not in _sk:
                        P.copy("dve", gsig[:, tb, :], ps[:, 360:384], reads=[ps], writes=[gsig])
                    if "d" not in _sk:
                        P.tt("dve", dtb[:, tb, :], ps[:, 504:512], ppv(PP_DTB, 8), ALU.add, reads=[ps, pp], writes=[dtb])

            if stop == "tmsmall":
                return
            wk = wb_get()
            wload(wk[:, :, 0:128], kpn(W[:, C_KCMP:C_KCMP + 128]), wk)
            wload(wk[:, :, 128:256], kpn(W[:, C_VCMP:C_VCMP + 128]), wk)
            wload(wk[:, :, 256:384], kpn(W[:, C_KSLC:C_KSLC + 128]), wk)
            wload(wk[:, :, 384:512], kpn(W[:, C_KWIN:C_KWIN + 128]), wk)
            wx7 = wb_get()
            wload(wx7[:, :, 0:128], kpn(W[:, C_Q + 448:C_Q + 576]), wx7)
            wload(wx7[:, :, 128:256], kpn(W[:, C_KWIN + 64:C_KWIN + 192]), wx7)

            def norm64(ps, gcol, out_ap, out_acc):
                sq = sq_get()
                P.act(sq[:, :], ps[:, :], AF.Square, reads=[ps], writes=[sq])
                ps2 = ps_get()
                P.mm(ps2[:, :], ONES64, sq[:, :], reads=[ctb, sq], writes=[ps2])
                r = fs_get()
                P.act(r[0:64, :], ps2[0:64, :], AF.Sqrt, reads=[ps2], writes=[r], bias=EPS, scale=1.0 / 64)
                P.op("dve", lambda e, r=r: e.reciprocal(r[0:64, :], r[0:64, :]), reads=[r], writes=[r])
                P.stt(out_ap, ps[0:64, :], pp[0:64, gcol:gcol + 1], r[0:64, :], ALU.mult, ALU.mult,
                      reads=[ps, pp, r], writes=out_acc)

            def kproj_tile(tt):
                for which in range(4):
                    for g in range(2):
                        sl = slice(tt * TT, (tt + 1) * TT)
                        ps = ps_get()
                        if which == 3 and g == 1:
                            proj_fm(wx7, 128, 128, tt, ps)
                        else:
                            proj_fm(wk, which * 128 + g * 64, 128, tt, ps)
                        if which == 0:
                            P.copy("act", cmpk[:, g, sl], ps[0:64, :], reads=[ps], writes=CK)
                        elif which == 1:
                            P.copy("act", cmpv[:, g, sl], ps[0:64, :], reads=[ps], writes=CV)
                        elif which == 2:
                            norm64(ps, PP_KG12, kaug_s[0:64, g, sl], KS)
                        else:
                            norm64(ps, PP_KG12 + 1, kaug_w[0:64, g, sl], KW)

            def after_n1(tt):
                tm_tile(tt)
                kproj_tile(tt)
            norm_phase(None if src is None else src, s, l, 0, after_tile=after_n1, stage=(oT[2], oT[3]))
            P.act(gsig[:], gsig[:], AF.Sigmoid, reads=[gsig], writes=[gsig])
            P.act(dtb[:], dtb[:], AF.Exp, reads=[dtb], writes=[dtb])
            P.act(dtb[:], dtb[:], AF.Ln, reads=[dtb], writes=[dtb], bias=1.0, scale=1.0)
            for kv in range(2):
                wcm = wb_get()
                wcf = wcm[:].rearrange("p k n -> p (k n)")
                w1b_v = wcf[0:64, 0:2048].rearrange("p (l e) -> p l e", l=32)
                w1f_v = wcf[:, 2048:3072].rearrange("p (c e) -> p c e", c=16)
                w2b_v = wcf[0:64, 3072:3136]
                w1b = w1f = w2b = wcm
                P.dma(w1b_v, w1_d[l, kv].rearrange("(l d) e -> d l e", d=64), writes=[wcm], eng="pool")
                P.dma(w1f_v, w1_d[l, kv].rearrange("(c p) e -> p c e", p=128), writes=[wcm], eng="pool")
                P.dma(w2b_v, w2_d[l, kv], writes=[wcm], eng="pool")
                psc = ps_get()
                for c in range(16):
                    P.mm(psc[0:64, 0:1], w1f_v[:, c, :], peb[:, kv * 16 + c:kv * 16 + c + 1], start=(c == 0), stop=(c == 15),
                         reads=[w1f, peb], writes=[psc])
                P.copy("dve", cbias[:, kv:kv + 1], psc[0:64, 0:1], reads=[psc], writes=[cbias])
                rawb, racc = (cmpk, CK) if kv == 0 else (cmpv, CV)
                for g in range(2):
                    ps = ps_get()
                    for li in range(32):
                        P.mm(ps[0:64, 0:127], w1b_v[:, li, :], rawb[:, g, li:li + 16 * 126 + 1:16], start=(li == 0), stop=(li == 31),
                             reads=[w1b] + racc, writes=[ps])
                    P.act(GTb[:, 0:127], ps[0:64, 0:127], AF.Gelu_apprx_tanh, reads=[ps, cbias], writes=[GTb],
                          bias=cbias[:, kv:kv + 1], scale=1.0)
                    ps2 = ps_get()
                    P.mm(ps2[0:127, 0:64], GTb[:, 0:127], w2b_v, reads=[GTb, w2b], writes=[ps2])
                    if kv == 0:
                        junk = fs_get()
                        P.act(junk[0:127, 0:64], ps2[0:127, 0:64], AF.Square, reads=[ps2], writes=[junk, smalls],
                              accum_out=smalls[0:127, 0:1])
                        P.act(smalls[0:127, 1:2], smalls[0:127, 0:1], AF.Sqrt, reads=[smalls], writes=[smalls],
                              bias=EPS, scale=1.0 / 64)
                        P.op("dve", lambda e: e.reciprocal(smalls[0:127, 2:3], smalls[0:127, 1:2]), reads=[smalls], writes=[smalls])
                        P.stt(kcn[0:127, :], ps2[0:127, 0:64], smalls[0:127, 2:3], pp[0:127, PP_KG0:PP_KG0 + 64],
                              ALU.mult, ALU.mult, reads=[ps2, smalls, pp], writes=[kcn])
                        pst = ps_get()
                        P.tr(psbf(pst)[0:64, 0:127], kcn[0:127, :], identB[0:127, 0:127], reads=[kcn, ctb], writes=[pst])
                        P.copy("dve", kcaug[0:64, g, 0:127], psbf(pst)[0:64, 0:127], reads=[pst], writes=[kcaug])
                    else:
                        P.copy("dve", VC[0:127, g, 0:64], ps2[0:127, 0:64], reads=[ps2], writes=[VC])

            if stop == "compress":
                return
            wb_get()
            wq = wb_get()
            wload(wq[:], kpn(W[:, C_Q:C_Q + 512]), wq)
            gs4 = gsig[:].rearrange("p b (h i) -> p b h i", i=3)
            m2v = ctf[:, CF_M2:CF_M2 + 512].rearrange("p (b j) -> p b j", b=16)
            Ev = ctb[:, CB_E:CB_E + 2048].rearrange("p (c k) -> p c k", c=16)
            DMv = ctb[:, CB_DM:CB_DM + 2048].rearrange("p (i t) -> p i t", i=4)
            WMv = ctb[:, CB_WM:CB_WM + 384]
            CMv = ctb[:, CB_CM:CB_CM + T]
            sm = smalls
            qaugs = [qaug, _view(oT[1], 0, 2, 128, [8, TT])]
            QAs = [QA, [CS(oT[1], (0, 1))]]
            ocombs = [ocomb, _view(oT[1], 2, 2, 128, [4, 8, 64], F32)]
            OCs = [OC, [CS(oT[1], (2, 3))]]

            for qq in range(2):
                P.op("dve", lambda e, qq=qq: e.memset(qaugs[qq][64:128, :, :], 0.0), writes=QAs[qq])

            def MBacc(g, tt_):
                return [(mT, 4 * (6 + g) + tt_)]

            def prep_steps(tt):
                sl = slice(tt * TT, (tt + 1) * TT)
                qa, QAa = qaugs[tt % 2], QAs[tt % 2]
                oc, OCa = ocombs[tt % 2], OCs[tt % 2]
                nmax = min(127, 32 * tt + 31)
                steps = []
                steps.append(lambda: P.dma(qa[64:68, :, :], rh_d[:, :, sl], writes=QAa, eng="pool"))

                def qstep(h):
                    ps = ps_get()
                    if h < 7:
                        proj_fm(wq, h * 64, 128, tt, ps)
                    else:
                        proj_fm(wx7, 0, 128, tt, ps)
                    norm64(ps, PP_QG, qa[0:64, h, :], QAa)
                for h in range(8):
                    steps.append(lambda h=h: qstep(h))

                def cstep(g, r):
                    h = 4 * g + r
                    pss = ps_get()
                    P.mm(pss[:, :], kcaug[:, g, :], qa[:, h, :], start=True, stop=False,
                         reads=[kcaug] + QAa, writes=[pss])
                    P.mm(pss[:, :], identB, CMv[:, sl], start=False, stop=True,
                         reads=[ctb], writes=[pss])
                    P.act(cmpP[:, r, :], pss[:, :], AF.Exp, reads=[pss], writes=[(oT[2], 8 + r)], scale=0.125)

                def sstep(g, qb):
                    tb = 4 * tt + qb
                    qs = slice(qb * 128, (qb + 1) * 128)
                    pso = ps_get()
                    pso_v = pso[:, 0:388].rearrange("p (r c) -> p r c", r=4)
                    for r in range(4):
                        P.mm(pso_v[:, r, :], cmpP[:, r, qs], VC[:, g, :], reads=CP + [VC], writes=[pso])
                    sp_ = smsel
                    P.ts("dve", sp_[:, 0:4], pso_v[:, :, 96], 1e-30, None, ALU.max, reads=[pso], writes=[sp_])
                    P.op("dve", lambda e: e.reciprocal(sp_[:, 4:8], sp_[:, 0:4]), reads=[sp_], writes=[sp_])
                    tmp = fs_get()
                    tv = tmp[:, 0:128].rearrange("p (r j) -> p r j", r=4)
                    P.tt("dve", tv, pso_v[:, :, 64:96], sp_[:, 4:8].unsqueeze(2).broadcast_to([128, 4, 32]), ALU.mult,
                         reads=[pso, sp_], writes=[tmp])
                    P.op("dve", lambda e, tv=tv: e.tensor_reduce(sp_[:, 8:40], tv.rearrange("p r j -> p j r"), AX.X, ALU.add),
                         reads=[tmp], writes=[sp_])
                    P.tt("dve", sp_[:, 40:44], sp_[:, 4:8], gs4[:, tb, 4 * g:4 * g + 4, 0], ALU.mult, reads=[sp_, gsig], writes=[sp_])
                    P.tt("dve", oc[:, qb, 4 * g:4 * g + 4, :], pso_v[:, :, 0:64],
                         sp_[:, 40:44].unsqueeze(2).broadcast_to([128, 4, 64]), ALU.mult, reads=[pso, sp_], writes=OCa)
                    P.stt(sp_[:, 44:76], m2v[:, tb, :], 0.0, sp_[:, 8:40], ALU.is_equal, ALU.mult, reads=[sp_, ctf], writes=[sp_])
                    P.tt("dve", sp_[:, 44:76], sp_[:, 44:76], m2v[:, tb, :], ALU.add, reads=[sp_, ctf], writes=[sp_])
                    P.op("dve", lambda e: e.max(sp_[:, 76:84], sp_[:, 44:76]), reads=[sp_], writes=[sp_])
                    nsel = nselb
                    P.ts("dve", nsel[:, 0:32], sp_[:, 44:76], sp_[:, 83:84], None, ALU.is_lt, reads=[sp_], writes=[nsel])
                    pst = ps_get()
                    P.tr(psbf(pst)[:, 0:128], nsel[:, :], identB, reads=[nsel, ctb], writes=[pst])
                    P.ts("dve", MB[:, g, tb * 128:(tb + 1) * 128], psbf(pst)[:, 0:128], NEG, None, ALU.mult,
                         reads=[pst], writes=MBacc(g, tt))
                for g in range(2):
                    for r in range(4):
                        steps.append(lambda g=g, r=r: cstep(g, r))
                    for qb in range(4):
                        steps.append(lambda g=g, qb=qb: sstep(g, qb))
                return steps

            for st_ in prep_steps(0):
                st_()
            for tt in range(NTT):
                sl = slice(tt * TT, (tt + 1) * TT)
                qaug, QA = qaugs[tt % 2], QAs[tt % 2]
                ocomb, OC = ocombs[tt % 2], OCs[tt % 2]
                nxt_steps = prep_steps(tt + 1) if tt + 1 < NTT else []
                items = []
                for h in range(8):
                    for branch in (1, 2):
                        if branch == 1:
                            chunks = list(range(0, 4 * tt + 4))
                        else:
                            chunks = list(range(max(0, 4 * tt - 2), 4 * tt + 4))
                        for ci, c in enumerate(chunks):
                            items.append((h, branch, ci, c, ci == len(chunks) - 1))
                accs = {}
                ptn = [0]

                def stage1(it):
                    h, branch, ci, c, last = it
                    g = h // 4
                    if ci == 0:
                        acc = ps_get(hold=True)
                        accs[(h, branch)] = acc
                        P.mm(acc[:, 0:260], ZEROL, ZEROR, start=True, stop=False,
                             reads=[ctb], writes=[acc], skip_group_check=True)
                    ks = slice(c * 128, (c + 1) * 128)
                    i = c - 4 * tt
                    pss = ps_get()
                    pi = ptn[0] % 4
                    ptn[0] += 1
                    pt = PTs[:, pi, :]
                    pta = [(oT[2], 12 + pi)]
                    if branch == 1:
                        P.mm(pss[:, :], kaug_s[:, g, ks], qaug[:, h, :], start=True, stop=False,
                             reads=KS + QA, writes=[pss])
                        P.mm(pss[:, :], Ev[:, c, :], MB[:, g, sl], start=False, stop=(i < 0),
                             reads=[ctb] + MBacc(g, tt), writes=[pss])
                        if i >= 0:
                            P.mm(pss[:, :], identB, DMv[:, i, :], start=False, stop=True, reads=[ctb], writes=[pss])
                        P.act(pt, pss[:, :], AF.Exp, reads=[pss], writes=pta, scale=0.125)
                        return (pt, pta, None)
                    qlo = max(i, 0)
                    qhi = min(i + 2, 3)
                    nq = qhi - qlo + 1
                    rel_lo = qlo - i
                    N = nq * 128
                    P.mm(pss[:, 0:N], kaug_w[:, g, ks], qaug[:, h, qlo * 128:(qhi + 1) * 128], start=True, stop=False,
                         reads=KW + QA, writes=[pss])
                    P.mm(pss[:, 0:N], identB, WMv[:, rel_lo * 128:(rel_lo + nq) * 128], start=False, stop=True,
                         reads=[ctb], writes=[pss])
                    P.act(pt[:, 0:N], pss[:, 0:N], AF.Exp, reads=[pss], writes=pta, scale=0.125)
                    return (pt, pta, (qlo, nq))

                def stage2(it, st1):
                    h, branch, ci, c, last = it
                    g = h // 4
                    pt, pta, wq_ = st1
                    acc = accs[(h, branch)]
                    acc_v = acc[:, 0:260].rearrange("p (q c) -> p q c", q=4)
                    i = c - 4 * tt
                    if branch == 1:
                        for qb in range(max(i, 0), 4):
                            P.mm(acc_v[:, qb, :], pt[:, qb * 128:(qb + 1) * 128], vslc[:, c, g, 0:65], start=False, stop=False,
                                 reads=pta + VS, writes=[acc], skip_group_check=True)
                    else:
                        qlo, nq = wq_
                        for qi in range(nq):
                            qb = qlo + qi
                            P.mm(acc_v[:, qb, :], pt[:, qi * 128:(qi + 1) * 128], vwin[:, c, g, 0:65], start=False, stop=False,
                                 reads=pta + VW, writes=[acc], skip_group_check=True)
                    if last:
                        o_ = 0
                        sc_ = smcmb
                        P.ts("dve", sc_[:, o_:o_ + 4], acc_v[:, :, 64], 1e-30, None, ALU.max, reads=[acc], writes=[sc_])
                        P.op("dve", lambda e: e.reciprocal(sc_[:, o_ + 4:o_ + 8], sc_[:, o_:o_ + 4]), reads=[sc_], writes=[sc_])
                        P.tt("dve", sc_[:, o_ + 8:o_ + 12], sc_[:, o_ + 4:o_ + 8], gs4[:, 4 * tt:4 * tt + 4, h, branch], ALU.mult,
                             reads=[sc_, gsig], writes=[sc_])
                        tmp = fs_get()
                        tv = tmp[:, 0:256].rearrange("p (q d) -> p q d", q=4)
                        P.tt("dve", tv, acc_v[:, :, 0:64], sc_[:, o_ + 8:o_ + 12].unsqueeze(2).broadcast_to([128, 4, 64]), ALU.mult,
                             reads=[acc, sc_], writes=[tmp])
                        P.tt("dve", ocomb[:, :, h, :], ocomb[:, :, h, :], tv, ALU.add, reads=OC + [tmp], writes=OC)
                        ps_release(acc)

                LOOK = 2
                pend = [stage1(items[q]) for q in range(min(LOOK, len(items)))]
                for ii, it in enumerate(items):
                    if ii + LOOK < len(items):
                        pend.append(stage1(items[ii + LOOK]))
                    stage2(it, pend.pop(0))
                    if nxt_steps and ii % 2 == 1:
                        nxt_steps.pop(0)()
                while nxt_steps:
                    nxt_steps.pop(0)()
                for qb in range(4):
                    tb = 4 * tt + qb
                    pst = ps_get()
                    oc2 = ocomb[:, qb, :, :].rearrange("p h d -> p (h d)")
                    for cc in range(4):
                        P.tr(pst[:, cc * 128:(cc + 1) * 128], oc2[:, cc * 128:(cc + 1) * 128], identF, reads=OC + [ctf], writes=[pst])
                    P.copy("act", oT[3][:, :, tb * 128:(tb + 1) * 128], pst[:, :].rearrange("p (c t) -> p c t", c=4),
                           reads=[pst], writes=[TSA(oT[3], 4, tt)])

            if stop == "nsa":
                return
            wz = wb_get()
            wload(wz[:], kpn(W[:, C_Z:C_Z + 512]), wz)
            wx = [wb_get(), wb_get()]
            wload(wx[0][:], kpn(W[:, C_XBC:C_XBC + 512]), wx[0])
            wload(wx[1][:], kpn(W[:, C_XBC + 512:C_XBC + 1024]), wx[1])
            xbcT = _view(mT, 0, 2, 128, [8, TT])
            XBC = [CS(mT, (0, 1))]
            dtab = _view(mT, 2, 2, 128, [8, 128], F32)
            DTA = [CS(mT, (2, 3))]
            LT = _view(mT, 4, 2, 128, [8, 128], F32)
            LTA = [CS(mT, (4, 5))]
            Mb = _view(mT, 6, 1, 128, [8, 128])
            MA = [CS(mT, 6)]
            s7 = _view(mT, 7, 1, 128, [2048])
            S7 = [CS(mT, 7)]
            xdt = s7[:, 0:512].rearrange("p (h d) -> p h d", h=8)
            xdd = s7[:, 512:1024].rearrange("p (h d) -> p h d", h=8)
            xs_sb = s7[:, 1024:1536].rearrange("p (h d) -> p h d", h=8)
            Btm = s7[:, 1536:1792]
            P.op("dve", lambda e: e.memset(state[:], 0.0), writes=[state])
            P.op("dve", lambda e: e.memset(stateb[:], 0.0), writes=[stateb])
            P.op("dve", lambda e: e.memset(tails[:], 0.0), writes=[tails])
            convw = pp[:, PP_CONVW:PP_CONVW + 32].rearrange("p (c k) -> p c k", c=8)
            dsk = pp[:, PP_DSKIP:PP_DSKIP + 8]
            for tt in range(NTT):
                for ch in range(8):
                    ps = ps_get()
                    proj_fm(wx[ch // 4], (ch % 4) * 128, 128, tt, ps)
                    rw = raw[rr["raw"] % 2]
                    rr["raw"] += 1
                    P.copy("dve", rw[:, 0:3], tails[:, ch, :], reads=[tails], writes=[rw])
                    P.copy("act", rw[:, 3:515], ps[:, :], reads=[ps], writes=[rw])
                    P.copy("dve", tails[:, ch, :], rw[:, 512:515], reads=[rw], writes=[tails])
                    acc = fs_get()
                    P.ts("dve", acc[:], rw[:, 0:512], convw[:, ch, 0:1], None, ALU.mult, reads=[rw, pp], writes=[acc])
                    for kk in range(1, 4):
                        P.stt(acc[:], rw[:, kk:kk + 512], convw[:, ch, kk:kk + 1], acc[:], ALU.mult, ALU.add,
                              reads=[rw, pp, acc], writes=[acc])
                    P.act(xbcT[:, ch, :], acc[:], AF.Silu, reads=[acc, pp], writes=XBC,
                          bias=pp[:, PP_CONVB + ch:PP_CONVB + ch + 1], scale=1.0)
                for lt in range(4):
                    tb = 4 * tt + lt
                    cs = slice(lt * 128, (lt + 1) * 128)
                    bs = slice(tb * 128, (tb + 1) * 128)
                    psx = ps_get()
                    pxb = psbf(psx)
                    for ch in range(4):
                        P.tr(pxb[:, ch * 128:(ch + 1) * 128], xbcT[:, ch, cs], identB, reads=XBC + [ctb], writes=[psx])
                    psB = ps_get()
                    pBb = psbf(psB)
                    for g in range(2):
                        P.tr(pBb[:, g * 128:(g + 1) * 128], xbcT[:, 4 + g, cs], identB, reads=XBC + [ctb], writes=[psB])
                    P.tt("dve", sm[:, 120:128], dtb[:, tb, :], negA[:], ALU.mult, reads=[dtb, negA], writes=[sm])
                    P.copy("dve", dtab, sm[:, 120:128].unsqueeze(2).broadcast_to([128, 8, 128]), reads=[sm], writes=DTA)
                    psa = ps_get()
                    P.mm(psa[:, 0:8], triU, sm[:, 120:128], reads=[ctf, sm], writes=[psa])
                    P.mm(psa[:, 8:16], onesF, sm[:, 120:128], reads=[ctf, sm], writes=[psa])
                    P.copy("dve", sm[:, 128:136], psa[:, 0:8], reads=[psa], writes=[sm])
                    P.tt("dve", sm[:, 136:144], psa[:, 8:16], sm[:, 128:136], ALU.subtract, reads=[psa, sm], writes=[sm])
                    P.copy("dve", sm[:, 144:152], psa[:, 8:16], reads=[psa], writes=[sm])
                    P.ts("dve", sm[:, 176:184], sm[:, 128:136], -1.0, None, ALU.mult, reads=[sm], writes=[sm])
                    P.act(sm[:, 152:176], sm[:, 128:152], AF.Exp, reads=[sm], writes=[sm])
                    ea = sm[:, 152:160]
                    dec = sm[:, 160:168]
                    cdec = sm[:, 168:176]
                    psl = [ps_get(), ps_get()]
                    for h in range(8):
                        pl_ = psl[h // 4]
                        o_ = pl_[:, (h % 4) * 128:(h % 4 + 1) * 128]
                        P.mm(o_, dtab[:, h, :], triU, start=True, stop=False, reads=DTA + [ctf], writes=[pl_])
                        P.mm(o_, identB, NEGM, start=False, stop=True, reads=[ctb], writes=[pl_])
                    for h in range(8):
                        pl_ = psl[h // 4]
                        o_ = pl_[:, (h % 4) * 128:(h % 4 + 1) * 128]
                        P.act(LT[:, h, :], o_, AF.Exp, reads=[pl_, sm], writes=LTA, bias=sm[:, 176 + h:177 + h], scale=1.0)
                    psc_ = ps_get()
                    for g in range(2):
                        P.mm(psc_[:, g * 128:(g + 1) * 128], xbcT[:, 4 + g, cs], xbcT[:, 6 + g, cs], reads=XBC, writes=[psc_])
                    cbT = fs_get()
                    P.copy("act", cbT[:, 0:256], psc_[:, 0:256], reads=[psc_], writes=[cbT])
                    P.tt("dve", Mb.rearrange("p (g r) l -> p g r l", g=2),
                         LT.rearrange("p (g r) l -> p g r l", g=2),
                         cbT[:, 0:256].rearrange("p (g l) -> p g l", g=2).unsqueeze(2).broadcast_to([128, 2, 4, 128]),
                         ALU.mult, reads=LTA + [cbT], writes=MA)
                    pxv = pxb[:, 0:512].rearrange("p (h d) -> p h d", h=8)
                    P.tt("dve", xdt, pxv, dtb[:, tb, :].unsqueeze(2).broadcast_to([128, 8, 64]), ALU.mult,
                         reads=[psx, dtb], writes=S7)
                    P.copy("act", Btm, pBb[:, 0:256], reads=[psB], writes=S7)
                    P.tt("dve", xdd, xdt, dec.unsqueeze(2).broadcast_to([128, 8, 64]), ALU.mult, reads=S7 + [sm], writes=S7)
                    psy = ps_get()
                    for h in range(8):
                        P.mm(psy[:, h * 64:(h + 1) * 64], Mb[:, h, :], xdt[:, h, :], reads=MA + S7, writes=[psy])
                    pso = ps_get()
                    for g in range(2):
                        P.mm(pso[:, g * 256:(g + 1) * 256], xbcT[:, 6 + g, cs], stateb[:, g * 256:(g + 1) * 256],
                             reads=XBC + [stateb], writes=[pso])
                    y1 = fs_get()
                    y1v = y1[:].rearrange("p (h d) -> p h d", h=8)
                    P.tt("dve", y1v, pso[:, :].rearrange("p (h d) -> p h d", h=8),
                         ea.unsqueeze(2).broadcast_to([128, 8, 64]), ALU.mult, reads=[pso, sm], writes=[y1])
                    P.tt("dve", y1[:], y1[:], psy[:, :], ALU.add, reads=[y1, psy], writes=[y1])
                    y2 = fs_get()
                    P.tt("dve", y2[:].rearrange("p (h d) -> p h d", h=8), pxv,
                         dsk.unsqueeze(2).broadcast_to([128, 8, 64]), ALU.mult, reads=[psx, pp], writes=[y2])
                    P.tt("dve", y1[:], y1[:], y2[:], ALU.add, reads=[y1, y2], writes=[y1])
                    pst_ = ps_get()
                    for g in range(2):
                        P.mm(pst_[:, g * 256:(g + 1) * 256], Btm[:, g * 128:(g + 1) * 128],
                             xdd[:, 4 * g:4 * g + 4, :].rearrange("p h d -> p (h d)"), reads=S7, writes=[pst_])
                    stv = state[:].rearrange("p (h d) -> p h d", h=8)
                    P.tt("dve", stv, stv, cdec.unsqueeze(2).broadcast_to([128, 8, 64]), ALU.mult, reads=[state, sm], writes=[state])
                    P.tt("dve", state[:], state[:], pst_[:, :], ALU.add, reads=[state, pst_], writes=[state])
                    P.copy("act", stateb[:], state[:], reads=[state], writes=[stateb])
                    psz = ps_get()
                    for k in range(KC):
                        P.mm(psz[:, :], hT[:, k, bs], wz[:, k, :], start=(k == 0), stop=(k == KC - 1), reads=[TS(hT, k, tt), wz], writes=[psz])
                    zs = fs_get()
                    P.act(zs[:], psz[:, :], AF.Silu, reads=[psz], writes=[zs])
                    P.tt("dve", y1[:], y1[:], zs[:], ALU.mult, reads=[y1, zs], writes=[y1])
                    for g in range(2):
                        P.act(zs[:, g * 256:(g + 1) * 256], y1[:, g * 256:(g + 1) * 256], AF.Square, reads=[y1], writes=[zs, sm],
                              accum_out=sm[:, 184 + g:185 + g])
                    P.act(sm[:, 186:188], sm[:, 184:186], AF.Sqrt, reads=[sm], writes=[sm], bias=EPS, scale=1.0 / 256)
                    P.op("dve", lambda e: e.reciprocal(sm[:, 188:190], sm[:, 186:188]), reads=[sm], writes=[sm])
                    oa = sq_get()
                    for g in range(2):
                        P.stt(oa[:, g * 256:(g + 1) * 256], y1[:, g * 256:(g + 1) * 256], sm[:, 188 + g:189 + g],
                              pp[:, PP_SSDNG + g * 256:PP_SSDNG + (g + 1) * 256], ALU.mult, ALU.mult,
                              reads=[y1, sm, pp], writes=[oa])
                    pso2 = ps_get()
                    po2 = psbf(pso2)
                    for ch in range(4):
                        P.tr(po2[:, ch * 128:(ch + 1) * 128], oa[:, ch * 128:(ch + 1) * 128], identB, reads=[oa, ctb], writes=[pso2])
                    P.copy("act", oT[0][:, :, bs], po2[:, 0:512].rearrange("p (c t) -> p c t", c=4), reads=[pso2], writes=[TSA(oT[0], 4, tt)])

            if stop == "ssd":
                return
            wB = [wb_get(), wb_get(), wb_get()]
            for i_, c0 in enumerate((C_B, C_C, C_HX)):
                wload(wB[i_][:], kpn(W[:, c0:c0 + 512]), wB[i_])
            chx = _view(mT, 0, 3, 128, [2050], F32)
            CHX = [CS(mT, (0, 1, 2))]
            accB = _view(mT, 3, 2, 128, [2048], F32)
            ACB = [CS(mT, (3, 4))]
            bsb = _view(mT, 5, 1, 128, [2048])
            BSB = [CS(mT, 5)]
            P.op("dve", lambda e: e.memset(chx[:, 0:2], 0.0), writes=CHX)
            scw = pp[:, PP_SCW:PP_SCW + 12].rearrange("p (c k) -> p c k", c=4)
            for cc in range(4):
                for tt in range(NTT):
                    sl = slice(tt * TT, (tt + 1) * TT)
                    psb_ = ps_get()
                    proj_fm(wB[0], cc * 128, 128, tt, psb_)
                    psc_ = ps_get()
                    proj_fm(wB[1], cc * 128, 128, tt, psc_)
                    psh = ps_get()
                    proj_fm(wB[2], cc * 128, 128, tt, psh)
                    P.copy("act", bsb[:, sl], psb_[:, :], reads=[psb_], writes=BSB)
                    hx = fs_get()
                    P.copy("act", hx[:], psh[:, :], reads=[psh], writes=[hx])
                    P.tt("dve", chx[:, 2 + tt * TT:2 + (tt + 1) * TT], psc_[:, :], hx[:], ALU.mult, reads=[psc_, hx], writes=CHX)
                P.ts("dve", accB, chx[:, 0:2048], scw[:, cc, 0:1], None, ALU.mult, reads=CHX + [pp], writes=ACB)
                P.stt(accB, chx[:, 1:2049], scw[:, cc, 1:2], accB, ALU.mult, ALU.add, reads=CHX + [pp] + ACB, writes=ACB)
                P.stt(accB, chx[:, 2:2050], scw[:, cc, 2:3], accB, ALU.mult, ALU.add, reads=CHX + [pp] + ACB, writes=ACB)
                P.tt("dve", oT[1][:, cc, :], accB, bsb, ALU.mult, reads=ACB + BSB, writes=[CS(oT[1], cc)])

            if stop == "sconv":
                return
            wC = [wb_get(), wb_get()]
            wload(wC[0][:], kpn(W[:, C_GU:C_GU + 512]), wC[0])
            wload(wC[1][:], kpn(W[:, C_GV:C_GV + 512]), wC[1])
            sgw32 = fs_get()
            sgv = sgw32[:].rearrange("p (g t) -> p g t", g=4)
            P.dma(sgv, sgw_d[l], writes=[sgw32])
            P.tt("dve", sgwb[:], sgv, triU.unsqueeze(1).broadcast_to([128, 4, 128]), ALU.mult,
                 reads=[sgw32, ctf], writes=[sgwb])
            for cc in range(4):
                for tt in range(NTT):
                    sl = slice(tt * TT, (tt + 1) * TT)
                    ps = ps_get()
                    proj_fm(wC[0], cc * 128, 128, tt, ps)
                    P.act(oT[2][:, cc, sl], ps[:, :], AF.Gelu_apprx_tanh, reads=[ps], writes=[TS(oT[2], cc, tt)])
            sgb = pp[:, PP_SGB:PP_SGB + 512]
            for tb in range(NB):
                bs = slice(tb * 128, (tb + 1) * 128)
                ps = ps_get()
                for k in range(KC):
                    P.mm(ps[:, :], hT[:, k, bs], wC[1][:, k, :], start=(k == 0), stop=(k == KC - 1), reads=[TS(hT, k, tb // 4), wC[1]], writes=[ps])
                vg = fs_get()
                P.act(vg[:], ps[:, :], AF.Gelu_apprx_tanh, reads=[ps], writes=[vg])
                junk = fs_get()
                P.act(junk[:], vg[:], AF.Square, reads=[vg], writes=[junk, sm], accum_out=sm[:, 192:193])
                P.act(sm[:, 193:194], sm[:, 192:193], AF.Sqrt, reads=[sm], writes=[sm], bias=EPS, scale=1.0 / 512)
                P.op("dve", lambda e: e.reciprocal(sm[:, 194:195], sm[:, 193:194]), reads=[sm], writes=[sm])
                vn = sq_get()
                P.stt(vn[:], vg[:], sm[:, 194:195], pp[:, PP_SGNG:PP_SGNG + 512], ALU.mult, ALU.mult, reads=[vg, sm, pp], writes=[vn])
                ps2 = ps_get()
                for g in range(4):
                    P.mm(ps2[:, g * 128:(g + 1) * 128], vn[:, g * 128:(g + 1) * 128], sgwb[:, g, :], reads=[vn, sgwb], writes=[ps2])
                t1 = fs_get()
                P.tt("dve", t1[:], ps2[:, :], sgb, ALU.add, reads=[ps2, pp], writes=[t1])
                P.tt("dve", oT[2][:, :, bs], t1[:].rearrange("p (g t) -> p g t", g=4), oT[2][:, :, bs], ALU.mult,
                     reads=[t1, TSA(oT[2], 4, tb // 4)], writes=[TSA(oT[2], 4, tb // 4)])

            if debug and l == 0 and s == 0:
                for i in range(4):
                    for cc in range(4):
                        stg32 = _view(mT, 0, 2, 128, [T], F32)
                        P.copy("dve", stg32, oT[i][:, cc, :], reads=[CS(oT[i], cc)], writes=[CS(mT, (0, 1))])
                        P.dma(dbg_d[i, :, cc, :], stg32, reads=[CS(mT, (0, 1))], final=True)

            if stop == "sgate":
                return
            for j in range(KC):
                js = slice(j * 128, (j + 1) * 128)
                wg = wb_get()
                for i in range(4):
                    wload(wg[:, :, i * 128:(i + 1) * 128], kpn(wg_d[l, i][:, js]), wg)
                wbr = wb_get()
                wbv = wbr[:, 0:4, :].rearrange("p k (i n) -> p i k n", i=4)
                for i in range(4):
                    wload(wbv[:, i, :, :], kpn(wb_d[l, i][:, js]), wbr)
                for tt in range(NTT):
                    sl = slice(tt * TT, (tt + 1) * TT)
                    macc = fl_get()
                    for i in range(4):
                        psg = ps_get()
                        proj_fm(wg, i * 128, 128, tt, psg)
                        psp = ps_get()
                        for kk in range(4):
                            P.mm(psp[:, :], wbv[:, i, kk, :], oT[i][:, kk, sl], start=(kk == 0), stop=(kk == 3),
                                 reads=[wbr, TS(oT[i], kk, tt)], writes=[psp])
                        sg = fs_get()
                        P.act(sg[:], psg[:, :], AF.Sigmoid, reads=[psg], writes=[sg])
                        if i == 0:
                            P.tt("dve", macc[:], sg[:], psp[:, :], ALU.mult, reads=[sg, psp], writes=[macc])
                        else:
                            P.tt("dve", sg[:], sg[:], psp[:, :], ALU.mult, reads=[sg, psp], writes=[sg])
                            if i < 3:
                                P.tt("dve", macc[:], macc[:], sg[:], ALU.add, reads=[macc, sg], writes=[macc])
                            else:
                                P.tt("dve", mT[:, j, sl], macc[:], sg[:], ALU.add, reads=[macc, sg], writes=[TS(mT, j, tt)])

            if stop == "merge":
                return
            wo2 = [wb_get(), wb_get()]
            for jj in range(2):
                wload(wo2[jj][:], kpn(wo_d[l][:, jj * 512:(jj + 1) * 512]), wo2[jj])
            for tt in range(NTT):
                sl = slice(tt * TT, (tt + 1) * TT)
                half = tt % 2
                xt = _view(oT[half], 0, 4, 128, [KC, TT], F32)
                xacc = [CS(oT[half], range(4))]
                sap = kpn(xT_d[s][:, sl]) if src is None else kpn(src.t[:, sl])
                sacc = [] if src is None else [(src, [k * 4 + tt for k in range(KC)])]
                P.dma(xt, sap, reads=sacc, writes=xacc)
                for chn in range(KC):
                    wo = wo2[chn // 4]
                    j4 = chn % 4
                    ps = ps_get()
                    for k in range(KC):
                        P.mm(ps[:, :], wo[:, k, j4 * 128:(j4 + 1) * 128], mT[:, k, sl], start=(k == 0), stop=(k == KC - 1),
                             reads=[wo, TS(mT, k, tt)], writes=[ps])
                    P.stt(xt[:, chn, :], ps[:, :], gate1(chn), xt[:, chn, :], ALU.mult, ALU.add, reads=[ps, modT] + xacc, writes=xacc)
                P.dma(kpn(xb_d[:, sl]), xt, reads=xacc, writes=[(XB, [k * 4 + tt for k in range(KC)])])
                norm_tile(xt, xacc, s, l, 1, tt)

            def hid(hc, tt):
                if hc < 16:
                    return oT[hc // 4][:, hc % 4, :], [TS(oT[hc // 4], hc % 4, tt)]
                return mT[:, hc - 16, :], [TS(mT, hc - 16, tt)]

            for hp in range(NHC // 2):
                wf = wb_get()
                wload(wf[:, :, 0:256], kpn(wfi_d[l][:, hp * 256:(hp + 1) * 256]), wf)
                wload(wf[:, :, 256:512], kpn(wfi_d[l][:, D_FF + hp * 256:D_FF + (hp + 1) * 256]), wf)
                for hh in range(2):
                    hc = hp * 2 + hh
                    for tt in range(NTT):
                        hap, hacc = hid(hc, tt)
                        sl = slice(tt * TT, (tt + 1) * TT)
                        psa_ = ps_get()
                        proj_fm(wf, hh * 128, 128, tt, psa_)
                        psb2 = ps_get()
                        proj_fm(wf, 256 + hh * 128, 128, tt, psb2)
                        sa = fs_get()
                        P.act(sa[:], psa_[:, :], AF.Silu, reads=[psa_], writes=[sa])
                        P.tt("dve", hap[:, sl], sa[:], psb2[:, :], ALU.mult, reads=[sa, psb2], writes=hacc)
            for j in range(KC):
                js = slice(j * 128, (j + 1) * 128)
                wf2 = wb_get()
                w2v = wf2[:].rearrange("p k n -> p (k n)")[:, 0:NHC * 128].rearrange("p (c n) -> p c n", c=NHC)
                wload(w2v, wfo_d[l][:, js].rearrange("(c p) n -> p c n", p=128), wf2)
                for tt in range(NTT):
                    sl = slice(tt * TT, (tt + 1) * TT)
                    ps = ps_get()
                    for hc in range(NHC):
                        hap, hacc = hid(hc, tt)
                        P.mm(ps[:, :], w2v[:, hc, :], hap[:, sl], start=(hc == 0), stop=(hc == NHC - 1), reads=[wf2] + hacc, writes=[ps])
                    xin = fs_get()
                    P.dma(xin[:], xb_d[js, sl], reads=[(XB, j * 4 + tt)], writes=[xin])
                    P.stt(xin[:], ps[:, :], gate2(j), xin[:], ALU.mult, ALU.add, reads=[ps, modT, xin], writes=[xin])
                    if dst_is_out:
                        P.dma(outT_d[s][js, sl], xin[:], reads=[xin], writes=[(OUT, j * 4 + tt)], final=True)
                    else:
                        P.dma(xa_d[js, sl], xin[:], reads=[xin], writes=[(XA, j * 4 + tt)])

        for s in range(n_seq):
            for l in range(n_layers):
                layer(s, l, None if l == 0 else XA, XA, l == n_layers - 1)

        stats = P.finalize()
        P.emit()
    return nc, stats


_CACHE = {}


def _host_inputs(inp, n_seq=SEQ_PER_CORE):
    ctf, ctb, lk, lkc, rh = _const_tables()
    pp, pl = _pack_params(inp)
    shared = dict(
        pp=pp, pl=pl, ctf=ctf, ctb=ctb, lk=lk, lkc=lkc, rh=rh,
        sgwT=np.ascontiguousarray(np.asarray(inp["sg_w"], np.float32).transpose(0, 3, 1, 2)),
        ada_w=np.asarray(inp["ada_w"], np.float32), w_in=np.asarray(inp["w_in"], np.float32),
        cmp_w1=np.asarray(inp["nsa_cmp_w1"], np.float32), cmp_w2=np.asarray(inp["nsa_cmp_w2"], np.float32),
        w_branch=np.asarray(inp["w_branch"], np.float32), w_branch_gate=np.asarray(inp["w_branch_gate"], np.float32),
        w_out=np.asarray(inp["w_out"], np.float32), w_ffn_in=np.asarray(inp["w_ffn_in"], np.float32),
        w_ffn_out=np.asarray(inp["w_ffn_out"], np.float32),
    )
    x = np.asarray(inp["x"], np.float32)
    c = np.asarray(inp["c"], np.float32)
    maps = []
    for core in range(NCORES):
        b0 = core * SEQ_PER_CORE
        xs = x[b0:b0 + n_seq]
        cs = c[b0:b0 + n_seq]
        m = dict(shared)
        m["xT"] = np.ascontiguousarray(xs.transpose(0, 2, 1))
        m["cT"] = np.ascontiguousarray(cs.reshape(n_seq, KC, 128).transpose(2, 1, 0))
        maps.append(m)
    return maps


def kernel(**inputs):
    if "nc" not in _CACHE:
        _CACHE["nc"], _CACHE["stats"] = build_program()
    nc = _CACHE["nc"]
    maps = _host_inputs(inputs)
    res = run_bass_kernel_spmd(nc, maps, core_ids=list(range(NCORES)))
    out = np.empty((NCORES * SEQ_PER_CORE, T, D), np.float32)
    for core in range(NCORES):
        o = np.asarray(res.results[core]["outT"])
        out[core * SEQ_PER_CORE:(core + 1) * SEQ_PER_CORE] = o.transpose(0, 2, 1)
    return out
```

```python
import numpy as np
import concourse.bass as bass
import concourse.mybir as mybir
from concourse.bass_utils import run_bass_kernel_spmd
from contextlib import ExitStack

F32 = mybir.dt.float32
BF16 = mybir.dt.bfloat16
AF = mybir.ActivationFunctionType
ALU = mybir.AluOpType
AX = mybir.AxisListType

ENGS = ("pe", "act", "dve", "pool", "sp")
N_DMA_SEMS = 16

D = 1024
T = 2048
DEPTH = 4
NCORES = 8
SEQ_PER_CORE = 2
KC = 8
TT = 512
NTT = 4
NB = 16
D_IN = 5408
D_FF = 2816
NHC = 22
EPS = 1e-6
NEG = -30000.0
C_Z, C_XBC, C_DT = 0, 512, 1536
C_B, C_C, C_HX = 1544, 2056, 2568
C_GU, C_GV = 3080, 3592
C_Q = 4104
C_KCMP, C_VCMP, C_KSLC, C_VSLC, C_KWIN, C_VWIN = 4616, 4744, 4872, 5000, 5128, 5256
C_GATES = 5384


class Buf:
    __slots__ = ("t", "name", "nslots", "lw", "rd")

    def __init__(self, t, name, nslots=1):
        self.t = t
        self.name = name
        self.nslots = nslots
        self.lw = [None] * nslots
        self.rd = [[] for _ in range(nslots)]

    def __getitem__(self, idx):
        return self.t[idx]


class Op:
    __slots__ = ("eng", "fn", "deps", "is_dma", "sig", "sig_idx", "dsem", "dval", "idx", "waits", "prewait")

    def __init__(self, eng, fn, is_dma):
        self.eng = eng
        self.fn = fn
        self.is_dma = is_dma
        self.deps = set()
        self.sig = False
        self.sig_idx = 0
        self.dsem = None
        self.dval = 0
        self.waits = []
        self.prewait = None


class Prog:
    def __init__(self, nc, stack):
        self.nc = nc
        self.stack = stack
        self.ops = []
        self.final_dma = []
        self.nbuf = 0

    def sbuf(self, shape, dtype, name=None, nslots=1):
        self.nbuf += 1
        name = f"sb{self.nbuf}_{name or ""}"
        t = self.stack.enter_context(self.nc.sbuf_tensor(name, list(shape), dtype))
        return Buf(t, name, nslots)

    def psum(self, shape, dtype=F32, name=None, nslots=1):
        self.nbuf += 1
        name = f"ps{self.nbuf}_{name or ""}"
        t = self.stack.enter_context(self.nc.psum_tensor(name, list(shape), dtype))
        return Buf(t, name, nslots)

    @staticmethod
    def _norm(acc):
        out = []
        for a in acc:
            if a is None:
                continue
            if isinstance(a, Buf):
                out.append((a, range(a.nslots)))
            else:
                b, s = a
                if isinstance(s, int):
                    s = (s,)
                out.append((b, s))
        return out

    def op(self, eng, fn, reads=(), writes=(), dma=False, final=False):
        o = Op(eng, fn, dma)
        o.idx = len(self.ops)
        rl = self._norm(reads)
        wl = self._norm(writes)
        for b, slots in rl:
            for s in slots:
                w = b.lw[s]
                if w is not None:
                    o.deps.add(w)
        for b, slots in wl:
            for s in slots:
                w = b.lw[s]
                if w is not None:
                    o.deps.add(w)
                for r in b.rd[s]:
                    o.deps.add(r)
        for b, slots in rl:
            for s in slots:
                b.rd[s].append(o)
        for b, slots in wl:
            for s in slots:
                b.lw[s] = o
                b.rd[s] = []
        o.deps.discard(o)
        self.ops.append(o)
        if final:
            self.final_dma.append(o)
        return o

    def mm(self, out, lhsT, rhs, start=True, stop=True, reads=(), writes=(), **kw):
        return self.op("pe", lambda e: e.matmul(out, lhsT, rhs, start=start, stop=stop, **kw), reads, writes)

    def tr(self, out, in_, ident, reads=(), writes=()):
        return self.op("pe", lambda e: e.transpose(out, in_, ident), reads, writes)

    def act(self, out, in_, func, reads=(), writes=(), **kw):
        return self.op("act", lambda e: e.activation(out, in_, func, **kw), reads, writes)

    def tt(self, eng, out, in0, in1, op, reads=(), writes=()):
        return self.op(eng, lambda e: e.tensor_tensor(out, in0, in1, op), reads, writes)

    def ts(self, eng, out, in0, s1, s2, op0, op1=None, reads=(), writes=()):
        if op1 is None:
            return self.op(eng, lambda e: e.tensor_scalar(out, in0, s1, None, op0), reads, writes)
        return self.op(eng, lambda e: e.tensor_scalar(out, in0, s1, s2, op0, op1), reads, writes)

    def stt(self, out, in0, scalar, in1, op0, op1, reads=(), writes=()):
        return self.op("dve", lambda e: e.scalar_tensor_tensor(out, in0, scalar, in1, op0, op1), reads, writes)

    def copy(self, eng, out, in_, reads=(), writes=()):
        if eng == "act":
            return self.op(eng, lambda e: e.copy(out, in_), reads, writes)
        return self.op(eng, lambda e: e.tensor_copy(out, in_), reads, writes)

    def dma(self, out, in_, reads=(), writes=(), eng="sp", final=False, **kw):
        if eng == "pool":
            kw.setdefault("max_dma_last_dim", 4096)
        return self.op(eng, lambda e: e.dma_start(out, in_, **kw), reads, writes, dma=True, final=final)

    def finalize(self):
        ops = self.ops
        for o in ops:
            need = []
            for d in o.deps:
                if d.is_dma or o.is_dma:
                    need.append(d)
                elif d.eng != o.eng:
                    need.append(d)
                elif o.eng != "pe":
                    need.append(d)
            best = {}
            keep = []
            for d in need:
                if d.is_dma:
                    keep.append(d)
                else:
                    b = best.get(d.eng)
                    if b is None or d.idx > b.idx:
                        best[d.eng] = d
            keep.extend(best.values())
            o.deps = keep
            for d in keep:
                if not d.is_dma:
                    d.sig = True
        cnt = {e: 0 for e in ENGS}
        dcnt = {e: 0 for e in ENGS}
        for o in ops:
            if o.is_dma:
                i = dcnt[o.eng]
                dcnt[o.eng] += 1
                o.dsem = (o.eng, i % N_DMA_SEMS)
                o.dval = 16 * (i // N_DMA_SEMS + 1)
                if i >= N_DMA_SEMS:
                    o.prewait = (o.dsem, o.dval - 16)
            elif o.sig:
                cnt[o.eng] += 1
                o.sig_idx = cnt[o.eng]
        waited = {e: {} for e in ENGS}
        nw = 0
        for o in ops:
            w = waited[o.eng]
            req = {}
            if o.prewait is not None:
                req[o.prewait[0]] = o.prewait[1]
            for d in o.deps:
                if d.is_dma:
                    k, v = d.dsem, d.dval
                else:
                    k, v = d.eng, d.sig_idx
                if req.get(k, 0) < v:
                    req[k] = v
            for k, v in req.items():
                if w.get(k, 0) < v:
                    w[k] = v
                    o.waits.append((k, v))
                    nw += 1
        self.stats = dict(n_ops=len(ops), n_waits=nw, sig=dict(cnt), dma=dict(dcnt),
                          per_eng={e: sum(1 for o in ops if o.eng == e) for e in ENGS})
        return self.stats

    def emit(self):
        nc = self.nc
        st = self.stack
        esem = {e: st.enter_context(nc.semaphore(f"s_{e}")) for e in ENGS}
        dsem = {}
        for e in ENGS:
            if self.stats["dma"][e]:
                for j in range(N_DMA_SEMS):
                    dsem[(e, j)] = st.enter_context(nc.semaphore(f"d_{e}{j}"))

        def semof(k):
            return dsem[k] if isinstance(k, tuple) else esem[k]

        per = {e: [o for o in self.ops if o.eng == e] for e in ENGS}
        finals = self.final_dma
        block = st.enter_context(nc.Block())

        def run(eh, name):
            for o in per[name]:
                for k, v in o.waits:
                    eh.wait_ge(semof(k), v)
                ins = o.fn(eh)
                if o.is_dma:
                    ins.then_inc(dsem[o.dsem], 16)
                elif o.sig:
                    ins.then_inc(esem[name], 1)
            if name == "sp":
                for o in finals:
                    eh.wait_ge(dsem[o.dsem], o.dval)

        @block.tensor
        def _(e):
            run(e, "pe")

        @block.scalar
        def _(e):
            run(e, "act")

        @block.vector
        def _(e):
            run(e, "dve")

        @block.gpsimd
        def _(e):
            run(e, "pool")

        @block.sync
        def _(e):
            run(e, "sp")


CF_IDENT, CF_TRIU, CF_ONES, CF_M2 = 0, 128, 256, 384
NCF = 896
CB_IDENT, CB_ONES, CB_NEGM, CB_CM, CB_DM, CB_WM, CB_E, CB_OVL, CB_ONES64, CB_ZERO = 0, 128, 256, 384, 2432, 4480, 4864, 6912, 6946, 7074
NCB = 7334


def _const_tables():
    p = np.arange(128)
    ctf = np.zeros((128, NCF), np.float32)
    ctf[:, CF_IDENT:CF_IDENT + 128] = np.eye(128)
    ctf[:, CF_TRIU:CF_TRIU + 128] = (p[:, None] <= p[None, :])
    ctf[:, CF_ONES:CF_ONES + 128] = 1.0
    m1 = np.zeros((128, 16, 32), np.float32)
    m2 = np.zeros((128, 16, 32), np.float32)
    j = np.arange(32)
    for qb in range(16):
        t = qb * 128 + p
        cur = t // 64
        forced = (j[None, :] == 0) | (j[None, :] == cur[:, None]) | (j[None, :] == cur[:, None] - 1)
        future = j[None, :] > cur[:, None]
        m1[:, qb, :] = np.where(forced | future, 0.0, 1.0)
        m2[:, qb, :] = np.where(future, -1e30, np.where(forced, 1e9, 0.0))
    ctf[:, CF_M2:CF_M2 + 512] = m2.reshape(128, 512)

    ctb = np.zeros((128, NCB), np.float32)
    ctb[:, CB_IDENT:CB_IDENT + 128] = np.eye(128)
    ctb[:, CB_ONES:CB_ONES + 128] = 1.0
    ctb[:, CB_NEGM:CB_NEGM + 128] = np.where(p[None, :] < p[:, None], -10000.0, 0.0)
    tpos = np.arange(T)
    n = np.arange(128)
    ctb[:, CB_CM:CB_CM + T] = np.where(tpos[None, :] >= 16 * n[:, None] + 31, 0.0, NEG)
    dm = np.zeros((128, 4, 512), np.float32)
    tl = np.arange(512)
    for i in range(4):
        dm[:, i, :] = np.where(tl[None, :] >= 128 * i + p[:, None], 0.0, NEG)
    ctb[:, CB_DM:CB_DM + 2048] = dm.reshape(128, 2048)
    wm = np.zeros((128, 3, 128), np.float32)
    b = np.arange(128)
    wm[:, 0, :] = np.where(b[None, :] >= p[:, None], 0.0, NEG)
    wm[:, 2, :] = np.where(b[None, :] < p[:, None], 0.0, NEG)
    ctb[:, CB_WM:CB_WM + 384] = wm.reshape(128, 384)
    e = np.zeros((128, 16, 128), np.float32)
    for c in range(16):
        for key in range(128):
            e[2 * c + key // 64, c, key] = 1.0
    ctb[:, CB_E:CB_E + 2048] = e.reshape(128, 2048)
    ovl = np.zeros((128, 33), np.float32)
    cs = 16 * n
    ss = 64 * np.arange(32)
    ovl[:, 0:32] = ((cs[:, None] < ss[None, :] + 64) & (cs[:, None] + 32 > ss[None, :]))
    ovl[:, 32] = 1.0
    ctb[:, CB_OVL:CB_OVL + 33] = ovl
    ctb[0:64, CB_ONES64:CB_ONES64 + 64] = 1.0

    lk = np.zeros((4, T), np.float32)
    lk[0] = 128 * (tpos // 128)
    lk[1] = tpos % 128
    lk[2] = 1.0
    lk[3] = 1.0
    lkc = np.zeros((4, 128), np.float32)
    lkc[0] = 16 * n
    lkc[1] = 31
    lkc[2] = 1.0
    lkc[3] = 1.0
    rh = np.zeros((4, 8, T), np.float32)
    for h in range(8):
        s8 = 8.0 * 2.0 ** (-(h + 1))
        rh[0, h] = s8
        rh[1, h] = s8
        rh[2, h] = -s8 * 128 * (tpos // 128)
        rh[3, h] = -s8 * (tpos % 128)
    return ctf, ctb, lk, lkc, rh


PP_CONVW, PP_CONVB, PP_DTB, PP_ALOG, PP_DSKIP, PP_SSDNG = 0, 32, 40, 48, 56, 64
PP_SCW, PP_SGNG, PP_SGB, PP_QG, PP_KG12, PP_KG0, PP_PE = 576, 588, 1100, 1612, 1613, 1615, 1679
NPP = 1711
PL_ADAB, PL_GMIX, PL_GFFN = 0, 48, 56
NPL = 64


def _pack_params(inp):
    L = DEPTH
    rep = lambda v: np.broadcast_to(np.asarray(v, np.float32).reshape(1, -1), (128, v.size))
    pp = np.zeros((L, 128, NPP), np.float32)
    pl = np.zeros((128, L, NPL), np.float32)
    for l in range(L):
        pp[l, :, PP_CONVW:PP_CONVW + 32] = inp["ssd_conv_w"][l].reshape(4, 8, 128).transpose(2, 1, 0).reshape(128, 32)
        pp[l, :, PP_CONVB:PP_CONVB + 8] = inp["ssd_conv_b"][l].reshape(8, 128).T
        pp[l, :, PP_DTB:PP_DTB + 8] = rep(inp["ssd_dt_bias"][l])
        pp[l, :, PP_ALOG:PP_ALOG + 8] = rep(inp["ssd_a_log"][l])
        pp[l, :, PP_DSKIP:PP_DSKIP + 8] = rep(inp["ssd_d"][l])
        pp[l, :, PP_SSDNG:PP_SSDNG + 512] = rep(inp["ssd_norm_g"][l])
        pp[l, :, PP_SCW:PP_SCW + 12] = inp["sc_conv_w"][l].reshape(3, 4, 128).transpose(2, 1, 0).reshape(128, 12)
        pp[l, :, PP_SGNG:PP_SGNG + 512] = rep(inp["sg_norm_g"][l])
        pp[l, :, PP_SGB:PP_SGB + 512] = rep(inp["sg_b"][l])
        pp[l, :, PP_QG] = np.tile(inp["nsa_q_norm_g"][l], 2)
        pp[l, :, PP_KG12] = np.tile(inp["nsa_k_norm_g"][l, 1], 2)
        pp[l, :, PP_KG12 + 1] = np.tile(inp["nsa_k_norm_g"][l, 2], 2)
        pp[l, :, PP_KG0:PP_KG0 + 64] = rep(inp["nsa_k_norm_g"][l, 0])
        pp[l, :, PP_PE:PP_PE + 32] = inp["nsa_cmp_pe"][l].reshape(2, 16, 128).transpose(2, 0, 1).reshape(128, 32)
        pl[:, l, PL_ADAB:PL_ADAB + 48] = inp["ada_b"][l].reshape(48, 128).T
        pl[:, l, PL_GMIX:PL_GMIX + 8] = inp["norm_mix_g"][l].reshape(8, 128).T
        pl[:, l, PL_GFFN:PL_GFFN + 8] = inp["norm_ffn_g"][l].reshape(8, 128).T
    return pp, pl


def build_program(n_layers=DEPTH, n_seq=SEQ_PER_CORE, debug=False, stop=None):
    nc = bass.Bass("TRN2", target_bir_lowering=False)
    L = DEPTH
    din = lambda name, shape: nc.dram_tensor(name, list(shape), F32, kind="ExternalInput").ap()
    xT_d = din("xT", [n_seq, D, T])
    cT_d = din("cT", [128, KC, n_seq])
    pp_d = din("pp", [L, 128, NPP])
    pl_d = din("pl", [128, L, NPL])
    ctf_d = din("ctf", [128, NCF])
    ctb_d = din("ctb", [128, NCB])
    lk_d = din("lk", [4, T])
    lkc_d = din("lkc", [4, 128])
    rh_d = din("rh", [4, 8, T])
    sgw_d = din("sgwT", [L, 128, 4, 128])
    ada_w_d = din("ada_w", [L, D, 6 * D])
    w_in_d = din("w_in", [L, D, D_IN])
    w1_d = din("cmp_w1", [L, 2, 2048, 64])
    w2_d = din("cmp_w2", [L, 2, 64, 64])
    wb_d = din("w_branch", [L, 4, 512, D])
    wg_d = din("w_branch_gate", [L, 4, D, D])
    wo_d = din("w_out", [L, D, D])
    wfi_d = din("w_ffn_in", [L, D, 2 * D_FF])
    wfo_d = din("w_ffn_out", [L, D_FF, D])
    outT_d = nc.dram_tensor("outT", [n_seq, D, T], F32, kind="ExternalOutput").ap()
    xa_d = nc.dram_tensor("xres_a", [D, T], F32, kind="Internal").ap()
    xb_d = nc.dram_tensor("xres_b", [D, T], F32, kind="Internal").ap()
    dbg_d = None
    if debug:
        dbg_d = nc.dram_tensor("dbg", [4, 128, 4, T], F32, kind="ExternalOutput").ap()

    with ExitStack() as st:
        P = Prog(nc, st)
        XA = Buf(xa_d, "xa", 32)
        XB = Buf(xb_d, "xb", 32)
        OUT = Buf(outT_d, "out", 32)

        ctf = P.sbuf([128, NCF], F32, "ctf")
        ctb = P.sbuf([128, NCB], BF16, "ctb")
        pp = P.sbuf([128, NPP], F32, "pp")
        plb = P.sbuf([128, L, NPL], F32, "pl")
        sc = P.sbuf([128, KC, n_seq], F32, "sc")
        modT = P.sbuf([128, L, 48, n_seq], F32, "modT")
        der = P.sbuf([128, 2, 8], F32, "der")
        hT = P.sbuf([128, KC, T], BF16, "hT", nslots=32)
        oT = [P.sbuf([128, 4, T], BF16, f"oT{i}", nslots=16) for i in range(4)]
        mT = P.sbuf([128, KC, T], BF16, "mT", nslots=32)
        wbufs = [P.sbuf([128, KC, 512], BF16, f"wb{i}") for i in range(3)]
        psb = [P.psum([128, 512], F32, f"bank{i}") for i in range(8)]
        sq_b = [P.sbuf([128, 512], BF16, f"sq{i}") for i in range(2)]
        f32s = [P.sbuf([128, 512], F32, f"fs{i}") for i in range(4)]
        f32l = [P.sbuf([128, 512], F32, f"fl{i}") for i in range(2)]
        smalls = P.sbuf([128, 256], F32, "smalls", nslots=1)
        smsel = P.sbuf([128, 84], F32, "smsel")
        smcmb = P.sbuf([128, 12], F32, "smcmb")
        dtb = P.sbuf([128, NB, 8], F32, "dt")
        gsig = P.sbuf([128, NB, 24], F32, "gsig")
        kcaug = P.sbuf([128, 2, 128], BF16, "kcaug")
        nselb = P.sbuf([128, 128], BF16, "nselb")
        VC = P.sbuf([128, 2, 97], BF16, "VC")
        cbias = P.sbuf([64, 2], F32, "cbias")
        GTb = P.sbuf([64, 128], BF16, "GT")
        kcn = P.sbuf([128, 64], BF16, "kcn")
        peb = P.sbuf([128, 32], BF16, "peb")
        sgwb = P.sbuf([128, 4, 128], BF16, "sgwb")
        state = P.sbuf([128, 512], F32, "state")
        stateb = P.sbuf([128, 512], BF16, "stateb")
        tails = P.sbuf([128, 8, 3], F32, "tails")
        raw = [P.sbuf([128, 515], F32, f"raw{i}") for i in range(2)]
        zero_b = P.sbuf([1, 260], BF16, "zerob")
        negA = P.sbuf([128, 8], F32, "negA")
        epsb = P.sbuf([128, 1], F32, "epsb")

        identF = ctf[:, CF_IDENT:CF_IDENT + 128]
        triU = ctf[:, CF_TRIU:CF_TRIU + 128]
        onesF = ctf[:, CF_ONES:CF_ONES + 128]
        identB = ctb[:, CB_IDENT:CB_IDENT + 128]
        onesB = ctb[:, CB_ONES:CB_ONES + 128]
        NEGM = ctb[:, CB_NEGM:CB_NEGM + 128]
        ONES64 = ctb[:, CB_ONES64:CB_ONES64 + 128]
        ZEROL = ctb[:, CB_ZERO:CB_ZERO + 128]
        ZEROR = ctb[:, CB_ZERO:CB_ZERO + 260]

        def CS(buf, chunks):
            if isinstance(chunks, int):
                chunks = (chunks,)
            return (buf, [4 * c + i for c in chunks for i in range(4)])

        def TS(buf, chunk, tt):
            return (buf, 4 * chunk + tt)

        def TSA(buf, nch, tt):
            return (buf, [4 * c + tt for c in range(nch)])

        held = set()
        rr = {"ps": 0, "wb": 0, "sq": 0, "fs": 0, "raw": 0, "fl": 0}

        def ps_get(hold=False):
            for _ in range(16):
                i = rr["ps"] % 8
                rr["ps"] += 1
                if i not in held:
                    if hold:
                        held.add(i)
                    return psb[i]
            raise RuntimeError("no psum bank")

        def ps_release(b):
            held.discard(psb.index(b))

        def psbf(b):
            return b.t[:].bitcast(BF16)

        def wb_get():
            i = rr["wb"] % 3
            rr["wb"] += 1
            return wbufs[i]

        def sq_get():
            i = rr["sq"] % 2
            rr["sq"] += 1
            return sq_b[i]

        def fs_get():
            i = rr["fs"] % 4
            rr["fs"] += 1
            return f32s[i]

        def fl_get():
            i = rr["fl"] % 2
            rr["fl"] += 1
            return f32l[i]

        def wload(dst_ap, src_ap, wbuf):
            P.dma(dst_ap, src_ap, writes=[wbuf], eng="pool")

        def kpn(ap2d):
            return ap2d.rearrange("(k p) n -> p k n", p=128)

        def mt_view(s0, ns, parts, shape_tail, dtype=BF16):
            return _view(mT, s0, ns, parts, shape_tail, dtype)

        def _view(buf, s0, ns, parts, shape_tail, dtype=BF16):
            ap = buf.t[0:parts, s0:s0 + ns, :].rearrange("p a b -> p (a b)")
            if dtype == F32:
                ap = ap.bitcast(F32)
            n = int(np.prod(shape_tail))
            ap = ap[:, 0:n]
            if len(shape_tail) == 1:
                return ap
            if len(shape_tail) == 2:
                return ap.rearrange("p (a b) -> p a b", a=shape_tail[0])
            if len(shape_tail) == 3:
                return ap.rearrange("p (a b c) -> p a b c", a=shape_tail[0], b=shape_tail[1])
            raise ValueError

        P.dma(ctf[:], ctf_d, writes=[ctf])
        P.dma(ctb[:], ctb_d, writes=[ctb], eng="pool")
        P.dma(plb[:], pl_d, writes=[plb])
        P.dma(sc[:], cT_d, writes=[sc])
        P.op("dve", lambda e: e.memset(zero_b[:], 0.0), writes=[zero_b])
        P.op("dve", lambda e: e.memset(kcaug[:], 0.0), writes=[kcaug])
        P.op("dve", lambda e: e.memset(VC[:], 0.0), writes=[VC])
        P.op("dve", lambda e: e.memset(nselb[:], 0.0), writes=[nselb])
        P.dma(kcaug[64:68, 0, :], lkc_d, writes=[kcaug], eng="pool")
        P.dma(kcaug[64:68, 1, :], lkc_d, writes=[kcaug], eng="pool")
        P.copy("dve", VC[:, 0, 64:97], ctb[:, CB_OVL:CB_OVL + 33], reads=[ctb], writes=[VC])
        P.copy("dve", VC[:, 1, 64:97], ctb[:, CB_OVL:CB_OVL + 33], reads=[ctb], writes=[VC])
        P.act(sc[:], sc[:], AF.Silu, reads=[sc], writes=[sc])
        P.op("dve", lambda e: e.memset(epsb[:], EPS), writes=[epsb])
        stg = [(_view(hT, 4 * i, 4, 128, [KC, 512], F32), [CS(hT, range(4 * i, 4 * i + 4))]) for i in range(2)]
        nblk = 0
        for l in range(n_layers):
            for cb in range(12):
                sv, sacc = stg[nblk % 2]
                nblk += 1
                P.dma(sv, kpn(ada_w_d[l][:, cb * 512:(cb + 1) * 512]), writes=sacc)
                ps = ps_get()
                for m in range(4):
                    for k in range(KC):
                        P.mm(ps[:, m * n_seq:(m + 1) * n_seq], sv[:, k, m * 128:(m + 1) * 128], sc[:, k, :],
                             start=(k == 0), stop=(k == KC - 1), reads=sacc + [sc], writes=[ps])
                P.tt("dve", modT[:, l, cb * 4:(cb + 1) * 4, :],
                     ps[:, 0:4 * n_seq].rearrange("p (m s) -> p m s", m=4),
                     plb[:, l, PL_ADAB + cb * 4:PL_ADAB + (cb + 1) * 4].unsqueeze(2).broadcast_to([128, 4, n_seq]),
                     ALU.add, reads=[ps, plb], writes=[modT])

        def xsrc_ap(src, s, k, tt):
            if src is XA:
                return xa_d[k * 128:(k + 1) * 128, tt * TT:(tt + 1) * TT], [(XA, k * 4 + tt)]
            if src is XB:
                return xb_d[k * 128:(k + 1) * 128, tt * TT:(tt + 1) * TT], [(XB, k * 4 + tt)]
            return xT_d[s][k * 128:(k + 1) * 128, tt * TT:(tt + 1) * TT], []

        def norm_tile(xt, xacc, s, l, which, tt):
            A = der[:, which, :]
            shift_c0 = 0 if which == 0 else 24
            sl = slice(tt * TT, (tt + 1) * TT)
            ps = ps_get()
            for k in range(KC):
                sq = sq_get()
                P.act(sq[:], xt[:, k, :], AF.Square, reads=xacc, writes=[sq])
                P.mm(ps[:], onesB, sq[:], start=(k == 0), stop=(k == KC - 1), reads=[ctb, sq], writes=[ps])
            r = fl_get()
            P.act(r[:], ps[:], AF.Sqrt, reads=[ps], writes=[r], bias=EPS, scale=1.0 / D)
            P.op("dve", lambda e, r=r: e.reciprocal(r[:], r[:]), reads=[r], writes=[r])
            for k in range(KC):
                tmp = fs_get()
                P.stt(tmp[:], xt[:, k, :], A[:, k:k + 1], r[:], ALU.mult, ALU.mult,
                      reads=xacc + [der, r], writes=[tmp])
                P.act(hT[:, k, sl], tmp[:], AF.Identity, reads=[tmp, modT], writes=[TS(hT, k, tt)],
                      bias=modT[:, l, shift_c0 + k, s:s + 1], scale=1.0)

        def norm_phase(src, s, l, which, after_tile=None, stage=None):
            for tt in range(NTT):
                sl = slice(tt * TT, (tt + 1) * TT)
                half = tt % 2
                if stage is None:
                    xt = _view(mT, 4 * half, 4, 128, [KC, TT], F32)
                    xacc = [CS(mT, range(4 * half, 4 * half + 4))]
                else:
                    xt = _view(stage[half], 0, 4, 128, [KC, TT], F32)
                    xacc = [CS(stage[half], range(4))]
                if src is None:
                    P.dma(xt, kpn(xT_d[s][:, sl]), writes=xacc)
                else:
                    dsl = [(src, [k * 4 + tt for k in range(KC)])]
                    P.dma(xt, kpn(src.t[:, sl]), reads=dsl, writes=xacc)
                norm_tile(xt, xacc, s, l, which, tt)
                if after_tile is not None:
                    after_tile(tt)

        def proj_fm(wbuf, wcol0, m, tt, ps, parts=128):
            sl = slice(tt * TT, (tt + 1) * TT)
            for k in range(KC):
                P.mm(ps[0:m, :], wbuf[:, k, wcol0:wcol0 + m], hT[:, k, sl], start=(k == 0), stop=(k == KC - 1),
                     reads=[wbuf, TS(hT, k, tt)], writes=[ps])

        def layer(s, l, src, dst, dst_is_out):
            ppv = lambda c0, n: pp[:, c0:c0 + n]
            P.dma(pp[:], pp_d[l], writes=[pp])
            P.stt(der[:, 0, :], modT[:, l, 8:16, s], 1.0, plb[:, l, PL_GMIX:PL_GMIX + 8], ALU.add, ALU.mult,
                  reads=[modT, plb], writes=[der])
            P.stt(der[:, 1, :], modT[:, l, 32:40, s], 1.0, plb[:, l, PL_GFFN:PL_GFFN + 8], ALU.add, ALU.mult,
                  reads=[modT, plb], writes=[der])
            gate1 = lambda k: modT[:, l, 16 + k, s:s + 1]
            gate2 = lambda k: modT[:, l, 40 + k, s:s + 1]
            P.act(negA[:], ppv(PP_ALOG, 8), AF.Exp, reads=[pp], writes=[negA])
            P.ts("dve", negA[:], negA[:], -1.0, None, ALU.mult, reads=[negA], writes=[negA])
            P.copy("dve", peb[:], ppv(PP_PE, 32), reads=[pp], writes=[peb])

            qaug = _view(mT, 0, 2, 128, [8, TT])
            QA = [CS(mT, (0, 1))]
            kaug_s = _view(mT, 2, 2, 128, [2, T])
            KS = [CS(mT, (2, 3))]
            kaug_w = _view(mT, 4, 2, 128, [2, T])
            KW = [CS(mT, (4, 5))]
            MB = _view(mT, 6, 2, 128, [2, T])
            MBA = [CS(mT, (6, 7))]
            vslc = _view(oT[0], 0, 2, 128, [NB, 2, 66])
            VS = [CS(oT[0], (0, 1))]
            vwin = _view(oT[0], 2, 2, 128, [NB, 2, 66])
            VW = [CS(oT[0], (2, 3))]
            cmpk = _view(oT[1], 0, 2, 64, [2, T])
            CK = [CS(oT[1], (0, 1))]
            cmpv = _view(oT[1], 2, 2, 64, [2, T])
            CV = [CS(oT[1], (2, 3))]
            ocomb = _view(oT[2], 0, 2, 128, [4, 8, 64], F32)
            OC = [CS(oT[2], (0, 1))]
            cmpP = _view(oT[2], 2, 1, 128, [4, TT])
            CP = [CS(oT[2], 2)]
            PTs = _view(oT[2], 3, 1, 128, [4, TT])
            PTA = [CS(oT[2], 3)]

            P.op("dve", lambda e: e.memset(kaug_s[64:128, :, :], 0.0), writes=KS)
            P.op("dve", lambda e: e.memset(kaug_w[64:128, :, :], 0.0), writes=KW)
            for g in range(2):
                P.dma(kaug_s[64:68, g, :], lk_d, writes=KS, eng="pool")
                P.dma(kaug_w[64:68, g, :], lk_d, writes=KW, eng="pool")
            P.op("dve", lambda e: e.memset(vslc[:, :, :, 64:65], 1.0), writes=VS)
            P.op("dve", lambda e: e.memset(vwin[:, :, :, 64:65], 1.0), writes=VW)

            if stop == "tm0":
                return
            wsm = wb_get()
            W = w_in_d[l]
            wload(wsm[:, :, 0:128], kpn(W[:, C_VSLC:C_VSLC + 128]), wsm)
            wload(wsm[:, :, 128:256], kpn(W[:, C_VWIN:C_VWIN + 128]), wsm)
            wload(wsm[:, :, 256:384], kpn(W[:, C_GATES - 104:C_GATES + 24]), wsm)
            wload(wsm[:, :, 384:512], kpn(W[:, C_DT - 120:C_DT + 8]), wsm)
            def tm_tile(tt):
                for tb in range(4 * tt, 4 * tt + 4):
                    bs = slice(tb * 128, (tb + 1) * 128)
                    ps = ps_get()
                    for k in range(KC):
                        P.mm(ps[:, 0:512], hT[:, k, bs], wsm[:, k, 0:512], start=(k == 0), stop=(k == KC - 1),
                             reads=[TS(hT, k, tb // 4), wsm], writes=[ps])
                    import os as _os
                    _sk = _os.environ.get("K_SKIP", "")
                    if "a" not in _sk:
                        P.copy("dve", vslc[:, tb, :, 0:64], ps[:, 0:128].rearrange("p (g d) -> p g d", g=2),
                               reads=[ps], writes=VS)
                    if "b" not in _sk:
                        P.copy("dve", vwin[:, tb, :, 0:64], ps[:, 128:256].rearrange("p (g d) -> p g d", g=2),
                               reads=[ps], writes=VW)
                    if "c" not in _sk:
                        P.copy("dve", gsig[:, tb, :], ps[:, 360:384], reads=[ps], writes=[gsig])
                    if "d" not in _sk:
                        P.tt("dve", dtb[:, tb, :], ps[:, 504:512], ppv(PP_DTB, 8), ALU.add, reads=[ps, pp], writes=[dtb])

            if stop == "tmsmall":
                return
            wk = wb_get()
            wload(wk[:, :, 0:128], kpn(W[:, C_KCMP:C_KCMP + 128]), wk)
            wload(wk[:, :, 128:256], kpn(W[:, C_VCMP:C_VCMP + 128]), wk)
            wload(wk[:, :, 256:384], kpn(W[:, C_KSLC:C_KSLC + 128]), wk)
            wload(wk[:, :, 384:512], kpn(W[:, C_KWIN:C_KWIN + 128]), wk)
            wx7 = wb_get()
            wload(wx7[:, :, 0:128], kpn(W[:, C_Q + 448:C_Q + 576]), wx7)
            wload(wx7[:, :, 128:256], kpn(W[:, C_KWIN + 64:C_KWIN + 192]), wx7)

            def norm64(ps, gcol, out_ap, out_acc):
                sq = sq_get()
                P.act(sq[:, :], ps[:, :], AF.Square, reads=[ps], writes=[sq])
                ps2 = ps_get()
                P.mm(ps2[:, :], ONES64, sq[:, :], reads=[ctb, sq], writes=[ps2])
                r = fs_get()
                P.act(r[0:64, :], ps2[0:64, :], AF.Ln, reads=[ps2, epsb], writes=[r], bias=epsb[0:64, 0:1], scale=1.0 / 64)
                P.act(r[0:64, :], r[0:64, :], AF.Exp, reads=[r], writes=[r], scale=-0.5)
                P.stt(out_ap, ps[0:64, :], pp[0:64, gcol:gcol + 1], r[0:64, :], ALU.mult, ALU.mult,
                      reads=[ps, pp, r], writes=out_acc)

            def kproj_tile(tt):
                for which in range(4):
                    for g in range(2):
                        sl = slice(tt * TT, (tt + 1) * TT)
                        ps = ps_get()
                        if which == 3 and g == 1:
                            proj_fm(wx7, 128, 128, tt, ps)
                        else:
                            proj_fm(wk, which * 128 + g * 64, 128, tt, ps)
                        if which == 0:
                            P.copy("act", cmpk[:, g, sl], ps[0:64, :], reads=[ps], writes=CK)
                        elif which == 1:
                            P.copy("act", cmpv[:, g, sl], ps[0:64, :], reads=[ps], writes=CV)
                        elif which == 2:
                            norm64(ps, PP_KG12, kaug_s[0:64, g, sl], KS)
                        else:
                            norm64(ps, PP_KG12 + 1, kaug_w[0:64, g, sl], KW)

            def after_n1(tt):
                tm_tile(tt)
                kproj_tile(tt)
            norm_phase(None if src is None else src, s, l, 0, after_tile=after_n1, stage=(oT[2], oT[3]))
            P.act(gsig[:], gsig[:], AF.Sigmoid, reads=[gsig], writes=[gsig])
            P.act(dtb[:], dtb[:], AF.Exp, reads=[dtb], writes=[dtb])
            P.act(dtb[:], dtb[:], AF.Ln, reads=[dtb], writes=[dtb], bias=1.0, scale=1.0)
            for kv in range(2):
                wcm = wb_get()
                wcf = wcm[:].rearrange("p k n -> p (k n)")
                w1b_v = wcf[0:64, 0:2048].rearrange("p (l e) -> p l e", l=32)
                w1f_v = wcf[:, 2048:3072].rearrange("p (c e) -> p c e", c=16)
                w2b_v = wcf[0:64, 3072:3136]
                w1b = w1f = w2b = wcm
                P.dma(w1b_v, w1_d[l, kv].rearrange("(l d) e -> d l e", d=64), writes=[wcm], eng="pool")
                P.dma(w1f_v, w1_d[l, kv].rearrange("(c p) e -> p c e", p=128), writes=[wcm], eng="pool")
                P.dma(w2b_v, w2_d[l, kv], writes=[wcm], eng="pool")
                psc = ps_get()
                for c in range(16):
                    P.mm(psc[0:64, 0:1], w1f_v[:, c, :], peb[:, kv * 16 + c:kv * 16 + c + 1], start=(c == 0), stop=(c == 15),
                         reads=[w1f, peb], writes=[psc])
                P.copy("dve", cbias[:, kv:kv + 1], psc[0:64, 0:1], reads=[psc], writes=[cbias])
                rawb, racc = (cmpk, CK) if kv == 0 else (cmpv, CV)
                for g in range(2):
                    ps = ps_get()
                    for li in range(32):
                        P.mm(ps[0:64, 0:127], w1b_v[:, li, :], rawb[:, g, li:li + 16 * 126 + 1:16], start=(li == 0), stop=(li == 31),
                             reads=[w1b] + racc, writes=[ps])
                    P.act(GTb[:, 0:127], ps[0:64, 0:127], AF.Gelu_apprx_tanh, reads=[ps, cbias], writes=[GTb],
                          bias=cbias[:, kv:kv + 1], scale=1.0)
                    ps2 = ps_get()
                    P.mm(ps2[0:127, 0:64], GTb[:, 0:127], w2b_v, reads=[GTb, w2b], writes=[ps2])
                    if kv == 0:
                        junk = fs_get()
                        P.act(junk[0:127, 0:64], ps2[0:127, 0:64], AF.Square, reads=[ps2], writes=[junk, smalls],
                              accum_out=smalls[0:127, 0:1])
                        P.act(smalls[0:127, 1:2], smalls[0:127, 0:1], AF.Sqrt, reads=[smalls], writes=[smalls],
                              bias=EPS, scale=1.0 / 64)
                        P.op("dve", lambda e: e.reciprocal(smalls[0:127, 2:3], smalls[0:127, 1:2]), reads=[smalls], writes=[smalls])
                        P.stt(kcn[0:127, :], ps2[0:127, 0:64], smalls[0:127, 2:3], pp[0:127, PP_KG0:PP_KG0 + 64],
                              ALU.mult, ALU.mult, reads=[ps2, smalls, pp], writes=[kcn])
                        pst = ps_get()
                        P.tr(psbf(pst)[0:64, 0:127], kcn[0:127, :], identB[0:127, 0:127], reads=[kcn, ctb], writes=[pst])
                        P.copy("dve", kcaug[0:64, g, 0:127], psbf(pst)[0:64, 0:127], reads=[pst], writes=[kcaug])
                    else:
                        P.copy("dve", VC[0:127, g, 0:64], ps2[0:127, 0:64], reads=[ps2], writes=[VC])

            if stop == "compress":
                return
            wb_get()
            wq = wb_get()
            wload(wq[:], kpn(W[:, C_Q:C_Q + 512]), wq)
            gs4 = gsig[:].rearrange("p b (h i) -> p b h i", i=3)
            m2v = ctf[:, CF_M2:CF_M2 + 512].rearrange("p (b j) -> p b j", b=16)
            Ev = ctb[:, CB_E:CB_E + 2048].rearrange("p (c k) -> p c k", c=16)
            DMv = ctb[:, CB_DM:CB_DM + 2048].rearrange("p (i t) -> p i t", i=4)
            WMv = ctb[:, CB_WM:CB_WM + 384]
            CMv = ctb[:, CB_CM:CB_CM + T]
            sm = smalls
            qaugs = [qaug, _view(oT[1], 0, 2, 128, [8, TT])]
            QAs = [QA, [CS(oT[1], (0, 1))]]
            ocombs = [ocomb, _view(oT[1], 2, 2, 128, [4, 8, 64], F32)]
            OCs = [OC, [CS(oT[1], (2, 3))]]

            for qq in range(2):
                P.op("dve", lambda e, qq=qq: e.memset(qaugs[qq][64:128, :, :], 0.0), writes=QAs[qq])

            def MBacc(g, tt_):
                return [(mT, 4 * (6 + g) + tt_)]

            def prep_steps(tt):
                sl = slice(tt * TT, (tt + 1) * TT)
                qa, QAa = qaugs[tt % 2], QAs[tt % 2]
                oc, OCa = ocombs[tt % 2], OCs[tt % 2]
                nmax = min(127, 32 * tt + 31)
                steps = []
                steps.append(lambda: P.dma(qa[64:68, :, :], rh_d[:, :, sl], writes=QAa, eng="pool"))

                def qstep(h):
                    ps = ps_get()
                    if h < 7:
                        proj_fm(wq, h * 64, 128, tt, ps)
                    else:
                        proj_fm(wx7, 0, 128, tt, ps)
                    norm64(ps, PP_QG, qa[0:64, h, :], QAa)
                for h in range(8):
                    steps.append(lambda h=h: qstep(h))

                def cstep(g, r):
                    h = 4 * g + r
                    pss = ps_get()
                    P.mm(pss[:, :], kcaug[:, g, :], qa[:, h, :], start=True, stop=False,
                         reads=[kcaug] + QAa, writes=[pss])
                    P.mm(pss[:, :], identB, CMv[:, sl], start=False, stop=True,
                         reads=[ctb], writes=[pss])
                    P.act(cmpP[:, r, :], pss[:, :], AF.Exp, reads=[pss], writes=[(oT[2], 8 + r)], scale=0.125)

                def sstep(g, qb):
                    tb = 4 * tt + qb
                    qs = slice(qb * 128, (qb + 1) * 128)
                    pso = ps_get()
                    pso_v = pso[:, 0:388].rearrange("p (r c) -> p r c", r=4)
                    for r in range(4):
                        P.mm(pso_v[:, r, :], cmpP[:, r, qs], VC[:, g, :], reads=CP + [VC], writes=[pso])
                    sp_ = smsel
                    P.ts("dve", sp_[:, 0:4], pso_v[:, :, 96], 1e-30, None, ALU.max, reads=[pso], writes=[sp_])
                    P.op("dve", lambda e: e.reciprocal(sp_[:, 4:8], sp_[:, 0:4]), reads=[sp_], writes=[sp_])
                    tmp = fs_get()
                    tv = tmp[:, 0:128].rearrange("p (r j) -> p r j", r=4)
                    P.tt("dve", tv, pso_v[:, :, 64:96], sp_[:, 4:8].unsqueeze(2).broadcast_to([128, 4, 32]), ALU.mult,
                         reads=[pso, sp_], writes=[tmp])
                    P.op("dve", lambda e, tv=tv: e.tensor_reduce(sp_[:, 8:40], tv.rearrange("p r j -> p j r"), AX.X, ALU.add),
                         reads=[tmp], writes=[sp_])
                    P.tt("dve", sp_[:, 40:44], sp_[:, 4:8], gs4[:, tb, 4 * g:4 * g + 4, 0], ALU.mult, reads=[sp_, gsig], writes=[sp_])
                    P.tt("dve", oc[:, qb, 4 * g:4 * g + 4, :], pso_v[:, :, 0:64],
                         sp_[:, 40:44].unsqueeze(2).broadcast_to([128, 4, 64]), ALU.mult, reads=[pso, sp_], writes=OCa)
                    P.stt(sp_[:, 44:76], m2v[:, tb, :], 0.0, sp_[:, 8:40], ALU.is_equal, ALU.mult, reads=[sp_, ctf], writes=[sp_])
                    P.tt("dve", sp_[:, 44:76], sp_[:, 44:76], m2v[:, tb, :], ALU.add, reads=[sp_, ctf], writes=[sp_])
                    P.op("dve", lambda e: e.max(sp_[:, 76:84], sp_[:, 44:76]), reads=[sp_], writes=[sp_])
                    nsel = nselb
                    P.ts("dve", nsel[:, 0:32], sp_[:, 44:76], sp_[:, 83:84], None, ALU.is_lt, reads=[sp_], writes=[nsel])
                    pst = ps_get()
                    P.tr(psbf(pst)[:, 0:128], nsel[:, :], identB, reads=[nsel, ctb], writes=[pst])
                    P.ts("dve", MB[:, g, tb * 128:(tb + 1) * 128], psbf(pst)[:, 0:128], NEG, None, ALU.mult,
                         reads=[pst], writes=MBacc(g, tt))
                for g in range(2):
                    for r in range(4):
                        steps.append(lambda g=g, r=r: cstep(g, r))
                    for qb in range(4):
                        steps.append(lambda g=g, qb=qb: sstep(g, qb))
                return steps

            for st_ in prep_steps(0):
                st_()
            for tt in range(NTT):
                sl = slice(tt * TT, (tt + 1) * TT)
                qaug, QA = qaugs[tt % 2], QAs[tt % 2]
                ocomb, OC = ocombs[tt % 2], OCs[tt % 2]
                nxt_steps = prep_steps(tt + 1) if tt + 1 < NTT else []
                items = []
                for h in range(8):
                    for branch in (1, 2):
                        if branch == 1:
                            chunks = list(range(0, 4 * tt + 4))
                        else:
                            chunks = list(range(max(0, 4 * tt - 2), 4 * tt + 4))
                        for ci, c in enumerate(chunks):
                            items.append((h, branch, ci, c, ci == len(chunks) - 1))
                accs = {}
                ptn = [0]

                def stage1(it):
                    h, branch, ci, c, last = it
                    g = h // 4
                    if ci == 0:
                        acc = ps_get(hold=True)
                        accs[(h, branch)] = acc
                        P.mm(acc[:, 0:260], ZEROL, ZEROR, start=True, stop=False,
                             reads=[ctb], writes=[acc], skip_group_check=True)
                    ks = slice(c * 128, (c + 1) * 128)
                    i = c - 4 * tt
                    pss = ps_get()
                    pi = ptn[0] % 4
                    ptn[0] += 1
                    pt = PTs[:, pi, :]
                    pta = [(oT[2], 12 + pi)]
                    if branch == 1:
                        P.mm(pss[:, :], kaug_s[:, g, ks], qaug[:, h, :], start=True, stop=False,
                             reads=KS + QA, writes=[pss])
                        P.mm(pss[:, :], Ev[:, c, :], MB[:, g, sl], start=False, stop=(i < 0),
                             reads=[ctb] + MBacc(g, tt), writes=[pss])
                        if i >= 0:
                            P.mm(pss[:, :], identB, DMv[:, i, :], start=False, stop=True, reads=[ctb], writes=[pss])
                        P.act(pt, pss[:, :], AF.Exp, reads=[pss], writes=pta, scale=0.125)
                        return (pt, pta, None)
                    qlo = max(i, 0)
                    qhi = min(i + 2, 3)
                    nq = qhi - qlo + 1
                    rel_lo = qlo - i
                    N = nq * 128
                    P.mm(pss[:, 0:N], kaug_w[:, g, ks], qaug[:, h, qlo * 128:(qhi + 1) * 128], start=True, stop=False,
                         reads=KW + QA, writes=[pss])
                    P.mm(pss[:, 0:N], identB, WMv[:, rel_lo * 128:(rel_lo + nq) * 128], start=False, stop=True,
                         reads=[ctb], writes=[pss])
                    P.act(pt[:, 0:N], pss[:, 0:N], AF.Exp, reads=[pss], writes=pta, scale=0.125)
                    return (pt, pta, (qlo, nq))

                def stage2(it, st1):
                    h, branch, ci, c, last = it
                    g = h // 4
                    pt, pta, wq_ = st1
                    acc = accs[(h, branch)]
                    acc_v = acc[:, 0:260].rearrange("p (q c) -> p q c", q=4)
                    i = c - 4 * tt
                    if branch == 1:
                        for qb in range(max(i, 0), 4):
                            P.mm(acc_v[:, qb, :], pt[:, qb * 128:(qb + 1) * 128], vslc[:, c, g, 0:65], start=False, stop=False,
                                 reads=pta + VS, writes=[acc], skip_group_check=True)
                    else:
                        qlo, nq = wq_
                        for qi in range(nq):
                            qb = qlo + qi
                            P.mm(acc_v[:, qb, :], pt[:, qi * 128:(qi + 1) * 128], vwin[:, c, g, 0:65], start=False, stop=False,
                                 reads=pta + VW, writes=[acc], skip_group_check=True)
                    if last:
                        o_ = 0
                        sc_ = smcmb
                        P.ts("dve", sc_[:, o_:o_ + 4], acc_v[:, :, 64], 1e-30, None, ALU.max, reads=[acc], writes=[sc_])
                        P.op("dve", lambda e: e.reciprocal(sc_[:, o_ + 4:o_ + 8], sc_[:, o_:o_ + 4]), reads=[sc_], writes=[sc_])
                        P.tt("dve", sc_[:, o_ + 8:o_ + 12], sc_[:, o_ + 4:o_ + 8], gs4[:, 4 * tt:4 * tt + 4, h, branch], ALU.mult,
                             reads=[sc_, gsig], writes=[sc_])
                        tmp = fs_get()
                        tv = tmp[:, 0:256].rearrange("p (q d) -> p q d", q=4)
                        P.tt("dve", tv, acc_v[:, :, 0:64], sc_[:, o_ + 8:o_ + 12].unsqueeze(2).broadcast_to([128, 4, 64]), ALU.mult,
                             reads=[acc, sc_], writes=[tmp])
                        P.tt("dve", ocomb[:, :, h, :], ocomb[:, :, h, :], tv, ALU.add, reads=OC + [tmp], writes=OC)
                        ps_release(acc)

                LOOK = 2
                pend = [stage1(items[q]) for q in range(min(LOOK, len(items)))]
                for ii, it in enumerate(items):
                    if ii + LOOK < len(items):
                        pend.append(stage1(items[ii + LOOK]))
                    stage2(it, pend.pop(0))
                    if nxt_steps and ii % 2 == 1:
                        nxt_steps.pop(0)()
                while nxt_steps:
                    nxt_steps.pop(0)()
                for qb in range(4):
                    tb = 4 * tt + qb
                    pst = ps_get()
                    oc2 = ocomb[:, qb, :, :].rearrange("p h d -> p (h d)")
                    for cc in range(4):
                        P.tr(pst[:, cc * 128:(cc + 1) * 128], oc2[:, cc * 128:(cc + 1) * 128], identF, reads=OC + [ctf], writes=[pst])
                    P.copy("act", oT[3][:, :, tb * 128:(tb + 1) * 128], pst[:, :].rearrange("p (c t) -> p c t", c=4),
                           reads=[pst], writes=[TSA(oT[3], 4, tt)])

            if stop == "nsa":
                return
            wz = wb_get()
            wload(wz[:], kpn(W[:, C_Z:C_Z + 512]), wz)
            wx = [wb_get(), wb_get()]
            wload(wx[0][:], kpn(W[:, C_XBC:C_XBC + 512]), wx[0])
            wload(wx[1][:], kpn(W[:, C_XBC + 512:C_XBC + 1024]), wx[1])
            xbcT = _view(mT, 0, 2, 128, [8, TT])
            XBC = [CS(mT, (0, 1))]
            dtab = _view(mT, 2, 2, 128, [8, 128], F32)
            DTA = [CS(mT, (2, 3))]
            LT = _view(mT, 4, 2, 128, [8, 128], F32)
            LTA = [CS(mT, (4, 5))]
            Mb = _view(mT, 6, 1, 128, [8, 128])
            MA = [CS(mT, 6)]
            s7 = _view(mT, 7, 1, 128, [2048])
            S7 = [CS(mT, 7)]
            xdt = s7[:, 0:512].rearrange("p (h d) -> p h d", h=8)
            xdd = s7[:, 512:1024].rearrange("p (h d) -> p h d", h=8)
            xs_sb = s7[:, 1024:1536].rearrange("p (h d) -> p h d", h=8)
            Btm = s7[:, 1536:1792]
            P.op("dve", lambda e: e.memset(state[:], 0.0), writes=[state])
            P.op("dve", lambda e: e.memset(stateb[:], 0.0), writes=[stateb])
            P.op("dve", lambda e: e.memset(tails[:], 0.0), writes=[tails])
            convw = pp[:, PP_CONVW:PP_CONVW + 32].rearrange("p (c k) -> p c k", c=8)
            dsk = pp[:, PP_DSKIP:PP_DSKIP + 8]
            for tt in range(NTT):
                for ch in range(8):
                    ps = ps_get()
                    proj_fm(wx[ch // 4], (ch % 4) * 128, 128, tt, ps)
                    rw = raw[rr["raw"] % 2]
                    rr["raw"] += 1
                    P.copy("dve", rw[:, 0:3], tails[:, ch, :], reads=[tails], writes=[rw])
                    P.copy("act", rw[:, 3:515], ps[:, :], reads=[ps], writes=[rw])
                    P.copy("dve", tails[:, ch, :], rw[:, 512:515], reads=[rw], writes=[tails])
                    acc = fs_get()
                    P.ts("dve", acc[:], rw[:, 0:512], convw[:, ch, 0:1], None, ALU.mult, reads=[rw, pp], writes=[acc])
                    for kk in range(1, 4):
                        P.stt(acc[:], rw[:, kk:kk + 512], convw[:, ch, kk:kk + 1], acc[:], ALU.mult, ALU.add,
                              reads=[rw, pp, acc], writes=[acc])
                    P.act(xbcT[:, ch, :], acc[:], AF.Silu, reads=[acc, pp], writes=XBC,
                          bias=pp[:, PP_CONVB + ch:PP_CONVB + ch + 1], scale=1.0)
                for lt in range(4):
                    tb = 4 * tt + lt
                    cs = slice(lt * 128, (lt + 1) * 128)
                    bs = slice(tb * 128, (tb + 1) * 128)
                    psx = ps_get()
                    pxb = psbf(psx)
                    for ch in range(4):
                        P.tr(pxb[:, ch * 128:(ch + 1) * 128], xbcT[:, ch, cs], identB, reads=XBC + [ctb], writes=[psx])
                    psB = ps_get()
                    pBb = psbf(psB)
                    for g in range(2):
                        P.tr(pBb[:, g * 128:(g + 1) * 128], xbcT[:, 4 + g, cs], identB, reads=XBC + [ctb], writes=[psB])
                    P.tt("dve", sm[:, 120:128], dtb[:, tb, :], negA[:], ALU.mult, reads=[dtb, negA], writes=[sm])
                    P.copy("dve", dtab, sm[:, 120:128].unsqueeze(2).broadcast_to([128, 8, 128]), reads=[sm], writes=DTA)
                    psa = ps_get()
                    P.mm(psa[:, 0:8], triU, sm[:, 120:128], reads=[ctf, sm], writes=[psa])
                    P.mm(psa[:, 8:16], onesF, sm[:, 120:128], reads=[ctf, sm], writes=[psa])
                    P.copy("dve", sm[:, 128:136], psa[:, 0:8], reads=[psa], writes=[sm])
                    P.tt("dve", sm[:, 136:144], psa[:, 8:16], sm[:, 128:136], ALU.subtract, reads=[psa, sm], writes=[sm])
                    P.copy("dve", sm[:, 144:152], psa[:, 8:16], reads=[psa], writes=[sm])
                    P.ts("dve", sm[:, 176:184], sm[:, 128:136], -1.0, None, ALU.mult, reads=[sm], writes=[sm])
                    P.act(sm[:, 152:176], sm[:, 128:152], AF.Exp, reads=[sm], writes=[sm])
                    ea = sm[:, 152:160]
                    dec = sm[:, 160:168]
                    cdec = sm[:, 168:176]
                    psl = [ps_get(), ps_get()]
                    for h in range(8):
                        pl_ = psl[h // 4]
                        o_ = pl_[:, (h % 4) * 128:(h % 4 + 1) * 128]
                        P.mm(o_, dtab[:, h, :], triU, start=True, stop=False, reads=DTA + [ctf], writes=[pl_])
                        P.mm(o_, identB, NEGM, start=False, stop=True, reads=[ctb], writes=[pl_])
                    for h in range(8):
                        pl_ = psl[h // 4]
                        o_ = pl_[:, (h % 4) * 128:(h % 4 + 1) * 128]
                        P.act(LT[:, h, :], o_, AF.Exp, reads=[pl_, sm], writes=LTA, bias=sm[:, 176 + h:177 + h], scale=1.0)
                    psc_ = ps_get()
                    for g in range(2):
                        P.mm(psc_[:, g * 128:(g + 1) * 128], xbcT[:, 4 + g, cs], xbcT[:, 6 + g, cs], reads=XBC, writes=[psc_])
                    cbT = fs_get()
                    P.copy("act", cbT[:, 0:256], psc_[:, 0:256], reads=[psc_], writes=[cbT])
                    P.tt("dve", Mb.rearrange("p (g r) l -> p g r l", g=2),
                         LT.rearrange("p (g r) l -> p g r l", g=2),
                         cbT[:, 0:256].rearrange("p (g l) -> p g l", g=2).unsqueeze(2).broadcast_to([128, 2, 4, 128]),
                         ALU.mult, reads=LTA + [cbT], writes=MA)
                    pxv = pxb[:, 0:512].rearrange("p (h d) -> p h d", h=8)
                    P.tt("dve", xdt, pxv, dtb[:, tb, :].unsqueeze(2).broadcast_to([128, 8, 64]), ALU.mult,
                         reads=[psx, dtb], writes=S7)
                    P.copy("act", Btm, pBb[:, 0:256], reads=[psB], writes=S7)
                    P.tt("dve", xdd, xdt, dec.unsqueeze(2).broadcast_to([128, 8, 64]), ALU.mult, reads=S7 + [sm], writes=S7)
                    psy = ps_get()
                    for h in range(8):
                        P.mm(psy[:, h * 64:(h + 1) * 64], Mb[:, h, :], xdt[:, h, :], reads=MA + S7, writes=[psy])
                    pso = ps_get()
                    for g in range(2):
                        P.mm(pso[:, g * 256:(g + 1) * 256], xbcT[:, 6 + g, cs], stateb[:, g * 256:(g + 1) * 256],
                             reads=XBC + [stateb], writes=[pso])
                    y1 = fs_get()
                    y1v = y1[:].rearrange("p (h d) -> p h d", h=8)
                    P.tt("dve", y1v, pso[:, :].rearrange("p (h d) -> p h d", h=8),
                         ea.unsqueeze(2).broadcast_to([128, 8, 64]), ALU.mult, reads=[pso, sm], writes=[y1])
                    P.tt("dve", y1[:], y1[:], psy[:, :], ALU.add, reads=[y1, psy], writes=[y1])
                    y2 = fs_get()
                    P.tt("dve", y2[:].rearrange("p (h d) -> p h d", h=8), pxv,
                         dsk.unsqueeze(2).broadcast_to([128, 8, 64]), ALU.mult, reads=[psx, pp], writes=[y2])
                    P.tt("dve", y1[:], y1[:], y2[:], ALU.add, reads=[y1, y2], writes=[y1])
                    pst_ = ps_get()
                    for g in range(2):
                        P.mm(pst_[:, g * 256:(g + 1) * 256], Btm[:, g * 128:(g + 1) * 128],
                             xdd[:, 4 * g:4 * g + 4, :].rearrange("p h d -> p (h d)"), reads=S7, writes=[pst_])
                    stv = state[:].rearrange("p (h d) -> p h d", h=8)
                    P.tt("dve", stv, stv, cdec.unsqueeze(2).broadcast_to([128, 8, 64]), ALU.mult, reads=[state, sm], writes=[state])
                    P.tt("dve", state[:], state[:], pst_[:, :], ALU.add, reads=[state, pst_], writes=[state])
                    P.copy("act", stateb[:], state[:], reads=[state], writes=[stateb])
                    psz = ps_get()
                    for k in range(KC):
                        P.mm(psz[:, :], hT[:, k, bs], wz[:, k, :], start=(k == 0), stop=(k == KC - 1), reads=[TS(hT, k, tt), wz], writes=[psz])
                    zs = fs_get()
                    P.act(zs[:], psz[:, :], AF.Exp, reads=[psz], writes=[zs], scale=-1.0)
                    P.ts("dve", zs[:], zs[:], 1.0, None, ALU.add, reads=[zs], writes=[zs])
                    P.op("dve", lambda e, zs=zs: e.reciprocal(zs[:], zs[:]), reads=[zs], writes=[zs])
                    P.tt("dve", zs[:], zs[:], psz[:, :], ALU.mult, reads=[zs, psz], writes=[zs])
                    P.tt("dve", y1[:], y1[:], zs[:], ALU.mult, reads=[y1, zs], writes=[y1])
                    for g in range(2):
                        P.act(zs[:, g * 256:(g + 1) * 256], y1[:, g * 256:(g + 1) * 256], AF.Square, reads=[y1], writes=[zs, sm],
                              accum_out=sm[:, 184 + g:185 + g])
                    P.act(sm[:, 186:188], sm[:, 184:186], AF.Ln, reads=[sm, epsb], writes=[sm], bias=epsb[:, 0:1], scale=1.0 / 256)
                    P.act(sm[:, 188:190], sm[:, 186:188], AF.Exp, reads=[sm], writes=[sm], scale=-0.5)
                    oa = sq_get()
                    for g in range(2):
                        P.stt(oa[:, g * 256:(g + 1) * 256], y1[:, g * 256:(g + 1) * 256], sm[:, 188 + g:189 + g],
                              pp[:, PP_SSDNG + g * 256:PP_SSDNG + (g + 1) * 256], ALU.mult, ALU.mult,
                              reads=[y1, sm, pp], writes=[oa])
                    pso2 = ps_get()
                    po2 = psbf(pso2)
                    for ch in range(4):
                        P.tr(po2[:, ch * 128:(ch + 1) * 128], oa[:, ch * 128:(ch + 1) * 128], identB, reads=[oa, ctb], writes=[pso2])
                    P.copy("act", oT[0][:, :, bs], po2[:, 0:512].rearrange("p (c t) -> p c t", c=4), reads=[pso2], writes=[TSA(oT[0], 4, tt)])

            if stop == "ssd":
                return
            wB = [wb_get(), wb_get(), wb_get()]
            for i_, c0 in enumerate((C_B, C_C, C_HX)):
                wload(wB[i_][:], kpn(W[:, c0:c0 + 512]), wB[i_])
            chx = _view(mT, 0, 3, 128, [2050], F32)
            CHX = [CS(mT, (0, 1, 2))]
            accB = _view(mT, 3, 2, 128, [2048], F32)
            ACB = [CS(mT, (3, 4))]
            bsb = _view(mT, 5, 1, 128, [2048])
            BSB = [CS(mT, 5)]
            P.op("dve", lambda e: e.memset(chx[:, 0:2], 0.0), writes=CHX)
            scw = pp[:, PP_SCW:PP_SCW + 12].rearrange("p (c k) -> p c k", c=4)
            for cc in range(4):
                for tt in range(NTT):
                    sl = slice(tt * TT, (tt + 1) * TT)
                    psb_ = ps_get()
                    proj_fm(wB[0], cc * 128, 128, tt, psb_)
                    psc_ = ps_get()
                    proj_fm(wB[1], cc * 128, 128, tt, psc_)
                    psh = ps_get()
                    proj_fm(wB[2], cc * 128, 128, tt, psh)
                    P.copy("act", bsb[:, sl], psb_[:, :], reads=[psb_], writes=BSB)
                    hx = fs_get()
                    P.copy("act", hx[:], psh[:, :], reads=[psh], writes=[hx])
                    P.tt("dve", chx[:, 2 + tt * TT:2 + (tt + 1) * TT], psc_[:, :], hx[:], ALU.mult, reads=[psc_, hx], writes=CHX)
                P.ts("dve", accB, chx[:, 0:2048], scw[:, cc, 0:1], None, ALU.mult, reads=CHX + [pp], writes=ACB)
                P.stt(accB, chx[:, 1:2049], scw[:, cc, 1:2], accB, ALU.mult, ALU.add, reads=CHX + [pp] + ACB, writes=ACB)
                P.stt(accB, chx[:, 2:2050], scw[:, cc, 2:3], accB, ALU.mult, ALU.add, reads=CHX + [pp] + ACB, writes=ACB)
                P.tt("dve", oT[1][:, cc, :], accB, bsb, ALU.mult, reads=ACB + BSB, writes=[CS(oT[1], cc)])

            if stop == "sconv":
                return
            wC = [wb_get(), wb_get()]
            wload(wC[0][:], kpn(W[:, C_GU:C_GU + 512]), wC[0])
            wload(wC[1][:], kpn(W[:, C_GV:C_GV + 512]), wC[1])
            sgw32 = fs_get()
            sgv = sgw32[:].rearrange("p (g t) -> p g t", g=4)
            P.dma(sgv, sgw_d[l], writes=[sgw32])
            P.tt("dve", sgwb[:], sgv, triU.unsqueeze(1).broadcast_to([128, 4, 128]), ALU.mult,
                 reads=[sgw32, ctf], writes=[sgwb])
            for cc in range(4):
                for tt in range(NTT):
                    sl = slice(tt * TT, (tt + 1) * TT)
                    ps = ps_get()
                    proj_fm(wC[0], cc * 128, 128, tt, ps)
                    P.act(oT[2][:, cc, sl], ps[:, :], AF.Gelu_apprx_tanh, reads=[ps], writes=[TS(oT[2], cc, tt)])
            sgb = pp[:, PP_SGB:PP_SGB + 512]
            for tb in range(NB):
                bs = slice(tb * 128, (tb + 1) * 128)
                ps = ps_get()
                for k in range(KC):
                    P.mm(ps[:, :], hT[:, k, bs], wC[1][:, k, :], start=(k == 0), stop=(k == KC - 1), reads=[TS(hT, k, tb // 4), wC[1]], writes=[ps])
                vg = fs_get()
                P.act(vg[:], ps[:, :], AF.Gelu_apprx_tanh, reads=[ps], writes=[vg])
                junk = fs_get()
                P.act(junk[:], vg[:], AF.Square, reads=[vg], writes=[junk, sm], accum_out=sm[:, 192:193])
                P.act(sm[:, 193:194], sm[:, 192:193], AF.Sqrt, reads=[sm], writes=[sm], bias=EPS, scale=1.0 / 512)
                P.op("dve", lambda e: e.reciprocal(sm[:, 194:195], sm[:, 193:194]), reads=[sm], writes=[sm])
                vn = sq_get()
                P.stt(vn[:], vg[:], sm[:, 194:195], pp[:, PP_SGNG:PP_SGNG + 512], ALU.mult, ALU.mult, reads=[vg, sm, pp], writes=[vn])
                ps2 = ps_get()
                for g in range(4):
                    P.mm(ps2[:, g * 128:(g + 1) * 128], vn[:, g * 128:(g + 1) * 128], sgwb[:, g, :], reads=[vn, sgwb], writes=[ps2])
                t1 = fs_get()
                P.tt("dve", t1[:], ps2[:, :], sgb, ALU.add, reads=[ps2, pp], writes=[t1])
                P.tt("dve", oT[2][:, :, bs], t1[:].rearrange("p (g t) -> p g t", g=4), oT[2][:, :, bs], ALU.mult,
                     reads=[t1, TSA(oT[2], 4, tb // 4)], writes=[TSA(oT[2], 4, tb // 4)])

            if debug and l == 0 and s == 0:
                for i in range(4):
                    for cc in range(4):
                        stg32 = _view(mT, 0, 2, 128, [T], F32)
                        P.copy("dve", stg32, oT[i][:, cc, :], reads=[CS(oT[i], cc)], writes=[CS(mT, (0, 1))])
                        P.dma(dbg_d[i, :, cc, :], stg32, reads=[CS(mT, (0, 1))], final=True)

            if stop == "sgate":
                return
            for j in range(KC):
                js = slice(j * 128, (j + 1) * 128)
                wg = wb_get()
                for i in range(4):
                    wload(wg[:, :, i * 128:(i + 1) * 128], kpn(wg_d[l, i][:, js]), wg)
                wbr = wb_get()
                wbv = wbr[:, 0:4, :].rearrange("p k (i n) -> p i k n", i=4)
                for i in range(4):
                    wload(wbv[:, i, :, :], kpn(wb_d[l, i][:, js]), wbr)
                for tt in range(NTT):
                    sl = slice(tt * TT, (tt + 1) * TT)
                    macc = fl_get()
                    for i in range(4):
                        psg = ps_get()
                        proj_fm(wg, i * 128, 128, tt, psg)
                        psp = ps_get()
                        for kk in range(4):
                            P.mm(psp[:, :], wbv[:, i, kk, :], oT[i][:, kk, sl], start=(kk == 0), stop=(kk == 3),
                                 reads=[wbr, TS(oT[i], kk, tt)], writes=[psp])
                        sg = fs_get()
                        P.act(sg[:], psg[:, :], AF.Sigmoid, reads=[psg], writes=[sg])
                        if i == 0:
                            P.tt("dve", macc[:], sg[:], psp[:, :], ALU.mult, reads=[sg, psp], writes=[macc])
                        else:
                            P.tt("dve", sg[:], sg[:], psp[:, :], ALU.mult, reads=[sg, psp], writes=[sg])
                            if i < 3:
                                P.tt("dve", macc[:], macc[:], sg[:], ALU.add, reads=[macc, sg], writes=[macc])
                            else:
                                P.tt("dve", mT[:, j, sl], macc[:], sg[:], ALU.add, reads=[macc, sg], writes=[TS(mT, j, tt)])

            if stop == "merge":
                return
            wo2 = [wb_get(), wb_get()]
            for jj in range(2):
                wload(wo2[jj][:], kpn(wo_d[l][:, jj * 512:(jj + 1) * 512]), wo2[jj])
            for tt in range(NTT):
                sl = slice(tt * TT, (tt + 1) * TT)
                half = tt % 2
                xt = _view(oT[half], 0, 4, 128, [KC, TT], F32)
                xacc = [CS(oT[half], range(4))]
                sap = kpn(xT_d[s][:, sl]) if src is None else kpn(src.t[:, sl])
                sacc = [] if src is None else [(src, [k * 4 + tt for k in range(KC)])]
                P.dma(xt, sap, reads=sacc, writes=xacc)
                for chn in range(KC):
                    wo = wo2[chn // 4]
                    j4 = chn % 4
                    ps = ps_get()
                    for k in range(KC):
                        P.mm(ps[:, :], wo[:, k, j4 * 128:(j4 + 1) * 128], mT[:, k, sl], start=(k == 0), stop=(k == KC - 1),
                             reads=[wo, TS(mT, k, tt)], writes=[ps])
                    P.stt(xt[:, chn, :], ps[:, :], gate1(chn), xt[:, chn, :], ALU.mult, ALU.add, reads=[ps, modT] + xacc, writes=xacc)
                P.dma(kpn(xb_d[:, sl]), xt, reads=xacc, writes=[(XB, [k * 4 + tt for k in range(KC)])])
                norm_tile(xt, xacc, s, l, 1, tt)

            def hid(hc, tt):
                if hc < 16:
                    return oT[hc // 4][:, hc % 4, :], [TS(oT[hc // 4], hc % 4, tt)]
                return mT[:, hc - 16, :], [TS(mT, hc - 16, tt)]

            for hp in range(NHC // 2):
                wf = wb_get()
                wload(wf[:, :, 0:256], kpn(wfi_d[l][:, hp * 256:(hp + 1) * 256]), wf)
                wload(wf[:, :, 256:512], kpn(wfi_d[l][:, D_FF + hp * 256:D_FF + (hp + 1) * 256]), wf)
                for hh in range(2):
                    hc = hp * 2 + hh
                    for tt in range(NTT):
                        hap, hacc = hid(hc, tt)
                        sl = slice(tt * TT, (tt + 1) * TT)
                        psa_ = ps_get()
                        proj_fm(wf, hh * 128, 128, tt, psa_)
                        psb2 = ps_get()
                        proj_fm(wf, 256 + hh * 128, 128, tt, psb2)
                        sa = fs_get()
                        P.act(sa[:], psa_[:, :], AF.Silu, reads=[psa_], writes=[sa])
                        P.tt("dve", hap[:, sl], sa[:], psb2[:, :], ALU.mult, reads=[sa, psb2], writes=hacc)
            for j in range(KC):
                js = slice(j * 128, (j + 1) * 128)
                wf2 = wb_get()
                w2v = wf2[:].rearrange("p k n -> p (k n)")[:, 0:NHC * 128].rearrange("p (c n) -> p c n", c=NHC)
                wload(w2v, wfo_d[l][:, js].rearrange("(c p) n -> p c n", p=128), wf2)
                for tt in range(NTT):
                    sl = slice(tt * TT, (tt + 1) * TT)
                    ps = ps_get()
                    for hc in range(NHC):
                        hap, hacc = hid(hc, tt)
                        P.mm(ps[:, :], w2v[:, hc, :], hap[:, sl], start=(hc == 0), stop=(hc == NHC - 1), reads=[wf2] + hacc, writes=[ps])
                    xin = fs_get()
                    P.dma(xin[:], xb_d[js, sl], reads=[(XB, j * 4 + tt)], writes=[xin])
                    P.stt(xin[:], ps[:, :], gate2(j), xin[:], ALU.mult, ALU.add, reads=[ps, modT, xin], writes=[xin])
                    if dst_is_out:
                        P.dma(outT_d[s][js, sl], xin[:], reads=[xin], writes=[(OUT, j * 4 + tt)], final=True)
                    else:
                        P.dma(xa_d[js, sl], xin[:], reads=[xin], writes=[(XA, j * 4 + tt)])

        for s in range(n_seq):
            for l in range(n_layers):
                layer(s, l, None if l == 0 else XA, XA, l == n_layers - 1)

        stats = P.finalize()
        P.emit()
    return nc, stats


_CACHE = {}


def _host_inputs(inp, n_seq=SEQ_PER_CORE):
    ctf, ctb, lk, lkc, rh = _const_tables()
    pp, pl = _pack_params(inp)
    shared = dict(
        pp=pp, pl=pl, ctf=ctf, ctb=ctb, lk=lk, lkc=lkc, rh=rh,
        sgwT=np.ascontiguousarray(np.asarray(inp["sg_w"], np.float32).transpose(0, 3, 1, 2)),
        ada_w=np.asarray(inp["ada_w"], np.float32), w_in=np.asarray(inp["w_in"], np.float32),
        cmp_w1=np.asarray(inp["nsa_cmp_w1"], np.float32), cmp_w2=np.asarray(inp["nsa_cmp_w2"], np.float32),
        w_branch=np.asarray(inp["w_branch"], np.float32), w_branch_gate=np.asarray(inp["w_branch_gate"], np.float32),
        w_out=np.asarray(inp["w_out"], np.float32), w_ffn_in=np.asarray(inp["w_ffn_in"], np.float32),
        w_ffn_out=np.asarray(inp["w_ffn_out"], np.float32),
    )
    x = np.asarray(inp["x"], np.float32)
    c = np.asarray(inp["c"], np.float32)
    maps = []
    for core in range(NCORES):
        b0 = core * SEQ_PER_CORE
        xs = x[b0:b0 + n_seq]
        cs = c[b0:b0 + n_seq]
        m = dict(shared)
        m["xT"] = np.ascontiguousarray(xs.transpose(0, 2, 1))
        m["cT"] = np.ascontiguousarray(cs.reshape(n_seq, KC, 128).transpose(2, 1, 0))
        maps.append(m)
    return maps


def kernel(**inputs):
    if "nc" not in _CACHE:
        _CACHE["nc"], _CACHE["stats"] = build_program()
    nc = _CACHE["nc"]
    maps = _host_inputs(inputs)
    res = run_bass_kernel_spmd(nc, maps, core_ids=list(range(NCORES)))
    out = np.empty((NCORES * SEQ_PER_CORE, T, D), np.float32)
    for core in range(NCORES):
        o = np.asarray(res.results[core]["outT"])
        out[core * SEQ_PER_CORE:(core + 1) * SEQ_PER_CORE] = o.transpose(0, 2, 1)
    return out
```
